# Optimizing a Trainium2 kernel written in Bass

```python
import jax, jax.numpy as jnp
from jax import lax
import numpy as np

D_MODEL = 1024
BATCH = 32
SEQ = 256
DEPTH = 4
DEC_BATCH = 2
DEC_SEQ = 4096
PAST_LEN = 256

GRID_W = 64
HEAD_DIM = 64
GQA_HEADS = 6
GQA_KV_HEADS = 2
GQA_GROUP = GQA_HEADS // GQA_KV_HEADS
GQA_W = GQA_HEADS * HEAD_DIM
MLSTM_HEADS = 4
MLSTM_DK = 64
MLSTM_DV = 64
MLSTM_W = MLSTM_HEADS * MLSTM_DV
MLSTM_CHUNK = 64
MLA_HEADS = 6
MLA_Q_RANK = 256
MLA_KV_RANK = 256
MLA_NOPE = 64
MLA_ROPE = 32
MLA_V = 64
MLA_QK = MLA_NOPE + MLA_ROPE
MLA_W = MLA_HEADS * MLA_V
MIX_W = GQA_W + MLSTM_W + MLA_W
D_FF = 2816
Q_BLOCK = 128
ROPE_BASE = 10000.0
EPS = 1e-6
N_MOD = 9
IN_SPLITS = (GQA_W, GQA_KV_HEADS * HEAD_DIM, GQA_KV_HEADS * HEAD_DIM,
             MLSTM_HEADS * MLSTM_DK, MLSTM_HEADS * MLSTM_DK, MLSTM_W, MLSTM_W, 4 * MLSTM_HEADS,
             MLA_Q_RANK, MLA_KV_RANK, MLA_ROPE)
D_IN = sum(IN_SPLITS)

kernel_name = 'hybrid_gqa_mlstm_mla_dit_step'


def rms_norm(x, w):
    xf = x.astype(jnp.float32)
    y = xf * lax.rsqrt(jnp.mean(xf * xf, axis=-1, keepdims=True) + EPS)
    return (y * w).astype(x.dtype)


def swiglu(h, wg, wu, wd):
    return (jax.nn.silu(h @ wg) * (h @ wu)) @ wd


def axial_rope(rows, rot_dim):
    half = rot_dim // 2
    freqs = ROPE_BASE ** (-jnp.arange(0, half, 2, dtype=jnp.float32) / half)
    r = jnp.repeat(jnp.arange(rows, dtype=jnp.float32), GRID_W)
    c = jnp.tile(jnp.arange(GRID_W, dtype=jnp.float32), rows)
    ang = jnp.concatenate([r[:, None] * freqs, c[:, None] * freqs], axis=-1)
    return jnp.cos(ang), jnp.sin(ang)


def apply_rope(x, cos, sin):
    half = x.shape[-1] // 2
    x1, x2 = x[..., :half], x[..., half:]
    c, s = cos[None, :, None, :], sin[None, :, None, :]
    return jnp.concatenate([x1 * c - x2 * s, x1 * s + x2 * c], axis=-1).astype(x.dtype)


def block_attention(q, k, v, scale):
    B, T, KVH, G, Dq = q.shape
    nb = T // Q_BLOCK
    qb = jnp.moveaxis(q.reshape(B, nb, Q_BLOCK, KVH, G, Dq), 1, 0)

    def one_block(qblk):
        s = jnp.einsum('bqhgd,bkhd->bhgqk', qblk, k, preferred_element_type=jnp.float32) * scale
        p = jax.nn.softmax(s, axis=-1)
        return jnp.einsum('bhgqk,bkhd->bqhgd', p.astype(v.dtype), v)

    out = lax.map(one_block, qb)
    return jnp.moveaxis(out, 0, 1).reshape(B, T, KVH, G, v.shape[-1])


def mlstm_chunkwise(q, k, v, log_i, log_f, C0, n0, m0):
    B, H, T, _ = q.shape
    L = MLSTM_CHUNK
    nc = T // L

    def to_chunks(a):
        return jnp.moveaxis(a.reshape(B, H, nc, L, *a.shape[3:]), 2, 0)

    xs = tuple(to_chunks(a) for a in (q, k, v, log_i, log_f))
    causal = jnp.tril(jnp.ones((L, L), dtype=bool))

    def step(carry, inp):
        C, n, m = carry
        qc, kc, vc, li, lf = inp
        b = jnp.cumsum(lf, axis=-1)
        D = jnp.where(causal, b[..., :, None] - b[..., None, :] + li[..., None, :], -jnp.inf)
        inter = b + m[..., None]
        m_t = jnp.maximum(inter, jnp.max(D, axis=-1))
        w_inter = jnp.exp(inter - m_t)
        qk = jnp.einsum('bhtd,bhsd->bhts', qc, kc) * jnp.exp(D - m_t[..., None])
        num = w_inter[..., None] * jnp.einsum('bhtd,bhde->bhte', qc, C) + jnp.einsum('bhts,bhse->bhte', qk, vc)
        den = w_inter * jnp.einsum('bhtd,bhd->bht', qc, n) + jnp.sum(qk, axis=-1)
        h = num / jnp.maximum(jnp.abs(den), jnp.exp(-m_t))[..., None]
        bL = b[..., -1]
        decay = bL[..., None] - b + li
        m_new = jnp.maximum(bL + m, jnp.max(decay, axis=-1))
        w_old = jnp.exp(bL + m - m_new)
        w_s = jnp.exp(decay - m_new[..., None])
        C_new = w_old[..., None, None] * C + jnp.einsum('bhs,bhsd,bhse->bhde', w_s, kc, vc)
        n_new = w_old[..., None] * n + jnp.einsum('bhs,bhsd->bhd', w_s, kc)
        return (C_new, n_new, m_new), h

    (C, n, m), hs = lax.scan(step, (C0, n0, m0), xs)
    h = jnp.moveaxis(hs, 0, 2).reshape(B, H, T, v.shape[-1])
    return h, C, n, m


def maybe_flip(a, rev):
    return jnp.flip(a, axis=2) if rev else a


def mlstm_bidir(q, k, v, gates, C0, n0, m0):
    hs, Cs, ns, ms = [], [], [], []
    for d in range(2):
        rev = d == 1
        h, C, n, m = mlstm_chunkwise(
            maybe_flip(q, rev), maybe_flip(k, rev), maybe_flip(v, rev),
            maybe_flip(gates[..., 2 * d], rev),
            jax.nn.log_sigmoid(maybe_flip(gates[..., 2 * d + 1], rev)),
            C0[:, d], n0[:, d], m0[:, d])
        hs.append(maybe_flip(h, rev))
        Cs.append(C)
        ns.append(n)
        ms.append(m)
    return hs[0] + hs[1], jnp.stack(Cs, axis=1), jnp.stack(ns, axis=1), jnp.stack(ms, axis=1)


def mixer_inputs(h, lp):
    B, T, _ = h.shape
    idx = np.cumsum(IN_SPLITS)[:-1].tolist()
    gq, gk, gv, mq, mk, mv, mo, mg, qlat, kvlat, krope = jnp.split(h @ lp['w_in'], idx, axis=-1)
    gqa_q = rms_norm(gq.reshape(B, T, GQA_HEADS, HEAD_DIM), lp['gqa_q_norm'])
    gqa_k = rms_norm(gk.reshape(B, T, GQA_KV_HEADS, HEAD_DIM), lp['gqa_k_norm'])
    gqa_v = gv.reshape(B, T, GQA_KV_HEADS, HEAD_DIM)

    def heads(a, d):
        return a.reshape(B, T, MLSTM_HEADS, d).transpose(0, 2, 1, 3).astype(jnp.float32)

    ml_q = heads(mq, MLSTM_DK) * (MLSTM_DK ** -0.5)
    ml_k = heads(mk, MLSTM_DK)
    ml_v = heads(mv, MLSTM_DV)
    ml_g = (mg.reshape(B, T, 4, MLSTM_HEADS) + lp['mlstm_gate_b']).transpose(0, 3, 1, 2).astype(jnp.float32)
    mla_q = (rms_norm(qlat, lp['mla_q_norm']) @ lp['mla_w_uq']).reshape(B, T, MLA_HEADS, MLA_QK)
    mla_ckv = rms_norm(kvlat, lp['mla_kv_norm'])
    return gqa_q, gqa_k, gqa_v, ml_q, ml_k, ml_v, mo, ml_g, mla_q, mla_ckv, krope


def mla_attend(q, ckv, krope, w_ukv):
    B, S, _ = ckv.shape
    kv = (ckv @ w_ukv).reshape(B, S, MLA_HEADS, MLA_NOPE + MLA_V)
    k_nope, v = kv[..., :MLA_NOPE], kv[..., MLA_NOPE:]
    k = jnp.concatenate([k_nope, jnp.broadcast_to(krope[:, :, None, :], (B, S, MLA_HEADS, MLA_ROPE))], axis=-1)
    return block_attention(q[:, :, :, None, :], k, v, MLA_QK ** -0.5)


def mixer_output(gqa_o, ml_h, ml_o, mla_o, lp):
    B, T = gqa_o.shape[:2]
    hn = rms_norm(ml_h, lp['mlstm_out_norm'].reshape(MLSTM_HEADS, 1, MLSTM_DV))
    ml = hn.transpose(0, 2, 1, 3).reshape(B, T, MLSTM_W).astype(ml_o.dtype) * jax.nn.sigmoid(ml_o)
    cat = jnp.concatenate([gqa_o.reshape(B, T, GQA_W), ml, mla_o.reshape(B, T, MLA_W)], axis=-1)
    return cat @ lp['w_out']


def mix_context(h, lp):
    B, T, _ = h.shape
    gqa_q, gqa_k, gqa_v, ml_q, ml_k, ml_v, ml_o, ml_g, mla_q, mla_ckv, mla_kr = mixer_inputs(h, lp)
    gqa_o = block_attention(gqa_q.reshape(B, T, GQA_KV_HEADS, GQA_GROUP, HEAD_DIM), gqa_k, gqa_v, HEAD_DIM ** -0.5)
    C0 = jnp.zeros((B, 2, MLSTM_HEADS, MLSTM_DK, MLSTM_DV), jnp.float32)
    n0 = jnp.zeros((B, 2, MLSTM_HEADS, MLSTM_DK), jnp.float32)
    m0 = jnp.zeros((B, 2, MLSTM_HEADS), jnp.float32)
    ml_h, C, n, m = mlstm_bidir(ml_q, ml_k, ml_v, ml_g, C0, n0, m0)
    mla_o = mla_attend(mla_q, mla_ckv, mla_kr, lp['mla_w_ukv'])
    return mixer_output(gqa_o, ml_h, ml_o, mla_o, lp), (gqa_k, gqa_v, mla_ckv, mla_kr, C, n, m)


def mix_latent(h, lp, ctx, rope_hd, rope_mla):
    B, T, _ = h.shape
    ck, cv, cckv, ckr, C0, n0, m0 = ctx
    gqa_q, gqa_k, gqa_v, ml_q, ml_k, ml_v, ml_o, ml_g, mla_q, mla_ckv, mla_kr = mixer_inputs(h, lp)
    q = apply_rope(gqa_q, *rope_hd).reshape(B, T, GQA_KV_HEADS, GQA_GROUP, HEAD_DIM)
    k = jnp.concatenate([ck, apply_rope(gqa_k, *rope_hd)], axis=1)
    v = jnp.concatenate([cv, gqa_v], axis=1)
    gqa_o = block_attention(q, k, v, HEAD_DIM ** -0.5)
    ml_h, _, _, _ = mlstm_bidir(ml_q, ml_k, ml_v, ml_g,
                                C0.astype(jnp.float32), n0.astype(jnp.float32), m0.astype(jnp.float32))
    q_mla = jnp.concatenate([mla_q[..., :MLA_NOPE], apply_rope(mla_q[..., MLA_NOPE:], *rope_mla)], axis=-1)
    kr = apply_rope(mla_kr[:, :, None, :], *rope_mla)[:, :, 0]
    mla_o = mla_attend(q_mla, jnp.concatenate([cckv, mla_ckv], axis=1),
                       jnp.concatenate([ckr, kr], axis=1), lp['mla_w_ukv'])
    return mixer_output(gqa_o, ml_h, ml_o, mla_o, lp)


def trunk_layer(x, cond, lp, mix):
    mods = jnp.split(jax.nn.silu(cond) @ lp['w_ada'] + lp['b_ada'], N_MOD, axis=-1)
    sh1, sc1, g1, sh2, sc2, g2, sh3, sc3, g3 = [mm[:, None, :] for mm in mods]
    h = rms_norm(x, lp['norm_w'][0]) * (1 + sc1) + sh1
    x = x + 0.5 * g1 * swiglu(h, lp['ffn_w_gate'][0], lp['ffn_w_up'][0], lp['ffn_w_down'][0])
    h = rms_norm(x, lp['norm_w'][1]) * (1 + sc2) + sh2
    out, extra = mix(h)
    x = x + g2 * out
    h = rms_norm(x, lp['norm_w'][2]) * (1 + sc3) + sh3
    x = x + 0.5 * g3 * swiglu(h, lp['ffn_w_gate'][1], lp['ffn_w_up'][1], lp['ffn_w_down'][1])
    return x, extra


def setup_inputs(seed: int = 0) -> dict:
    key = jax.random.key(seed)
    ks = jax.random.split(key, 32)

    def nrm(k, shape, s=1.0):
        return s * jax.random.normal(k, shape, jnp.float32)

    return {
        'x_prompt': nrm(ks[0], (BATCH, SEQ, D_MODEL)),
        'x_sample': nrm(ks[1], (DEC_BATCH, DEC_SEQ, D_MODEL)),
        'c': nrm(ks[2], (DEC_BATCH, D_MODEL)),
        'cache_gqa_k': nrm(ks[3], (DEC_BATCH, DEPTH, PAST_LEN, GQA_KV_HEADS, HEAD_DIM)),
        'cache_gqa_v': nrm(ks[4], (DEC_BATCH, DEPTH, PAST_LEN, GQA_KV_HEADS, HEAD_DIM)),
        'cache_mla_ckv': nrm(ks[5], (DEC_BATCH, DEPTH, PAST_LEN, MLA_KV_RANK)),
        'cache_mla_krope': nrm(ks[6], (DEC_BATCH, DEPTH, PAST_LEN, MLA_ROPE)),
        'state_mlstm_C': nrm(ks[7], (DEC_BATCH, DEPTH, 2, MLSTM_HEADS, MLSTM_DK, MLSTM_DV)),
        'state_mlstm_n': nrm(ks[8], (DEC_BATCH, DEPTH, 2, MLSTM_HEADS, MLSTM_DK)),
        'state_mlstm_m': jax.random.uniform(ks[9], (DEC_BATCH, DEPTH, 2, MLSTM_HEADS), jnp.float32, 0.0, 3.0),
        'c_ctx': nrm(ks[10], (D_MODEL,)),
        'w_ada': nrm(ks[11], (DEPTH, D_MODEL, N_MOD * D_MODEL), 0.3 * D_MODEL ** -0.5),
        'b_ada': nrm(ks[12], (DEPTH, N_MOD * D_MODEL), 0.02),
        'norm_w': 1.0 + nrm(ks[13], (DEPTH, 3, D_MODEL), 0.05),
        'ffn_w_gate': nrm(ks[14], (DEPTH, 2, D_MODEL, D_FF), D_MODEL ** -0.5),
        'ffn_w_up': nrm(ks[15], (DEPTH, 2, D_MODEL, D_FF), D_MODEL ** -0.5),
        'ffn_w_down': nrm(ks[16], (DEPTH, 2, D_FF, D_MODEL), D_FF ** -0.5),
        'w_in': nrm(ks[17], (DEPTH, D_MODEL, D_IN), D_MODEL ** -0.5),
        'gqa_q_norm': 1.0 + nrm(ks[18], (DEPTH, HEAD_DIM), 0.05),
        'gqa_k_norm': 1.0 + nrm(ks[19], (DEPTH, HEAD_DIM), 0.05),
        'mlstm_gate_b': jnp.array([0.0, 3.0, 0.0, 3.0], jnp.float32)[None, :, None] + nrm(ks[20], (DEPTH, 4, MLSTM_HEADS), 0.1),
        'mlstm_out_norm': 1.0 + nrm(ks[21], (DEPTH, MLSTM_W), 0.05),
        'mla_q_norm': 1.0 + nrm(ks[22], (DEPTH, MLA_Q_RANK), 0.05),
        'mla_w_uq': nrm(ks[23], (DEPTH, MLA_Q_RANK, MLA_HEADS * MLA_QK), MLA_Q_RANK ** -0.5),
        'mla_kv_norm': 1.0 + nrm(ks[24], (DEPTH, MLA_KV_RANK), 0.05),
        'mla_w_ukv': nrm(ks[25], (DEPTH, MLA_KV_RANK, MLA_HEADS * (MLA_NOPE + MLA_V)), MLA_KV_RANK ** -0.5),
        'w_out': nrm(ks[26], (DEPTH, MIX_W, D_MODEL), MIX_W ** -0.5),
        'final_norm': 1.0 + nrm(ks[27], (D_MODEL,), 0.05),
    }


def reference(x_prompt, x_sample, c, cache_gqa_k, cache_gqa_v, cache_mla_ckv, cache_mla_krope,
              state_mlstm_C, state_mlstm_n, state_mlstm_m, c_ctx,
              w_ada, b_ada, norm_w, ffn_w_gate, ffn_w_up, ffn_w_down, w_in, gqa_q_norm, gqa_k_norm,
              mlstm_gate_b, mlstm_out_norm, mla_q_norm, mla_w_uq, mla_kv_norm, mla_w_ukv, w_out, final_norm):
    rows = x_sample.shape[1] // GRID_W
    rope_hd = axial_rope(rows, HEAD_DIM)
    rope_mla = axial_rope(rows, MLA_ROPE)
    cond_ctx = c_ctx[None, :]
    xp, xs = x_prompt, x_sample
    collected = [[] for _ in range(7)]
    for l in range(DEPTH):
        lp = {'w_ada': w_ada[l], 'b_ada': b_ada[l], 'norm_w': norm_w[l],
              'ffn_w_gate': ffn_w_gate[l], 'ffn_w_up': ffn_w_up[l], 'ffn_w_down': ffn_w_down[l],
              'w_in': w_in[l], 'gqa_q_norm': gqa_q_norm[l], 'gqa_k_norm': gqa_k_norm[l],
              'mlstm_gate_b': mlstm_gate_b[l], 'mlstm_out_norm': mlstm_out_norm[l],
              'mla_q_norm': mla_q_norm[l], 'mla_w_uq': mla_w_uq[l], 'mla_kv_norm': mla_kv_norm[l],
              'mla_w_ukv': mla_w_ukv[l], 'w_out': w_out[l]}
        xp, ctx_new = trunk_layer(xp, cond_ctx, lp, lambda h: mix_context(h, lp))
        for lst, t in zip(collected, ctx_new):
            lst.append(t)
        ctx_cached = (cache_gqa_k[:, l], cache_gqa_v[:, l], cache_mla_ckv[:, l], cache_mla_krope[:, l],
                      state_mlstm_C[:, l], state_mlstm_n[:, l], state_mlstm_m[:, l])
        xs, _ = trunk_layer(xs, c, lp, lambda h: (mix_latent(h, lp, ctx_cached, rope_hd, rope_mla), None))
    y_prompt = rms_norm(xp, final_norm)
    y_sample = rms_norm(xs, final_norm)
    new_gqa_k, new_gqa_v, new_mla_ckv, new_mla_krope, new_mlstm_C, new_mlstm_n, new_mlstm_m = [
        jnp.stack(lst, axis=1) for lst in collected]
    return (y_prompt, y_sample, new_gqa_k, new_gqa_v, new_mla_ckv, new_mla_krope, new_mlstm_C, new_mlstm_n, new_mlstm_m)
```

```python
import os
import numpy as np
import concourse.bass as bass
import concourse.mybir as mybir
from concourse.bass_utils import run_bass_kernel_spmd

F32 = mybir.dt.float32
BF16 = mybir.dt.bfloat16
AF = mybir.ActivationFunctionType
ALU = mybir.AluOpType
AX = mybir.AxisListType

D = 1024
DFF = 2816
NFF = DFF // 128
DEPTH = 4
NSEQ_P = 4
SEQ = 256
TS = 4096
PAST = 256
GT = 1024
NG = 5
TT = NG * GT
EPS = 1e-6
DIN = 2224
SAME_ENGINE_SYNC = True


class Buf:
    __slots__ = ("name", "writers", "readers", "sem", "ndma", "last_dma", "excl")

    def __init__(self, name, excl=False):
        self.name = name
        self.excl = excl
        self.writers = []
        self.readers = []
        self.sem = None
        self.ndma = 0
        self.last_dma = None

    def reset(self):
        self.writers = []
        self.readers = []
        self.sem = None
        self.ndma = 0
        self.last_dma = None


class V:
    __slots__ = ("ap", "buf")

    def __init__(self, ap, buf):
        self.ap = ap
        self.buf = buf

    def __getitem__(self, idx):
        return V(self.ap[idx], self.buf)

    def part(self, idx, name):
        return V(self.ap[idx], Buf(name))

    def re(self, pat, **kw):
        return V(self.ap.rearrange(pat, **kw), self.buf)

    def bc(self, dt):
        return V(self.ap.bitcast(dt), self.buf)


class Op:
    __slots__ = ("eng", "fn", "deps", "tick", "needs_inc", "is_dma", "key", "dma_idx", "pos")

    def __init__(self, eng, fn, is_dma=False, key=None):
        self.eng = eng
        self.fn = fn
        self.deps = []
        self.tick = 0
        self.needs_inc = is_dma
        self.is_dma = is_dma
        self.key = key
        self.dma_idx = 0
        self.pos = 0


class Phase:
    ENGS = ("pe", "act", "dve", "pool", "sp")

    def __init__(self, nc, name):
        self.nc = nc
        self.name = name
        self.ops = []
        self.touched = {}

    def sb(self, name, shape, dt):
        t = self.nc.alloc_sbuf_tensor(self.name + "_" + name, list(shape), dt)
        return V(t.ap(), Buf(name))

    def ps(self, name, shape=(128, 512), dt=F32):
        t = self.nc.alloc_psum_tensor(self.name + "_" + name, list(shape), dt)
        return V(t.ap(), Buf(name, excl=True))

    def _touch(self, b):
        self.touched[id(b)] = b

    def add(self, eng, fn, reads=(), writes=(), is_dma=False, key=None):
        op = Op(eng, fn, is_dma, key)
        op.pos = len(self.ops)
        deps = []
        rb = [v.buf for v in reads]
        wb = [v.buf for v in writes]
        for b in rb + wb:
            self._touch(b)
        raw = set()
        for b in rb:
            deps.extend(b.writers)
            raw.update(id(w) for w in b.writers)
            if b.excl:
                deps.extend(r for r in b.readers if r.eng != eng)
        for b in wb:
            deps.extend(b.writers)
            deps.extend(b.readers)
        if is_dma:
            self._touch(key)
            if key.last_dma is not None:
                deps.append(key.last_dma)
            key.ndma += 1
            op.dma_idx = key.ndma
            key.last_dma = op
        for b in wb:
            if b.readers:
                b.writers = [op]
                b.readers = []
            else:
                b.writers = [w for w in b.writers if (w.is_dma or w.eng != eng or is_dma)] + [op]
        for b in rb:
            if b not in wb:
                b.readers = [r for r in b.readers if (r.is_dma or r.eng != eng or is_dma)] + [op]
        seen = set()
        for d in deps:
            if d is op or id(d) in seen:
                continue
            seen.add(id(d))
            if (not d.is_dma) and (not is_dma) and d.eng == eng:
                if eng == "pe" or not SAME_ENGINE_SYNC or id(d) not in raw:
                    continue
            op.deps.append(d)
            d.needs_inc = True
        self.ops.append(op)
        return op

    def mm(self, out, lhsT, rhs, start=True, stop=True, **kw):
        return self.add("pe", lambda e: e.matmul(out.ap, lhsT.ap, rhs.ap, start=start, stop=stop, **kw),
                        reads=(lhsT, rhs), writes=(out,))

    def transpose(self, out, in_, ident):
        return self.add("pe", lambda e: e.transpose(out.ap, in_.ap, ident.ap), reads=(in_, ident), writes=(out,))

    def act(self, out, in_, func, bias=None, scale=None, extra_reads=(), accum_out=None):
        kw = {}
        rd = [in_] + list(extra_reads)
        wr = [out]
        if bias is not None:
            if isinstance(bias, V):
                kw["bias"] = bias.ap
                rd.append(bias)
            else:
                kw["bias"] = bias
        if scale is not None:
            if isinstance(scale, V):
                kw["scale"] = scale.ap
                rd.append(scale)
            else:
                kw["scale"] = scale
        if accum_out is not None:
            kw["accum_out"] = accum_out.ap
            wr.append(accum_out)
        return self.add("act", lambda e: e.activation(out=out.ap, in_=in_.ap, func=func, **kw), reads=rd, writes=wr)

    def tt(self, out, in0, in1, op, eng="dve"):
        return self.add(eng, lambda e: e.tensor_tensor(out=out.ap, in0=in0.ap, in1=in1.ap, op=op),
                        reads=(in0, in1), writes=(out,))

    def ts(self, out, in0, s1, op0, s2=None, op1=None, eng="dve"):
        rd = [in0]
        a1 = s1
        a2 = s2
        if isinstance(s1, V):
            rd.append(s1)
            a1 = s1.ap
        if isinstance(s2, V):
            rd.append(s2)
            a2 = s2.ap
        if op1 is None:
            return self.add(eng, lambda e: e.tensor_scalar(out=out.ap, in0=in0.ap, scalar1=a1, scalar2=None, op0=op0),
                            reads=rd, writes=(out,))
        return self.add(eng, lambda e: e.tensor_scalar(out=out.ap, in0=in0.ap, scalar1=a1, scalar2=a2, op0=op0, op1=op1),
                        reads=rd, writes=(out,))

    def stt(self, out, in0, scalar, in1, op0, op1):
        rd = [in0, in1]
        a = scalar
        if isinstance(scalar, V):
            rd.append(scalar)
            a = scalar.ap
        return self.add("dve", lambda e: e.scalar_tensor_tensor(out=out.ap, in0=in0.ap, scalar=a, in1=in1.ap, op0=op0, op1=op1),
                        reads=rd, writes=(out,))

    def copy(self, out, in_, eng="dve"):
        if eng == "act":
            return self.add("act", lambda e: e.copy(out=out.ap, in_=in_.ap), reads=(in_,), writes=(out,))
        return self.add(eng, lambda e: e.tensor_copy(out=out.ap, in_=in_.ap), reads=(in_,), writes=(out,))

    def recip(self, out, in_):
        return self.add("dve", lambda e: e.reciprocal(out=out.ap, in_=in_.ap), reads=(in_,), writes=(out,))

    def rsqrt_ln(self, out, in_, eps_ap, tmp=None):
        t = out if tmp is None else tmp
        self.act(t, in_, AF.Ln, bias=eps_ap)
        self.act(out, t, AF.Exp, scale=-0.5)

    def memset(self, out, val, eng="pool"):
        return self.add(eng, lambda e: e.memset(out.ap, val), reads=(), writes=(out,))

    def dma(self, out, in_, q="sp", key=None, **kw):
        if key is None:
            key = in_.buf if str(out.ap.space) == "DRAM" else out.buf
        return self.add(q, lambda e: e.dma_start(out=out.ap, in_=in_.ap, **kw), reads=(in_,), writes=(out,),
                        is_dma=True, key=key)

    def emit(self):
        nc = self.nc
        sems = {}
        for en in ("pe", "act", "dve", "pool"):
            sems[en] = nc.alloc_semaphore(self.name + "_s_" + en)
        keys = []
        cnt = {en: 0 for en in ("pe", "act", "dve", "pool")}
        for op in self.ops:
            if op.is_dma:
                if op.key.sem is None:
                    op.key.sem = nc.alloc_semaphore(self.name + "_d_" + op.key.name + str(len(keys)))
                    keys.append(op.key)
                op.tick = 16 * op.dma_idx
            elif op.needs_inc:
                cnt[op.eng] += 1
                op.tick = cnt[op.eng]
        streams = {en: [] for en in self.ENGS}
        for op in self.ops:
            streams[op.eng].append(op)

        def run_stream(en, e):
            waited = {}
            for op in streams[en]:
                need = {}
                for d in op.deps:
                    s = d.key.sem if d.is_dma else sems[d.eng]
                    k = id(s)
                    if k not in need or need[k][1] < d.tick:
                        need[k] = (s, d.tick)
                for k, (s, val) in need.items():
                    if waited.get(k, 0) >= val:
                        continue
                    waited[k] = val
                    e.wait_ge(s, val)
                ins = op.fn(e)
                if op.is_dma:
                    ins.then_inc(op.key.sem, 16)
                elif op.needs_inc:
                    ins.then_inc(sems[op.eng], 1)
            if en == "sp":
                for kb in keys:
                    e.wait_ge(kb.sem, 16 * kb.ndma)

        with nc.Block(self.name) as block:
            @block.tensor
            def _(e):
                run_stream("pe", e)

            @block.scalar
            def _(e):
                run_stream("act", e)

            @block.vector
            def _(e):
                run_stream("dve", e)

            @block.gpsimd
            def _(e):
                run_stream("pool", e)

            @block.sync
            def _(e):
                run_stream("sp", e)
        for b in self.touched.values():
            b.reset()
        for op in self.ops:
            op.fn = None
        self.ops = []


C_GQ, C_GK, C_GV, C_MQ, C_MK, C_MV, C_MO, C_MG, C_QL, C_KVL, C_KR = 0, 384, 512, 640, 896, 1152, 1408, 1664, 1680, 1936, 2192


class Builder:
    def __init__(self, depth=DEPTH, debug=False, stop_after=None):
        self.depth = depth
        self.debug = debug
        self.stop_after = stop_after
        nc = bass.Bass("TRN2", target_bir_lowering=False)
        self.nc = nc
        self.pn = 0
        L = depth

        def din(name, shape, dt=F32):
            return V(nc.dram_tensor(name, list(shape), dt, kind="ExternalInput").ap(), Buf(name))

        def dout(name, shape, dt=F32):
            return V(nc.dram_tensor(name, list(shape), dt, kind="ExternalOutput").ap(), Buf(name))

        def dscr(name, shape, dt=BF16):
            kind = "ExternalOutput" if debug else "Internal"
            return V(nc.dram_tensor(name, list(shape), dt, kind=kind).ap(), Buf(name))

        self.xp = din("xp", [GT, D])
        self.xs = din("xs", [TS, D])
        self.cond = din("cond", [16, 128])
        self.ck = din("ck", [L, PAST, 128])
        self.cv = din("cv", [L, PAST, 128])
        self.cckv = din("cckv", [L, PAST, 256])
        self.ckr = din("ckr", [L, PAST, 32])
        self.sC = din("sC", [L, 8, 64, 64])
        self.sn = din("sn", [L, 8, 64])
        self.sm = din("sm", [L, 8])
        self.w_ada = din("w_ada", [L, D, 9 * D])
        self.b_ada = din("b_ada", [L, 9 * D])
        self.norm_w = din("norm_w", [L * 3 * 8, 128])
        self.wg = din("ffn_w_gate", [L, 2, D, DFF])
        self.wu = din("ffn_w_up", [L, 2, D, DFF])
        self.wd = din("ffn_w_down", [L, 2, DFF, D])
        self.w_in = din("w_in", [L, D, DIN])
        self.gqn = din("gqa_q_norm", [L, 64])
        self.gkn = din("gqa_k_norm", [L, 64])
        self.mgb = din("mlstm_gate_b", [1, L * 16])
        self.mon = din("mlstm_out_norm", [L * 4, 64])
        self.mqn = din("mla_q_norm", [L * 2, 128])
        self.w_uq = din("mla_w_uq", [L, 256, 576])
        self.mkvn = din("mla_kv_norm", [L * 2, 128])
        self.w_ukv = din("mla_w_ukv", [L, 256, 768])
        self.w_out = din("w_out", [L, D, D])
        self.fnorm = din("final_norm", [8, 128])
        self.c_ident = din("c_ident", [128, 128])
        self.c_cos64 = din("c_cos64", [128, TS])
        self.c_sin64 = din("c_sin64", [128, TS])
        self.c_cos32 = din("c_cos32", [128, TS])
        self.c_sin32 = din("c_sin32", [128, TS])
        self.c_perm64 = din("c_perm64", [128, 128])
        self.c_perm32 = din("c_perm32", [128, 96])
        self.c_maskf = din("c_maskf", [128, 128])
        self.c_maskb = din("c_maskb", [128, 128])
        self.c_zero = din("c_zero", [64, PAST + TS])
        self.yp = dout("yp", [GT, D])
        self.ys = dout("ys", [TS, D])
        self.ngk = dout("ngk", [NSEQ_P, L, SEQ, 128])
        self.ngv = dout("ngv", [NSEQ_P, L, SEQ, 128])
        self.nckv = dout("nckv", [NSEQ_P, L, SEQ, 256])
        self.nkr = dout("nkr", [NSEQ_P, L, SEQ, 32])
        self.nC = dout("nC", [NSEQ_P, L, 8, 64, 64])
        self.nn = dout("nn", [NSEQ_P, L, 8, 64])
        self.nm = dout("nm", [NSEQ_P, L, 8])
        self.xT = dscr("xT", [NG, 128, 8, GT], F32)
        self.sQg = dscr("sQg", [3, 128, TT])
        self.sKg = dscr("sKg", [128, TT])
        self.sVg = dscr("sVg", [TT, 130])
        self.sQm = dscr("sQm", [6, 96, TT])
        self.sKm = dscr("sKm", [6, 96, TT])
        self.sVm = dscr("sVm", [TT, 390])
        self.sMq = dscr("sMq", [2, 128, TT])
        self.sMk = dscr("sMk", [2, 128, TT])
        self.sMkt = dscr("sMkt", [TT, 256])
        self.sMv = dscr("sMv", [TT, 260])
        self.sGate = dscr("sGate", [TT, 16], F32)
        self.sMo = dscr("sMo", [4, 64, TT])
        self.sCatM = dscr("sCatM", [4, 64, TT])

        def gsb(name, shape, dt):
            return V(nc.alloc_sbuf_tensor("g_" + name, list(shape), dt).ap(), Buf("g_" + name))

        self.ident_f = gsb("identf", [128, 128], F32)
        self.ident_b = gsb("identb", [128, 128], BF16)
        self.ones_dm = gsb("onesdm", [128, 128], BF16)
        self.ones64 = gsb("ones64", [128, 128], BF16)
        self.ones256 = gsb("ones256", [128, 128], BF16)
        self.ones_f = gsb("onesf", [128, 128], F32)
        self.ones_b = gsb("onesb", [128, 128], BF16)
        self.perm64 = gsb("perm64", [128, 128], BF16)
        self.perm32 = gsb("perm32", [128, 96], BF16)
        self.maskf4 = gsb("maskf4", [128, 512], F32)
        self.maskb4 = gsb("maskb4", [128, 512], F32)
        self.pcolA = gsb("pcolA", [128, 128], F32)
        self.pcolB = gsb("pcolB", [128, 32], F32)
        self.modc = gsb("modc", [128, L * 2 * 72], F32)
        self.gbias = gsb("gbias", [128, L * 16], F32)
        self.epsc = gsb("epsc", [128, 1], F32)

    def phase(self, name):
        self.pn += 1
        return Phase(self.nc, "p%d%s" % (self.pn, name))

    def mod(self, l, ci, slot, c=None):
        base = ((l * 2 + ci) * 9 + slot) * 8
        if c is None:
            return self.modc[:, base:base + 8]
        return self.modc[:, base + c:base + c + 1]

    def prologue(self):
        nc = self.nc
        L = self.depth
        with nc.cleanup_on_exit():
            ph = self.phase("pro")
            stA = ph.sb("stA", [128, 128], F32)
            stB = ph.sb("stB", [128, 128], F32)
            tmpf = ph.sb("tmpf", [128, 128], F32)
            tmp96 = ph.sb("tmp96", [128, 96], F32)
            ph.dma(self.ident_f, self.c_ident)
            ph.copy(self.ident_b, self.ident_f)
            ph.dma(tmpf, self.c_perm64)
            ph.copy(self.perm64, tmpf)
            ph.dma(tmp96, self.c_perm32)
            ph.copy(self.perm32, tmp96)
            for j in range(4):
                ph.dma(self.maskf4[:, j * 128:(j + 1) * 128], self.c_maskf)
                ph.dma(self.maskb4[:, j * 128:(j + 1) * 128], self.c_maskb)
            ph.memset(self.ones_dm, 1.0 / 1024.0)
            ph.memset(self.ones256, 1.0 / 256.0)
            ph.memset(self.ones_f, 1.0)
            ph.memset(self.ones_b, 1.0)
            ph.memset(self.epsc, EPS)
            ph.memset(self.ones64, 0.0)
            ph.memset(self.ones64[0:64, 0:64], 1.0 / 64.0)
            ph.memset(self.ones64[64:128, 64:128], 1.0 / 64.0)
            ph.dma(self.gbias, V(self.mgb.ap.partition_broadcast(128), self.mgb.buf))
            ph.memset(stA, 0.0)
            ph.memset(stB, 0.0)
            ph.dma(stA[0:L * 24, :], self.norm_w)
            ph.dma(stA[96:104, :], self.fnorm)
            ph.dma(stA[104:120, :], self.cond)
            ph.dma(stA[120:120 + L, 0:64], self.gqn)
            ph.dma(stA[120:120 + L, 64:128], self.gqn)
            ph.dma(stA[124:124 + L, 0:64], self.gkn)
            ph.dma(stA[124:124 + L, 64:128], self.gkn)
            ph.dma(stB[0:2 * L, :], self.mqn)
            ph.dma(stB[8:8 + 2 * L, :], self.mkvn)
            ph.dma(stB[16:16 + 4 * L, 0:64], self.mon)
            psT = ph.ps("psT")
            ph.transpose(psT[:, 0:128], stA, self.ident_f)
            ph.copy(self.pcolA, psT[:, 0:128])
            ph.transpose(psT[:, 128:256], stB, self.ident_f)
            ph.copy(self.pcolB, psT[:, 128:160])
            scT = ph.sb("scT", [128, 16], BF16)
            ph.act(scT, self.pcolA[:, 104:120], AF.Silu)
            onesr = ph.sb("onesr", [1, 2], BF16)
            ph.memset(onesr, 1.0)
            NB = 512
            wts = [ph.sb("wada%d" % i, [128, 8, NB], BF16) for i in range(3)]
            brow = ph.sb("brow", [1, 9 * D], BF16)
            psm = [ph.ps("psm%d" % i) for i in range(2)]
            mods = ph.sb("mods", [128, 72, 2], F32)
            wi = 0
            for l in range(L):
                ph.dma(brow, self.b_ada[l:l + 1, :], q="pool")
                pm = psm[l % 2]
                for nb in range(9 * D // NB):
                    wt = wts[wi % 3]
                    wi += 1
                    ph.dma(wt, self.w_ada[l, :, nb * NB:(nb + 1) * NB].re("(k p) f -> p k f", p=128), q="pool")
                    for jj in range(NB // 128):
                        j = nb * (NB // 128) + jj
                        for k in range(8):
                            ph.mm(pm[:, 2 * j:2 * j + 2], wt[:, k, jj * 128:(jj + 1) * 128],
                                  scT[:, :].re("p (c k) -> p c k", k=8)[:, :, k], start=(k == 0), stop=False)
                        ph.mm(pm[:, 2 * j:2 * j + 2], brow[0:1, j * 128:(j + 1) * 128], onesr[0:1, :], start=False, stop=True)
                ph.copy(mods, pm[:, 0:144].re("p (j c) -> p j c", c=2))
                for ci in range(2):
                    for i in range(3):
                        nw = self.pcolA[:, (l * 3 + i) * 8:(l * 3 + i) * 8 + 8]
                        sh = mods[:, (3 * i) * 8:(3 * i) * 8 + 8, ci]
                        sc = mods[:, (3 * i + 1) * 8:(3 * i + 1) * 8 + 8, ci]
                        gg = mods[:, (3 * i + 2) * 8:(3 * i + 2) * 8 + 8, ci]
                        ph.stt(self.mod(l, ci, 3 * i), sc, 1.0, nw, ALU.add, ALU.mult)
                        ph.copy(self.mod(l, ci, 3 * i + 1), sh)
                        ph.ts(self.mod(l, ci, 3 * i + 2), gg, 1.0 if i == 1 else 0.5, ALU.mult)
            ph.emit()

    def rms_stats(self, ph, xt, rstd, banks, sqb):
        for hf in range(2):
            pb = banks[hf % len(banks)]
            for c in range(8):
                sq = sqb[c % len(sqb)]
                ph.act(sq, xt[c][:, hf * 512:(hf + 1) * 512], AF.Square)
                ph.mm(pb, self.ones_dm, sq, start=(c == 0), stop=(c == 7))
            sd = sqb[0].buf
            ph.rsqrt_ln(rstd[:, hf * 512:(hf + 1) * 512], pb, self.epsc[:, 0:1])

    def ffn_phase(self, l, which, g, chain=False):
        nc = self.nc
        L = self.depth
        ci = 0 if g == 0 else 1
        first = (l == 0 and which == 0)
        last = (l == L - 1 and which == 1)
        with nc.cleanup_on_exit():
            ph = self.phase("ffn%d_%d_%d" % (l, which, g))
            xt_all = ph.sb("xt", [128, 8, GT], F32)
            xt = [xt_all.part((slice(None), c, slice(None)), "xt%d" % c) for c in range(8)]
            hT = ph.sb("hT", [128, 8, GT], BF16)
            actT = ph.sb("actT", [128, NFF, GT], BF16)
            rstd = ph.sb("rstd", [128, GT], F32)
            sqb = [ph.sb("sq%d" % i, [128, 512], BF16) for i in range(2)]
            tmpx = [ph.sb("tmpx%d" % i, [128, GT], F32) for i in range(2)]
            sgb = [ph.sb("sg%d" % i, [128, 512], F32) for i in range(2)]
            wgt = [ph.sb("wg%d" % i, [128, 8, 128], BF16) for i in range(2)]
            wut = [ph.sb("wu%d" % i, [128, 8, 128], BF16) for i in range(2)]
            wdt = [ph.sb("wd%d" % i, [128, NFF, 128], BF16) for i in range(2)]
            banks = [ph.ps("bk%d" % i) for i in range(8)]
            if first:
                src = self.xp if g == 0 else self.xs[(g - 1) * GT:g * GT, :]
                xin = [ph.sb("xin%d" % i, [128, D], F32) for i in range(2)]
                for tt in range(8):
                    xi = xin[tt % 2]
                    ph.dma(xi, src[tt * 128:(tt + 1) * 128, :])
                    for c2 in range(2):
                        pb = banks[(tt * 2 + c2) % 4]
                        for cc in range(4):
                            c = c2 * 4 + cc
                            ph.transpose(pb[:, cc * 128:(cc + 1) * 128], xi[:, c * 128:(c + 1) * 128], self.ident_f)
                        for cc in range(4):
                            c = c2 * 4 + cc
                            ph.copy(xt[c][:, tt * 128:(tt + 1) * 128], pb[:, cc * 128:(cc + 1) * 128],
                                    eng=("act" if c2 % 2 else "dve"))
            else:
                for c in range(8):
                    ph.dma(xt[c], self.xT[g, :, c, :])

            def core(ll, wh):
                sA, sB, sG = (0, 1, 2) if wh == 0 else (6, 7, 8)
                self.rms_stats(ph, xt, rstd, banks[4:6], sqb)
                for c in range(8):
                    tx = tmpx[c % 2]
                    ph.stt(tx, xt[c], self.mod(ll, ci, sA, c), rstd, ALU.mult, ALU.mult)
                    ph.act(hT[:, c, :], tx, AF.Identity, bias=self.mod(ll, ci, sB, c))
                for f in range(NFF):
                    wgf = wgt[f % 2]
                    wuf = wut[f % 2]
                    ph.dma(wgf, self.wg[ll, wh, :, f * 128:(f + 1) * 128].re("(k p) f -> p k f", p=128), q="pool")
                    ph.dma(wuf, self.wu[ll, wh, :, f * 128:(f + 1) * 128].re("(k p) f -> p k f", p=128), q="pool")
                    for hf in range(2):
                        j = (f * 2 + hf) % 2
                        pg, pu = banks[2 * j], banks[2 * j + 1]
                        for k in range(8):
                            ph.mm(pg, wgf[:, k, :], hT[:, k, hf * 512:(hf + 1) * 512], start=(k == 0), stop=(k == 7))
                        for k in range(8):
                            ph.mm(pu, wuf[:, k, :], hT[:, k, hf * 512:(hf + 1) * 512], start=(k == 0), stop=(k == 7))
                        sg = sgb[j]
                        ph.act(sg, pg, AF.Silu)
                        ph.tt(actT[:, f, hf * 512:(hf + 1) * 512], sg, pu, ALU.mult)
                for c in range(8):
                    wdc = wdt[c % 2]
                    ph.dma(wdc, self.wd[ll, wh, :, c * 128:(c + 1) * 128].re("(f p) d -> p f d", p=128), q="pool")
                    for hf in range(2):
                        pd = banks[6 + hf]
                        for f in range(NFF):
                            ph.mm(pd, wdc[:, f, :], actT[:, f, hf * 512:(hf + 1) * 512], start=(f == 0), stop=(f == NFF - 1))
                        xs_ = xt[c][:, hf * 512:(hf + 1) * 512]
                        ph.stt(xs_, pd, self.mod(ll, ci, sG, c), xs_, ALU.mult, ALU.add)

            core(l, which)
            lp = l
            do_proj = (which == 0)
            if which == 1 and chain and not last:
                core(l + 1, 0)
                lp = l + 1
                do_proj = True
            if last:
                self.final_out(ph, xt, rstd, banks, sqb, tmpx, g)
            else:
                for c in range(8):
                    ph.dma(self.xT[g, :, c, :], xt[c])
            if do_proj:
                self.rms_stats(ph, xt, rstd, banks[4:6], sqb)
                for c in range(8):
                    tx = tmpx[c % 2]
                    ph.stt(tx, xt[c], self.mod(lp, ci, 3, c), rstd, ALU.mult, ALU.mult)
                    ph.act(hT[:, c, :], tx, AF.Identity, bias=self.mod(lp, ci, 4, c))
                self.projections(ph, lp, g, hT, banks, sqb, sgb, tmpx, wgt + wut, [w.re("p f d -> p (f d)") for w in wdt])
            ph.emit()

    def final_out(self, ph, xt, rstd, banks, sqb, tmpx, g):
        self.rms_stats(ph, xt, rstd, banks[4:6], sqb)
        dst = self.yp if g == 0 else self.ys[(g - 1) * GT:g * GT, :]
        yt_all = ph.sb("ytall", [128, 8, GT], F32)
        for c in range(8):
            ph.stt(yt_all[:, c, :], xt[c], self.pcolA[:, 96 + c:97 + c], rstd, ALU.mult, ALU.mult)
        yo = [ph.sb("yo%d" % i, [128, D], F32) for i in range(2)]
        for tt in range(8):
            y = yo[tt % 2]
            for c2 in range(2):
                pb = banks[(tt * 2 + c2) % 4]
                for cc in range(4):
                    c = c2 * 4 + cc
                    ph.transpose(pb[:, cc * 128:(cc + 1) * 128], yt_all[:, c, tt * 128:(tt + 1) * 128], self.ident_f)
                ph.copy(y[:, c2 * 512:(c2 + 1) * 512], pb, eng=("act" if c2 else "dve"))
            ph.dma(dst[tt * 128:(tt + 1) * 128, :], y)

    def projections(self, ph, l, g, hT, banks, sqb, sgb, tmpx, wsm, wbig):
        ctx = (g == 0)
        t0 = g * GT
        W = self.w_in[l]
        st = {"b": 0, "o": 0, "w": 0}

        def bank():
            st["b"] += 1
            return banks[st["b"] % 8]

        obufs = [ph.sb("ob%d" % i, [128, 512], BF16) for i in range(4)]

        def obuf():
            st["o"] += 1
            return obufs[st["o"] % 4]

        def wfm():
            st["w"] += 1
            return wsm[st["w"] % len(wsm)]

        def pipeline(factories, nslot=2):
            active, free, it, more = [], list(range(nslot)), iter(factories), True
            while True:
                while free and more:
                    f = next(it, None)
                    if f is None:
                        more = False
                        break
                    sl = free.pop(0)
                    active.append((f(sl), sl))
                if not active:
                    break
                for item in list(active):
                    try:
                        next(item[0])
                    except StopIteration:
                        active.remove(item)
                        free.append(item[1])

        otf = [ph.sb("otf%d" % i, [128, 256], F32) for i in range(2)]
        qnb = [ph.sb("qnb%d" % i, [128, 512], BF16) for i in range(2)]
        qn2s = [ph.sb("qn2_%d" % i, [128, 2, 512], BF16) for i in range(2)]
        sqc = [ph.sb("sqc%d" % i, [128, 512], BF16) for i in range(2)]
        ta = [ph.sb("ta%d" % i, [128, 512], F32) for i in range(2)]
        tb = [ph.sb("tb%d" % i, [128, 512], F32) for i in range(2)]
        krbs = [ph.sb("krb%d" % i, [128, 512], BF16) for i in range(2)]
        eps = self.epsc[:, 0:1]
        if not ctx:
            p0 = (g - 1) * GT
            cos64 = ph.sb("cos64", [128, GT], F32)
            sin64 = ph.sb("sin64", [128, GT], F32)
            cos32 = ph.sb("cos32", [128, GT], F32)
            sin32 = ph.sb("sin32", [128, GT], F32)
            ph.dma(cos64, self.c_cos64[:, p0:p0 + GT])
            ph.dma(sin64, self.c_sin64[:, p0:p0 + GT])
            ph.dma(cos32, self.c_cos32[:, p0:p0 + GT])
            ph.dma(sin32, self.c_sin32[:, p0:p0 + GT])

        def loadw(wt, segs):
            off = 0
            for seg, n in segs:
                ph.dma(wt[:, :, off:off + n], seg.re("(k p) f -> p k f", p=128), q="pool")
                off += n

        def fm(wt, M, hf, pb):
            for k in range(8):
                ph.mm(pb[0:M, :], wt[:, k, 0:M], hT[:, k, hf * 512:(hf + 1) * 512], start=(k == 0), stop=(k == 7))

        def out_tok(src_f32, npart, ncol, hf, dst4, p_base=0):
            for tb_ in range(4):
                t = hf * 512 + tb_ * 128
                pb = bank()
                ph.transpose(pb[:, 0:npart], src_f32[p_base:p_base + npart, tb_ * 128:(tb_ + 1) * 128],
                             self.ident_f[p_base:p_base + npart, p_base:p_base + npart])
                yield
                ot = otf[tb_ % 2]
                ph.copy(ot[:, 0:npart], pb[:, 0:npart])
                ph.dma(dst4[t // SEQ, l, (t % SEQ):(t % SEQ) + 128, ncol:ncol + npart], ot[:, 0:npart])
                yield

        gq_w = {}

        def gqa_item(kind, i, hf):
            def gen(sl):
                if hf == 0:
                    wt = wfm()
                    gq_w[(kind, i)] = wt
                    if kind == "q":
                        loadw(wt, [(W[:, C_GQ + i * 64:C_GQ + (i + 1) * 64], 64), (W[:, C_GQ + (i + 3) * 64:C_GQ + (i + 4) * 64], 64)])
                    else:
                        loadw(wt, [(W[:, C_GK:C_GK + 128], 128)])
                wt = gq_w[(kind, i)]
                gcol = self.pcolA[:, 120 + l:121 + l] if kind == "q" else self.pcolA[:, 124 + l:125 + l]
                hsl = slice(hf * 512, (hf + 1) * 512)
                pq = bank()
                fm(wt, 128, hf, pq)
                yield
                sq = sqb[sl]
                ph.act(sq, pq, AF.Square)
                yield
                pm = bank()
                ph.mm(pm, self.ones64, sq)
                yield
                rs = sgb[sl]
                ph.act(rs, pm, AF.Ln, bias=eps)
                yield
                ph.act(rs, rs, AF.Exp, scale=-0.5)
                yield
                ob = obuf()
                if ctx:
                    if kind == "k":
                        knf = ta[sl]
                        ph.stt(knf, pq, gcol, rs, ALU.mult, ALU.mult)
                        yield
                        ph.copy(ob, knf, eng="act")
                        yield
                        yield from out_tok(knf, 128, 0, hf, self.ngk)
                    else:
                        ph.stt(ob, pq, gcol, rs, ALU.mult, ALU.mult)
                        yield
                else:
                    qn = qnb[sl]
                    ph.stt(qn, pq, gcol, rs, ALU.mult, ALU.mult)
                    yield
                    pr = bank()
                    ph.mm(pr, self.perm64, qn)
                    yield
                    ph.tt(ta[sl], qn, cos64[:, hsl], ALU.mult)
                    yield
                    ph.tt(tb[sl], pr, sin64[:, hsl], ALU.mult)
                    yield
                    ph.tt(ob, ta[sl], tb[sl], ALU.add)
                    yield
                dst = self.sQg[i] if kind == "q" else self.sKg
                ph.dma(dst[:, t0 + hf * 512:t0 + (hf + 1) * 512], ob)
            return gen

        pipeline([gqa_item(kind, i, hf) for kind, i in (("q", 0), ("q", 1), ("q", 2), ("k", 0)) for hf in range(2)])

        def tm(segs, N, evac):
            wt = wbig[st["w"] % 2].re("p (k n) -> p k n", k=8)[:, :, 0:N]
            st["w"] += 1
            loadw(wt, segs)
            for tt in range(8):
                pb = bank()
                for k in range(8):
                    ph.mm(pb[:, 0:N], hT[:, k, tt * 128:(tt + 1) * 128], wt[:, k, :], start=(k == 0), stop=(k == 7))
                evac(pb, tt)

        vts = [ph.sb("vt%d" % i, [128, 130], BF16) for i in range(2)]
        gts = [ph.sb("gt%d" % i, [128, 16], F32) for i in range(2)]
        mvts = [ph.sb("mvt%d" % i, [128, 260], BF16) for i in range(2)]
        mkts = [ph.sb("mkt%d" % i, [128, 256], BF16) for i in range(2)]
        vms = [ph.sb("vm%d" % i, [128, 390], BF16) for i in range(2)]
        for i in range(2):
            ph.memset(vts[i], 1.0)
            ph.memset(mvts[i], 1.0)
            ph.memset(vms[i], 1.0)

        def ev_a(pb, tt):
            vt = vts[tt % 2]
            ph.copy(vt.re("p (h e) -> p h e", e=65)[:, :, 0:64], pb[:, 0:128].re("p (h d) -> p h d", d=64))
            ph.dma(self.sVg[t0 + tt * 128:t0 + (tt + 1) * 128, :], vt)
            gt = gts[tt % 2]
            ph.tt(gt, pb[:, 128:144], self.gbias[:, l * 16:(l + 1) * 16], ALU.add)
            ph.dma(self.sGate[t0 + tt * 128:t0 + (tt + 1) * 128, :], gt)
            if ctx:
                ot = otf[tt % 2]
                ph.copy(ot[:, 0:128], pb[:, 0:128])
                t = tt * 128
                ph.dma(self.ngv[t // SEQ, l, (t % SEQ):(t % SEQ) + 128, :], ot[:, 0:128])

        def ev_b(pb, tt):
            mv = mvts[tt % 2]
            ph.copy(mv.re("p (h e) -> p h e", e=65)[:, :, 0:64], pb[:, 0:256].re("p (h d) -> p h d", d=64), eng="act")
            ph.dma(self.sMv[t0 + tt * 128:t0 + (tt + 1) * 128, :], mv)

        def ev_c(pb, tt):
            mk = mkts[tt % 2]
            ph.copy(mk, pb[:, 0:256], eng=("act" if tt % 2 else "dve"))
            ph.dma(self.sMkt[t0 + tt * 128:t0 + (tt + 1) * 128, :], mk)

        tm([(W[:, C_GV:C_GV + 128], 128), (W[:, C_MG:C_MG + 16], 16)], 144, ev_a)
        tm([(W[:, C_MV:C_MV + 256], 256)], 256, ev_b)
        tm([(W[:, C_MK:C_MK + 256], 256)], 256, ev_c)

        for kind, c in (("q", 0), ("q", 1), ("k", 0), ("k", 1)):
            wt = wfm()
            c0 = (C_MQ if kind == "q" else C_MK) + c * 128
            loadw(wt, [(W[:, c0:c0 + 128], 128)])
            for hf in range(2):
                pq = bank()
                fm(wt, 128, hf, pq)
                ob = obuf()
                if hf == 0:
                    ph.ts(ob, pq, 0.125 if kind == "q" else 1.0, ALU.mult)
                else:
                    ph.act(ob, pq, AF.Identity, scale=(0.125 if kind == "q" else 1.0))
                dst = self.sMq[c] if kind == "q" else self.sMk[c]
                ph.dma(dst[:, t0 + hf * 512:t0 + (hf + 1) * 512], ob)
        for h in range(4):
            wt = wfm()
            loadw(wt, [(W[:, C_MO + h * 64:C_MO + (h + 1) * 64], 64)])
            for hf in range(2):
                pq = bank()
                fm(wt, 64, hf, pq)
                ob = obuf()
                ph.act(ob[0:64, :], pq[0:64, :], AF.Sigmoid)
                ph.dma(self.sMo[h][:, t0 + hf * 512:t0 + (hf + 1) * 512], ob[0:64, :])

        wuq = ph.sb("wuq", [128, 2, 576], BF16)
        wukv = ph.sb("wukv", [128, 2, 768], BF16)
        wv = ph.sb("wv", [128, 2, 384], BF16)
        wkr = ph.sb("wkr", [128, 8, 96], BF16)
        ph.dma(wuq, self.w_uq[l].re("(c p) n -> p c n", p=128), q="pool")
        ph.dma(wukv, self.w_ukv[l].re("(c p) n -> p c n", p=128), q="pool")
        for c in range(2):
            ph.copy(wv[:, c, :].re("p (h d) -> p h d", d=64), wukv[:, c, :].re("p (h x) -> p h x", x=128)[:, :, 64:128], eng="pool")
        ph.memset(wkr, 0.0)
        ph.dma(wkr[:, :, 64:96], W[:, C_KR:C_KR + 32].re("(k p) f -> p k f", p=128), q="pool")

        def mla_item(kind, hf, wt):
            def gen(sl):
                gb = (0 if kind == "q" else 8) + l * 2
                hsl = slice(hf * 512, (hf + 1) * 512)
                ql = (ta[sl], tb[sl])
                sqs = (sqb[sl], sqc[sl])
                qn2 = qn2s[sl]
                krb = krbs[sl]
                for c in range(2):
                    pq = bank()
                    for k in range(8):
                        ph.mm(pq, wt[:, k, c * 128:(c + 1) * 128], hT[:, k, hsl], start=(k == 0), stop=(k == 7))
                    yield
                    ph.copy(ql[c], pq)
                    yield
                    ph.act(sqs[c], ql[c], AF.Square)
                    yield
                pm = bank()
                ph.mm(pm, self.ones256, sqs[0], start=True, stop=False)
                ph.mm(pm, self.ones256, sqs[1], start=False, stop=True)
                yield
                rs = sgb[sl]
                ph.act(rs, pm, AF.Ln, bias=eps)
                yield
                ph.act(rs, rs, AF.Exp, scale=-0.5)
                yield
                for c in range(2):
                    gcol = self.pcolB[:, gb + c:gb + c + 1]
                    if ctx and kind == "kv":
                        ph.stt(ql[c], ql[c], gcol, rs, ALU.mult, ALU.mult)
                        yield
                        ph.copy(qn2[:, c, :], ql[c], eng="act")
                        yield
                        yield from out_tok(ql[c], 128, c * 128, hf, self.nckv)
                    else:
                        ph.stt(qn2[:, c, :], ql[c], gcol, rs, ALU.mult, ALU.mult)
                        yield

                def rope32(dstb):
                    pr = bank()
                    ph.mm(pr[0:96, :], self.perm32[64:96, :], dstb[64:96, :])
                    yield
                    ph.tt(ta[sl][64:96, :], dstb[64:96, :], cos32[64:96, hsl], ALU.mult)
                    yield
                    ph.tt(tb[sl][64:96, :], pr[64:96, :], sin32[64:96, hsl], ALU.mult)
                    yield
                    ph.tt(dstb[64:96, :], ta[sl][64:96, :], tb[sl][64:96, :], ALU.add)
                    yield

                if kind == "q":
                    for h in range(6):
                        pa = bank()
                        ph.mm(pa[0:96, :], wuq[:, 0, h * 96:(h + 1) * 96], qn2[:, 0, :], start=True, stop=False)
                        ph.mm(pa[0:96, :], wuq[:, 1, h * 96:(h + 1) * 96], qn2[:, 1, :], start=False, stop=True)
                        yield
                        ob = obuf()
                        ph.copy(ob[0:96, :], pa[0:96, :], eng="act")
                        yield
                        if not ctx:
                            yield from rope32(ob)
                        ph.dma(self.sQm[h][:, t0 + hf * 512:t0 + (hf + 1) * 512], ob[0:96, :])
                else:
                    pk = bank()
                    for k in range(8):
                        ph.mm(pk[0:96, :], wkr[:, k, :], hT[:, k, hsl], start=(k == 0), stop=(k == 7))
                    yield
                    if ctx:
                        krf = ta[sl]
                        ph.copy(krf[64:96, :], pk[64:96, :])
                        yield
                        ph.copy(krb[64:96, :], krf[64:96, :], eng="act")
                        yield
                        yield from out_tok(krf, 32, 0, hf, self.nkr, p_base=64)
                    else:
                        ph.copy(krb[64:96, :], pk[64:96, :], eng="act")
                        yield
                        yield from rope32(krb)
                    for h in range(6):
                        pn = bank()
                        ph.mm(pn[0:64, :], wukv[:, 0, h * 128:h * 128 + 64], qn2[:, 0, :], start=True, stop=False)
                        ph.mm(pn[0:64, :], wukv[:, 1, h * 128:h * 128 + 64], qn2[:, 1, :], start=False, stop=True)
                        yield
                        ob = obuf()
                        ph.copy(ob[0:64, :], pn[0:64, :], eng="act")
                        ph.copy(ob[64:96, :], krb[64:96, :])
                        yield
                        ph.dma(self.sKm[h][:, t0 + hf * 512:t0 + (hf + 1) * 512], ob[0:96, :])
                    for tb_ in range(4):
                        pv = bank()
                        for c in range(2):
                            ph.mm(pv[:, 0:384], qn2[:, c, tb_ * 128:(tb_ + 1) * 128], wv[:, c, :], start=(c == 0), stop=(c == 1))
                        yield
                        vm = vms[(hf * 4 + tb_) % 2]
                        ph.copy(vm.re("p (h e) -> p h e", e=65)[:, :, 0:64], pv[:, 0:384].re("p (h d) -> p h d", d=64))
                        tok = t0 + hf * 512 + tb_ * 128
                        ph.dma(self.sVm[tok:tok + 128, :], vm)
                        yield
            return gen

        items = []
        for kind in ("q", "kv"):
            wt = wbig[st["w"] % 2].re("p (k n) -> p k n", k=8)[:, :, 0:256]
            st["w"] += 1
            c0 = C_QL if kind == "q" else C_KVL
            loadw(wt, [(W[:, c0:c0 + 256], 256)])
            for hf in range(2):
                items.append(mla_item(kind, hf, wt))
        pipeline(items)


def host_consts():
    def rope(rot_dim):
        half = rot_dim // 2
        freqs = (np.float32(10000.0) ** (-np.arange(0, half, 2, dtype=np.float32) / np.float32(half))).astype(np.float32)
        r = np.repeat(np.arange(TS // 64, dtype=np.float32), 64)
        c = np.tile(np.arange(64, dtype=np.float32), TS // 64)
        ang = np.concatenate([r[:, None] * freqs, c[:, None] * freqs], axis=-1).astype(np.float32)
        return np.cos(ang).astype(np.float32), np.sin(ang).astype(np.float32)

    c64, s64 = rope(64)
    c32, s32 = rope(32)
    p = np.arange(128)
    out = {
        "c_ident": np.eye(128, dtype=np.float32),
        "c_cos64": np.ascontiguousarray(c64[:, p % 32].T),
        "c_sin64": np.ascontiguousarray(s64[:, p % 32].T),
        "c_cos32": np.ascontiguousarray(c32[:, p % 16].T),
        "c_sin32": np.ascontiguousarray(s32[:, p % 16].T),
    }
    pm = np.zeros((128, 128), np.float32)
    for m in range(128):
        if (m % 64) < 32:
            pm[m + 32, m] = -1.0
        else:
            pm[m - 32, m] = 1.0
    out["c_perm64"] = pm
    p32 = np.zeros((128, 96), np.float32)
    for i in range(32):
        m = 64 + i
        if i < 16:
            p32[64 + i + 16, m] = -1.0
        else:
            p32[64 + i - 16, m] = 1.0
    out["c_perm32"] = p32
    s = np.arange(128)
    out["c_maskf"] = (s[:, None] <= s[None, :]).astype(np.float32)
    out["c_maskb"] = (s[:, None] >= s[None, :]).astype(np.float32)
    out["c_zero"] = np.zeros((64, PAST + TS), np.float32)
    return out


def core_inputs(inp, k, depth=DEPTH, consts=None):
    L = depth
    b = k // 4
    f = lambda a: np.ascontiguousarray(np.asarray(a, dtype=np.float32))
    m = {
        "xp": f(inp["x_prompt"][4 * k:4 * k + 4]).reshape(GT, D),
        "xs": f(inp["x_sample"][b]),
        "cond": f(np.concatenate([np.asarray(inp["c_ctx"]).reshape(8, 128), np.asarray(inp["c"][b]).reshape(8, 128)], 0)),
        "ck": f(inp["cache_gqa_k"][b][:L]).reshape(L, PAST, 128),
        "cv": f(inp["cache_gqa_v"][b][:L]).reshape(L, PAST, 128),
        "cckv": f(inp["cache_mla_ckv"][b][:L]),
        "ckr": f(inp["cache_mla_krope"][b][:L]),
        "sC": f(inp["state_mlstm_C"][b][:L]).reshape(L, 8, 64, 64),
        "sn": f(inp["state_mlstm_n"][b][:L]).reshape(L, 8, 64),
        "sm": f(inp["state_mlstm_m"][b][:L]).reshape(L, 8),
        "w_ada": f(inp["w_ada"][:L]),
        "b_ada": f(inp["b_ada"][:L]),
        "norm_w": f(inp["norm_w"][:L]).reshape(L * 24, 128),
        "ffn_w_gate": f(inp["ffn_w_gate"][:L]),
        "ffn_w_up": f(inp["ffn_w_up"][:L]),
        "ffn_w_down": f(inp["ffn_w_down"][:L]),
        "w_in": f(inp["w_in"][:L]),
        "gqa_q_norm": f(inp["gqa_q_norm"][:L]),
        "gqa_k_norm": f(inp["gqa_k_norm"][:L]),
        "mlstm_gate_b": f(inp["mlstm_gate_b"][:L]).reshape(1, L * 16),
        "mlstm_out_norm": f(inp["mlstm_out_norm"][:L]).reshape(L * 4, 64),
        "mla_q_norm": f(inp["mla_q_norm"][:L]).reshape(L * 2, 128),
        "mla_w_uq": f(inp["mla_w_uq"][:L]),
        "mla_kv_norm": f(inp["mla_kv_norm"][:L]).reshape(L * 2, 128),
        "mla_w_ukv": f(inp["mla_w_ukv"][:L]),
        "w_out": f(inp["w_out"][:L]),
        "final_norm": f(inp["final_norm"]).reshape(8, 128),
    }
    m.update(consts if consts is not None else host_consts())
    return m


def _mlstm_phase(self, l, seqs=None):
    nc = self.nc
    if seqs is None:
        seqs = [(s * SEQ, SEQ // 128, True, s) for s in range(NSEQ_P)] + [(GT, TS // 128, False, 0)]
    with nc.cleanup_on_exit():
        ph = self.phase("ml%d" % l)
        banks = [ph.ps("bk%d" % i) for i in range(8)]
        st = {"b": 0}

        def bank():
            st["b"] += 1
            return banks[st["b"] % 8]

        NB = 3
        qTs = [ph.sb("qT%d" % i, [128, 2, 128], BF16) for i in range(NB)]
        kTs = [ph.sb("kT%d" % i, [128, 2, 128], BF16) for i in range(NB)]
        kts = [ph.sb("kt%d" % i, [128, 256], BF16) for i in range(NB)]
        vts = [ph.sb("vt%d" % i, [128, 260], BF16) for i in range(NB)]
        gts = [ph.sb("gt%d" % i, [128, 16], F32) for i in range(NB)]
        mos = [ph.sb("mo%d" % i, [64, 4, 128], BF16) for i in range(NB)]

        class TSet:
            pass

        TS_ = []
        for k in range(2):
            T = TSet()
            T.e1 = ph.sb("e1_%d" % k, [128, 16], F32)
            T.l1 = ph.sb("l1_%d" % k, [128, 8], F32)
            T.c8 = ph.sb("c8_%d" % k, [128, 8], F32)
            T.d1 = ph.sb("d1_%d" % k, [128, 8], F32)
            T.u8 = ph.sb("u8_%d" % k, [128, 8], F32)
            T.wold = ph.sb("wold_%d" % k, [128, 8], F32)
            T.ec8 = ph.sb("ec8_%d" % k, [128, 8], F32)
            T.dg = ph.sb("dg_%d" % k, [128, 8, 128], BF16)
            T.expc = ph.sb("expc_%d" % k, [128, 1024], F32)
            T.EM = ph.sb("EM_%d" % k, [128, 1024], F32)
            T.Ku = ph.sb("Ku_%d" % k, [128, 256], BF16)
            T.PT = [ph.sb("PT%d_%d" % (d, k), [128, 512], BF16) for d in range(2)]
            T.PvT = [ph.sb("PvT%d_%d" % (d, k), [128, 2, 128], BF16) for d in range(2)]
            T.den = [ph.sb("den%d_%d" % (d, k), [128, 512], F32) for d in range(2)]
            T.bcs = [ph.sb("bcs%d_%d" % (d, k), [64, 512], F32) for d in range(2)]
            T.hh = [ph.sb("hh%d_%d" % (d, k), [64, 512], F32) for d in range(2)]
            T.sq = ph.sb("sq_%d" % k, [64, 512], BF16)
            T.rs = ph.sb("rs_%d" % k, [64, 512], F32)
            T.hn = ph.sb("hn_%d" % k, [64, 512], F32)
            T.catm = ph.sb("catm_%d" % k, [64, 4, 128], BF16)
            TS_.append(T)
        Sf = [ph.sb("Sf%d" % d, [128, 2, 65], F32) for d in range(2)]
        Sb = [ph.sb("Sb%d" % d, [128, 4, 65], BF16) for d in range(2)]
        Sst = ph.sb("Sst", [128, TS // 128, 4, 65], BF16)
        for d in range(2):
            ph.memset(Sb[d], 0.0)
        em0 = ph.sb("em0", [128, 8], F32)
        rows = [ph.sb("rows%d" % j, [8, 2], F32) for j in range(2)]
        rt = ph.sb("rt", [8, 8], F32)
        dg8 = ph.sb("dg8", [8, 8], F32)
        emb = ph.sb("emb", [128, 8], F32)
        Sout = ph.sb("Sout", [128, 2, 2, 65], F32)
        one_col = self.ones_f[:, 0:1]
        ld = {"n": 0}

        def gate_prep(gt, bwd_only, T):
            ph.act(T.e1, gt, AF.Exp, scale=-1.0)
            ph.act(T.l1[:, 0:4], T.e1[:, 4:8], AF.Ln, bias=one_col)
            ph.act(T.l1[:, 4:8], T.e1[:, 12:16], AF.Ln, bias=one_col)
            pg = bank()
            if not bwd_only:
                ph.mm(pg[:, 0:4], self.maskf4[:, 0:128], T.l1[:, 0:4])
                ph.mm(pg[:, 8:12], self.ones_f, T.l1[:, 0:4])
            ph.mm(pg[:, 4:8], self.maskb4[:, 0:128], T.l1[:, 4:8])
            ph.mm(pg[:, 12:16], self.ones_f, T.l1[:, 4:8])
            lo = 4 if bwd_only else 0
            ph.ts(T.c8[:, lo:8], pg[:, lo:8], -1.0, ALU.mult)
            if not bwd_only:
                ph.tt(T.d1[:, 0:4], gt[:, 0:4], T.c8[:, 0:4], ALU.subtract)
            ph.tt(T.d1[:, 4:8], gt[:, 8:12], T.c8[:, 4:8], ALU.subtract)
            ph.act(T.u8[:, lo:8], T.d1[:, lo:8], AF.Exp)
            ph.act(T.wold[:, lo:8], pg[:, 8 + lo:16], AF.Exp, scale=-1.0)

        def state_update(d, kt, vt, T):
            for h in range(4):
                ph.ts(T.Ku[:, h * 64:(h + 1) * 64], kt[:, h * 64:(h + 1) * 64], T.u8[:, d * 4 + h:d * 4 + h + 1], ALU.mult)
            pS = bank()
            for h in range(4):
                p = h // 2
                ph.mm(pS[:, h * 65:(h + 1) * 65], T.Ku[:, p * 128:(p + 1) * 128], vt[:, h * 65:(h + 1) * 65])
            for h in range(4):
                p, b = h // 2, (h % 2) * 64
                sv = Sf[d][b:b + 64, p, :]
                ph.tt(sv, pS[b:b + 64, h * 65:(h + 1) * 65], sv, ALU.add)
                ph.ts(sv, sv, T.wold[b:b + 64, d * 4 + h:d * 4 + h + 1], ALU.mult)
                ph.copy(Sb[d][b:b + 64, h, :], sv, eng="pool")

        def lockstep(gens):
            gens = list(gens)
            while gens:
                for g_ in list(gens):
                    try:
                        next(g_)
                    except StopIteration:
                        gens.remove(g_)

        def prep(j, i, T, tok, ctx):
            qT, kT, kt, vt, gt, mo = qTs[i], kTs[i], kts[i], vts[i], gts[i], mos[i]
            ph.dma(qT, self.sMq[:, :, tok:tok + 128].re("c p t -> p c t"))
            ph.dma(kT, self.sMk[:, :, tok:tok + 128].re("c p t -> p c t"))
            ph.dma(kt, self.sMkt[tok:tok + 128, :])
            ph.dma(vt, self.sMv[tok:tok + 128, :])
            ph.dma(gt, self.sGate[tok:tok + 128, :])
            ph.dma(mo, self.sMo[:, :, tok:tok + 128].re("h p t -> p h t"))
            yield
            ph.act(T.e1, gt, AF.Exp, scale=-1.0)
            yield
            ph.act(T.l1[:, 0:4], T.e1[:, 4:8], AF.Ln, bias=one_col)
            ph.act(T.l1[:, 4:8], T.e1[:, 12:16], AF.Ln, bias=one_col)
            yield
            pg = bank()
            ph.mm(pg[:, 0:4], self.maskf4[:, 0:128], T.l1[:, 0:4])
            ph.mm(pg[:, 8:12], self.ones_f, T.l1[:, 0:4])
            ph.mm(pg[:, 4:8], self.maskb4[:, 0:128], T.l1[:, 4:8])
            ph.mm(pg[:, 12:16], self.ones_f, T.l1[:, 4:8])
            yield
            ph.ts(T.c8, pg[:, 0:8], -1.0, ALU.mult)
            yield
            ph.tt(T.d1[:, 0:4], gt[:, 0:4], T.c8[:, 0:4], ALU.subtract)
            ph.tt(T.d1[:, 4:8], gt[:, 8:12], T.c8[:, 4:8], ALU.subtract)
            ph.act(T.ec8, T.c8, AF.Exp)
            yield
            ph.act(T.u8, T.d1, AF.Exp)
            ph.act(T.wold, pg[:, 8:16], AF.Exp, scale=-1.0)
            if ctx:
                pr = bank()
                ph.transpose(pr[0:8, 0:128], T.d1, self.ident_f)
                ph.mm(pr[0:8, 128:129], T.l1, one_col)
                yield
                ph.add("dve", lambda e, o=rows[j][:, 0:1].ap, a=pr[0:8, 0:128].ap: e.tensor_reduce(out=o, in_=a, axis=AX.X, op=ALU.max),
                       reads=(pr,), writes=(rows[j],))
                ph.copy(rows[j][:, 1:2], pr[0:8, 128:129])
            yield
            for hd in range(8):
                ph.ts(T.dg[:, hd, :], self.ident_b, T.ec8[:, hd:hd + 1], ALU.mult)
                if hd % 2:
                    yield
            pRs = [bank(), bank()]
            for d in range(2):
                ph.mm(pRs[d], self.ones_b, T.dg[:, d * 4:(d + 1) * 4, :].re("p a b -> p (a b)"))
            yield
            for d in range(2):
                ph.copy(T.expc[:, d * 512:(d + 1) * 512], pRs[d], eng="act")
            yield
            for d in range(2):
                ph.tt(T.EM[:, d * 512:(d + 1) * 512], pRs[d], (self.maskf4 if d == 0 else self.maskb4), ALU.mult)
                yield
            pAs = [bank(), bank()]
            for h in range(4):
                p, b = h // 2, (h % 2) * 64
                ph.mm(pAs[h % 2][:, h * 128:(h + 1) * 128], kT[b:b + 64, p, :], qT[b:b + 64, p, :])
            yield
            for d in range(2):
                for h in range(4):
                    hs = slice(h * 128, (h + 1) * 128)
                    es = slice(d * 512 + h * 128, d * 512 + (h + 1) * 128)
                    ph.stt(T.PT[d][:, hs], pAs[h % 2][:, hs], T.u8[:, d * 4 + h:d * 4 + h + 1], T.EM[:, es], ALU.mult, ALU.mult)
                    p, b = h // 2, (h % 2) * 64
                    ph.tt(T.PvT[d][b:b + 64, p, :], qT[b:b + 64, p, :], T.expc[b:b + 64, es], ALU.mult, eng="pool")
                    yield

        def finish(j, i, T, tok, ctx, last):
            kt, vt, mo = kts[i], vts[i], mos[i]
            pOs = [bank(), bank()]
            for d in range(2):
                pO = pOs[d]
                for h in range(4):
                    hs = slice(h * 128, (h + 1) * 128)
                    p = h // 2
                    ph.mm(pO[0:65, hs], vt[:, h * 65:(h + 1) * 65], T.PT[d][:, hs], start=True, stop=False)
                    sst = Sb[0][:, h, :] if d == 0 else Sst[:, j, h, :]
                    ph.mm(pO[0:65, hs], sst, T.PvT[d][:, p, :], start=False, stop=True)
                yield
            for d in range(2):
                ph.copy(T.den[d][64:65, :], pOs[d][64:65, :], eng="act")
            yield
            for d in range(2):
                ph.tt(T.den[d][64:65, :], T.den[d][64:65, :], T.den[d][64:65, :], ALU.mult)
            yield
            for d in range(2):
                ph.ts(T.den[d][64:65, :], T.den[d][64:65, :], 1.0, ALU.max)
            yield
            for d in range(2):
                ph.act(T.den[d][64:65, :], T.den[d][64:65, :], AF.Ln)
            yield
            for d in range(2):
                ph.act(T.den[d][64:65, :], T.den[d][64:65, :], AF.Exp, scale=-0.5)
            yield
            pBs = [bank(), bank()]
            for d in range(2):
                ph.mm(pBs[d][0:64, :], self.ones_f[64:65, 0:64], T.den[d][64:65, :])
            yield
            for d in range(2):
                ph.copy(T.bcs[d], pBs[d][0:64, :], eng="act")
            yield
            for d in range(2):
                ph.tt(T.hh[d], pOs[d][0:64, :], T.bcs[d], ALU.mult)
                yield
            ph.tt(T.hh[0], T.hh[0], T.hh[1], ALU.add)
            yield
            ph.tt(T.sq, T.hh[0], T.hh[0], ALU.mult)
            yield
            pM = bank()
            ph.mm(pM[0:64, :], self.ones64[0:64, 0:64], T.sq)
            yield
            ph.act(T.rs, pM[0:64, :], AF.Ln, bias=self.epsc[0:64, 0:1])
            yield
            ph.act(T.rs, T.rs, AF.Exp, scale=-0.5)
            yield
            for h in range(4):
                hs = slice(h * 128, (h + 1) * 128)
                ph.stt(T.hn[:, hs], T.hh[0][:, hs], self.pcolB[0:64, 16 + l * 4 + h:17 + l * 4 + h], T.rs[:, hs], ALU.mult, ALU.mult)
                if h % 2:
                    yield
            ph.tt(T.catm, T.hn.re("p (h t) -> p h t", h=4), mo, ALU.mult)
            ph.dma(self.sCatM[:, :, tok:tok + 128].re("h p t -> p h t"), T.catm)
            yield
            if ctx or not last:
                for h in range(4):
                    ph.ts(T.Ku[:, h * 64:(h + 1) * 64], kt[:, h * 64:(h + 1) * 64], T.u8[:, h:h + 1], ALU.mult)
                yield
                pS = bank()
                for h in range(4):
                    p = h // 2
                    ph.mm(pS[:, h * 65:(h + 1) * 65], T.Ku[:, p * 128:(p + 1) * 128], vt[:, h * 65:(h + 1) * 65])
                yield
                for h in range(4):
                    p, b = h // 2, (h % 2) * 64
                    sv = Sf[0][b:b + 64, p, :]
                    ph.tt(sv, pS[b:b + 64, h * 65:(h + 1) * 65], sv, ALU.add)
                    ph.ts(sv, sv, T.wold[b:b + 64, h:h + 1], ALU.mult)
                    ph.copy(Sb[0][b:b + 64, h, :], sv, eng="pool")
                    yield

        for (tok0, nblk, ctx, sidx) in seqs:
            if ctx:
                for d in range(2):
                    ph.memset(Sf[d], 0.0)
                    for h in range(4):
                        b = (h % 2) * 64
                        ph.memset(Sb[d][b:b + 64, h, :], 0.0)
            else:
                ph.dma(em0, V(self.sm.ap[l:l + 1, :].partition_broadcast(128), self.sm.buf))
                ph.act(em0, em0, AF.Exp)
                for hd in range(8):
                    d, h = hd // 4, hd % 4
                    p, b = h // 2, (h % 2) * 64
                    ph.dma(Sf[d][b:b + 64, p, 0:64], self.sC[l, hd])
                    ph.dma(Sf[d][b:b + 64, p, 64:65], self.sn[l, hd:hd + 1, :].re("o d -> d o"), allow_slow_non_contiguous=True)
                for hd in range(8):
                    d, h = hd // 4, hd % 4
                    p, b = h // 2, (h % 2) * 64
                    sv = Sf[d][b:b + 64, p, :]
                    ph.ts(sv, sv, em0[b:b + 64, hd:hd + 1], ALU.mult)
                    ph.copy(Sb[d][b:b + 64, h, :], sv, eng="pool")
            pending = None
            for j in range(nblk - 1, -1, -1):
                i = ld["n"] % NB
                ld["n"] += 1
                T = TS_[j % 2]
                tok = tok0 + j * 128
                need = ctx or j > 0
                if need:
                    ph.dma(kts[i], self.sMkt[tok:tok + 128, :])
                    ph.dma(vts[i], self.sMv[tok:tok + 128, :])
                    ph.dma(gts[i], self.sGate[tok:tok + 128, :])
                    gate_prep(gts[i], True, T)
                if pending is not None:
                    state_update(1, *pending)
                ph.copy(Sst[:, j], Sb[1], eng="pool")
                pending = (kts[i], vts[i], T) if need else None
            if pending is not None:
                state_update(1, *pending)
            slot = {}
            for j in range(nblk):
                i = ld["n"] % NB
                ld["n"] += 1
                slot[j] = (i, TS_[j % 2], tok0 + j * 128)
                gens = [prep(j, slot[j][0], slot[j][1], slot[j][2], ctx)]
                if j >= 1:
                    gens.append(finish(j - 1, slot[j - 1][0], slot[j - 1][1], slot[j - 1][2], ctx, False))
                lockstep(gens)
            jl = nblk - 1
            lockstep([finish(jl, slot[jl][0], slot[jl][1], slot[jl][2], ctx, True)])
            if ctx:
                r0, r1 = rows[0], rows[1]
                fsel = self.maskf4[0:8, 3:4]
                bsel = self.maskb4[0:8, 4:5]
                ph.ts(rt[:, 0:1], r0[:, 1:2], -1.0, ALU.mult)
                ph.ts(rt[:, 1:2], r1[:, 1:2], -1.0, ALU.mult)
                ph.tt(rt[:, 2:3], rt[:, 0:1], rt[:, 1:2], ALU.add)
                ph.stt(rt[:, 3:4], rt[:, 1:2], fsel, rt[:, 0:1], ALU.mult, ALU.add)
                ph.tt(rt[:, 3:4], rt[:, 3:4], r0[:, 0:1], ALU.add)
                ph.stt(rt[:, 4:5], rt[:, 0:1], bsel, rt[:, 1:2], ALU.mult, ALU.add)
                ph.tt(rt[:, 4:5], rt[:, 4:5], r1[:, 0:1], ALU.add)
                ph.tt(rt[:, 5:6], rt[:, 3:4], rt[:, 4:5], ALU.max)
                ph.tt(rt[:, 5:6], rt[:, 5:6], rt[:, 2:3], ALU.max)
                ph.act(rt[:, 6:7], rt[:, 5:6], AF.Exp, scale=-1.0)
                ph.ts(dg8, self.ident_f[0:8, 0:8], rt[:, 6:7], ALU.mult)
                pE = bank()
                ph.mm(pE[:, 0:8], self.ones_f[0:8, :], dg8)
                ph.copy(emb, pE[:, 0:8])
                for hd in range(8):
                    d, h = hd // 4, hd % 4
                    p, b = h // 2, (h % 2) * 64
                    ph.ts(Sout[b:b + 64, d, p, :], Sf[d][b:b + 64, p, :], emb[b:b + 64, hd:hd + 1], ALU.mult)
                for hd in range(8):
                    d, h = hd // 4, hd % 4
                    p, b = h // 2, (h % 2) * 64
                    ph.dma(self.nC[sidx, l, hd], Sout[b:b + 64, d, p, 0:64])
                    ph.dma(self.nn[sidx, l, hd:hd + 1, :].re("o d -> d o"), Sout[b:b + 64, d, p, 64:65], allow_slow_non_contiguous=True)
                ph.dma(self.nm[sidx, l:l + 1, :].re("o d -> d o"), rt[:, 5:6], allow_slow_non_contiguous=True)
        ph.emit()


Builder.mlstm_phase = _mlstm_phase


def _attn_phase(self, l, sample, qt_limit=None):
    nc = self.nc
    with nc.cleanup_on_exit():
        ph = self.phase("at%d_%d" % (l, int(sample)))
        NKC = (PAST + TS) // 128 if sample else SEQ // 128
        NK = NKC * 128
        QN = 512 if sample else 256
        NX = 3
        xts = [ph.ps("xs%d" % i, (128, 2 * QN)) for i in range(NX)]
        obank = ph.ps("obank")
        nbank = ph.ps("nwbank")
        wbank = nbank
        banks = [obank, nbank, xts[0][:, 0:512], xts[1][:, 0:512]]
        NSET = 1 if sample else 2
        KgTs = [ph.sb("KgT%d" % i, [128, NK], BF16) for i in range(NSET)]
        Vgs = [ph.sb("Vg%d" % i, [128, NKC, 130], BF16) for i in range(NSET)]
        KmTs = [ph.sb("KmT%d" % i, [128, 6, NK], BF16) for i in range(NSET)]
        Vms = [ph.sb("Vm%d" % i, [128, NKC, 390], BF16) for i in range(NSET)]
        KgT, Vg, KmT, Vm = KgTs[0], Vgs[0], KmTs[0], Vms[0]
        qgs = [ph.sb("qg%d" % i, [128, 6, QN], BF16) for i in range(2)]
        qms = [ph.sb("qm%d" % i, [128, 6, QN], BF16) for i in range(2)]
        for i in range(2):
            ph.dma(qgs[i][64:128, 0:3, :], self.c_zero[0:64, 0:3 * QN].re("p (h t) -> p h t", h=3), q="pool")
            ph.dma(qgs[i][0:64, 3:6, :], self.c_zero[0:64, 0:3 * QN].re("p (h t) -> p h t", h=3), q="pool")
            ph.dma(qms[i][96:128, :, :], self.c_zero[0:32, 0:6 * QN].re("p (h t) -> p h t", h=6), q="pool")
        for km in KmTs:
            for h in range(6):
                ph.dma(km[96:128, h, :], self.c_zero[0:32, 0:NK], q="pool")
        pts = [ph.sb("pt%d" % i, [128, 2 * QN], BF16) for i in range(NX)]
        osbs = [ph.sb("osb%d" % i, [128, 512], F32) for i in range(2)]
        catT = ph.sb("catT", [128, 16, QN], BF16)
        woT = ph.sb("woT", [128, 16, D], BF16)
        ph.memset(catT[64:128], 0.0, eng="dve")
        ph.memset(woT[64:128], 0.0, eng="dve")
        xcs = [ph.sb("xc%d" % i, [128, QN], F32) for i in range(2)]
        dens = [ph.sb("den%d" % i, [128, 512], F32) for i in range(2)]
        bcs = ph.sb("bcs", [64, 512], F32)
        DEF = 1 if NKC // 2 == 1 else 2
        ph.dma(woT[0:64], self.w_out[l].re("(s p) d -> p s d", p=64), q="pool")
        st = {"b": 0}

        def bank():
            st["b"] += 1
            return banks[st["b"] % 4]

        if sample:
            ckt = ph.sb("ckt", [128, 2, 128], F32)
            ckvt = ph.sb("ckvt", [128, 2, 256], F32)
            ckvT = ph.sb("ckvT", [128, 2, 256], BF16)
            krs = ph.sb("krs", [128, 2, 96], F32)
            krT = ph.sb("krT", [128, 256], BF16)
            wukv = ph.sb("wukv", [128, 2, 768], BF16)
            wv = ph.sb("wv", [128, 2, 384], BF16)
            ph.memset(Vg[:, 0:2, :], 1.0)
            ph.memset(Vm[:, 0:2, :], 1.0)
            ph.dma(wukv, self.w_ukv[l].re("(c p) n -> p c n", p=128), q="pool")
            for c in range(2):
                ph.copy(wv[:, c, :].re("p (h d) -> p h d", d=64), wukv[:, c, :].re("p (h x) -> p h x", x=128)[:, :, 64:128], eng="pool")
            ph.dma(ckt, self.ck[l].re("(c p) d -> p c d", p=128))
            ph.dma(ckvt, self.cckv[l].re("(c p) d -> p c d", p=128))
            ph.memset(krs, 0.0)
            ph.dma(krs[:, :, 64:96], self.ckr[l].re("(c p) d -> p c d", p=128))
            for kc in range(2):
                ph.dma(Vg[:, kc, :].re("p (h e) -> p h e", e=65)[:, :, 0:64],
                       self.cv[l, kc * 128:(kc + 1) * 128, :].re("p (h d) -> p h d", d=64), q="pool")
            for kc in range(2):
                pb = bank()
                ph.transpose(pb[:, 0:128], ckt[:, kc, :], self.ident_f)
                ph.copy(KgT[:, kc * 128:(kc + 1) * 128], pb[:, 0:128])
                for c in range(2):
                    pb = bank()
                    ph.transpose(pb[:, 0:128], ckvt[:, kc, c * 128:(c + 1) * 128], self.ident_f)
                    ph.copy(ckvT[:, c, kc * 128:(kc + 1) * 128], pb[:, 0:128])
                pb = bank()
                ph.transpose(pb[0:96, 0:128], krs[:, kc, :], self.ident_f)
                ph.copy(krT[64:96, kc * 128:(kc + 1) * 128], pb[64:96, 0:128])
            for h in range(6):
                pb = bank()
                for c in range(2):
                    ph.mm(pb[0:64, 0:256], wukv[:, c, h * 128:h * 128 + 64], ckvT[:, c, :], start=(c == 0), stop=(c == 1))
                ph.copy(KmT[0:64, h, 0:256], pb[0:64, 0:256], eng="act")
                ph.copy(KmT[64:96, h, 0:256], krT[64:96, :], eng="pool")
            for kc in range(2):
                pb = bank()
                for c in range(2):
                    ph.mm(pb[:, 0:384], ckvT[:, c, kc * 128:(kc + 1) * 128], wv[:, c, :], start=(c == 0), stop=(c == 1))
                ph.copy(Vm[:, kc, :].re("p (h e) -> p h e", e=65)[:, :, 0:64], pb[:, 0:384].re("p (h d) -> p h d", d=64))
            ph.dma(KgT[:, PAST:], self.sKg[:, GT:TT])
            for q4 in range(4):
                r0 = GT + q4 * GT
                ph.dma(Vg[:, 2 + q4 * 8:2 + (q4 + 1) * 8, :], self.sVg[r0:r0 + GT, :].re("(c p) e -> p c e", p=128))
                ph.dma(Vm[:, 2 + q4 * 8:2 + (q4 + 1) * 8, :], self.sVm[r0:r0 + GT, :].re("(c p) e -> p c e", p=128))
            for h in range(6):
                ph.dma(KmT[0:96, h, PAST:], self.sKm[h][:, GT:TT])
            tiles = [(GT + i * 512, 1 + i // 2, (i % 2) * 512) for i in range(TS // 512)]
        else:
            tiles = [(s * SEQ, 0, s * SEQ) for s in range(NSEQ_P)]
        if qt_limit is not None:
            tiles = tiles[:qt_limit]

        heads = [("g", h) for h in range(6)] + [("m", h) for h in range(6)]
        NP = NKC // 2
        for ti, (tok, g, xoff) in enumerate(tiles):
            ci = 0 if g == 0 else 1
            if not sample:
                KgT, Vg, KmT, Vm = KgTs[ti % 2], Vgs[ti % 2], KmTs[ti % 2], Vms[ti % 2]
                ph.dma(KgT, self.sKg[:, tok:tok + SEQ])
                ph.dma(Vg, self.sVg[tok:tok + SEQ, :].re("(c p) e -> p c e", p=128))
                ph.dma(Vm, self.sVm[tok:tok + SEQ, :].re("(c p) e -> p c e", p=128))
                for h in range(6):
                    ph.dma(KmT[0:96, h, :], self.sKm[h][:, tok:tok + SEQ])
            qg, qm = qgs[ti % 2], qms[ti % 2]
            for j in range(2):
                ph.dma(qg[j * 64:(j + 1) * 64, 3 * j:3 * j + 3, :], self.sQg[:, j * 64:(j + 1) * 64, tok:tok + QN].re("i p t -> p i t"))
            ph.dma(qm[0:96], self.sQm[:, :, tok:tok + QN].re("h p t -> p h t"))
            ph.dma(catT[0:64, 6:10, :], self.sCatM[:, :, tok:tok + QN].re("h p t -> p h t"))
            pairs = [(hi, kp) for hi in range(12) for kp in range(NP)]
            n = len(pairs)
            pend = []

            def s_pair(i):
                hi, kp = pairs[i]
                kind, h = heads[hi]
                X = xts[i % NX]
                for u in range(2):
                    kc = 2 * kp + u
                    if kind == "g":
                        ph.mm(X[:, u * QN:(u + 1) * QN], KgT[:, kc * 128:(kc + 1) * 128], qg[:, h, :])
                    else:
                        ph.mm(X[:, u * QN:(u + 1) * QN], KmT[:, h, kc * 128:(kc + 1) * 128], qm[:, h, :])

            def pv_pair(i):
                hi, kp = pairs[i]
                kind, h = heads[hi]
                X = xts[i % NX]
                pt = pts[i % NX]
                ob = obank
                ph.act(pt[:, 0:2 * QN], X[:, 0:2 * QN], AF.Exp, scale=(0.125 if kind == "g" else 96.0 ** -0.5))
                for u in range(2):
                    kc = 2 * kp + u
                    vt = Vg[:, kc, (h // 3) * 65:(h // 3 + 1) * 65] if kind == "g" else Vm[:, kc, h * 65:(h + 1) * 65]
                    ph.mm(ob[0:65, 0:QN], vt, pt[:, u * QN:(u + 1) * QN], start=(kc == 0), stop=(kc == NKC - 1))
                if kp == NP - 1:
                    slot = h if kind == "g" else 10 + h
                    den = dens[hi % 2]
                    osb = osbs[hi % 2]
                    ph.copy(osb[0:65, 0:QN], ob[0:65, 0:QN])
                    ph.recip(den[64:65, 0:QN], osb[64:65, 0:QN])

                    def fin(osb=osb, slot=slot, den=den):
                        ph.mm(nbank[0:64, 0:QN], self.ones_f[64:65, 0:64], den[64:65, 0:QN])
                        ph.tt(catT[0:64, slot, :], osb[0:64, 0:QN], nbank[0:64, 0:QN], ALU.mult)
                    pend.append((i + DEF, fin))

            LK = NX - 1
            for i in range(n + LK):
                if i < n:
                    s_pair(i)
                if i >= LK:
                    pv_pair(i - LK)
                while pend and pend[0][0] <= i - LK:
                    pend.pop(0)[1]()
            while pend:
                pend.pop(0)[1]()
            for c in range(8):
                xc = xcs[c % 2]
                ph.dma(xc, self.xT[g, :, c, xoff:xoff + QN])
                for s in range(16):
                    ph.mm(wbank[:, 0:QN], woT[:, s, c * 128:(c + 1) * 128], catT[:, s, :], start=(s == 0), stop=(s == 15))
                ph.stt(xc, wbank[:, 0:QN], self.mod(l, ci, 5, c), xc, ALU.mult, ALU.add)
                ph.dma(self.xT[g, :, c, xoff:xoff + QN], xc)
        ph.emit()


Builder.attn_phase = _attn_phase


def build_program(depth=DEPTH, debug=False):
    B = Builder(depth=depth, debug=debug)
    B.prologue()
    for g in range(NG):
        B.ffn_phase(0, 0, g)
    for l in range(depth):
        B.mlstm_phase(l)
        B.attn_phase(l, False)
        B.attn_phase(l, True)
        for g in range(NG):
            B.ffn_phase(l, 1, g, chain=True)
    return B


def kernel(**inputs):
    inp = {k: np.asarray(v) for k, v in inputs.items()}
    B = build_program(DEPTH)
    consts = host_consts()
    in_maps = [core_inputs(inp, k, DEPTH, consts) for k in range(8)]
    res = run_bass_kernel_spmd(B.nc, in_maps, core_ids=list(range(8)))
    r = res.results
    L = DEPTH
    y_prompt = np.concatenate([r[k]["yp"].reshape(NSEQ_P, SEQ, D) for k in range(8)], 0)
    y_sample = np.stack([r[0]["ys"], r[4]["ys"]], 0)
    cat = lambda name, shp: np.concatenate([r[k][name].reshape((NSEQ_P,) + shp) for k in range(8)], 0)
    new_k = cat("ngk", (L, SEQ, 2, 64))
    new_v = cat("ngv", (L, SEQ, 2, 64))
    new_ckv = cat("nckv", (L, SEQ, 256))
    new_kr = cat("nkr", (L, SEQ, 32))
    new_C = cat("nC", (L, 2, 4, 64, 64))
    new_n = cat("nn", (L, 2, 4, 64))
    new_m = cat("nm", (L, 2, 4))
    outs = (y_prompt, y_sample, new_k, new_v, new_ckv, new_kr, new_C, new_n, new_m)
    return tuple(np.ascontiguousarray(o, dtype=np.float32) for o in outs)
```

```python
import os
import numpy as np
import concourse.bass as bass
import concourse.mybir as mybir
from concourse.bass_utils import run_bass_kernel_spmd

F32 = mybir.dt.float32
BF16 = mybir.dt.bfloat16
AF = mybir.ActivationFunctionType
ALU = mybir.AluOpType
AX = mybir.AxisListType

D = 1024
DFF = 2816
NFF = DFF // 128
DEPTH = 4
NSEQ_P = 4
SEQ = 256
TS = 4096
PAST = 256
GT = 1024
NG = 5
TT = NG * GT
EPS = 1e-6
DIN = 2224
SAME_ENGINE_SYNC = True


class Buf:
    __slots__ = ("name", "writers", "readers", "sem", "ndma", "last_dma", "excl")

    def __init__(self, name, excl=False):
        self.name = name
        self.excl = excl
        self.writers = []
        self.readers = []
        self.sem = None
        self.ndma = 0
        self.last_dma = None

    def reset(self):
        self.writers = []
        self.readers = []
        self.sem = None
        self.ndma = 0
        self.last_dma = None


class V:
    __slots__ = ("ap", "buf")

    def __init__(self, ap, buf):
        self.ap = ap
        self.buf = buf

    def __getitem__(self, idx):
        return V(self.ap[idx], self.buf)

    def part(self, idx, name):
        return V(self.ap[idx], Buf(name))

    def re(self, pat, **kw):
        return V(self.ap.rearrange(pat, **kw), self.buf)

    def bc(self, dt):
        return V(self.ap.bitcast(dt), self.buf)


class Op:
    __slots__ = ("eng", "fn", "deps", "tick", "needs_inc", "is_dma", "key", "dma_idx", "pos")

    def __init__(self, eng, fn, is_dma=False, key=None):
        self.eng = eng
        self.fn = fn
        self.deps = []
        self.tick = 0
        self.needs_inc = is_dma
        self.is_dma = is_dma
        self.key = key
        self.dma_idx = 0
        self.pos = 0


class Phase:
    ENGS = ("pe", "act", "dve", "pool", "sp")

    def __init__(self, nc, name):
        self.nc = nc
        self.name = name
        self.ops = []
        self.touched = {}

    def sb(self, name, shape, dt):
        t = self.nc.alloc_sbuf_tensor(self.name + "_" + name, list(shape), dt)
        return V(t.ap(), Buf(name))

    def ps(self, name, shape=(128, 512), dt=F32):
        t = self.nc.alloc_psum_tensor(self.name + "_" + name, list(shape), dt)
        return V(t.ap(), Buf(name, excl=True))

    def _touch(self, b):
        self.touched[id(b)] = b

    def add(self, eng, fn, reads=(), writes=(), is_dma=False, key=None):
        op = Op(eng, fn, is_dma, key)
        op.pos = len(self.ops)
        deps = []
        rb = [v.buf for v in reads]
        wb = [v.buf for v in writes]
        for b in rb + wb:
            self._touch(b)
        raw = set()
        for b in rb:
            deps.extend(b.writers)
            raw.update(id(w) for w in b.writers)
            if b.excl:
                deps.extend(r for r in b.readers if r.eng != eng)
        for b in wb:
            deps.extend(b.writers)
            deps.extend(b.readers)
        if is_dma:
            self._touch(key)
            if key.last_dma is not None:
                deps.append(key.last_dma)
            key.ndma += 1
            op.dma_idx = key.ndma
            key.last_dma = op
        for b in wb:
            if b.readers:
                b.writers = [op]
                b.readers = []
            else:
                b.writers = [w for w in b.writers if (w.is_dma or w.eng != eng or is_dma)] + [op]
        for b in rb:
            if b not in wb:
                b.readers = [r for r in b.readers if (r.is_dma or r.eng != eng or is_dma)] + [op]
        seen = set()
        for d in deps:
            if d is op or id(d) in seen:
                continue
            seen.add(id(d))
            if (not d.is_dma) and (not is_dma) and d.eng == eng:
                if eng == "pe" or not SAME_ENGINE_SYNC or id(d) not in raw:
                    continue
            op.deps.append(d)
            d.needs_inc = True
        self.ops.append(op)
        return op

    def mm(self, out, lhsT, rhs, start=True, stop=True, **kw):
        return self.add("pe", lambda e: e.matmul(out.ap, lhsT.ap, rhs.ap, start=start, stop=stop, **kw),
                        reads=(lhsT, rhs), writes=(out,))

    def transpose(self, out, in_, ident):
        return self.add("pe", lambda e: e.transpose(out.ap, in_.ap, ident.ap), reads=(in_, ident), writes=(out,))

    def act(self, out, in_, func, bias=None, scale=None, extra_reads=(), accum_out=None):
        kw = {}
        rd = [in_] + list(extra_reads)
        wr = [out]
        if bias is not None:
            if isinstance(bias, V):
                kw["bias"] = bias.ap
                rd.append(bias)
            else:
                kw["bias"] = bias
        if scale is not None:
            if isinstance(scale, V):
                kw["scale"] = scale.ap
                rd.append(scale)
            else:
                kw["scale"] = scale
        if accum_out is not None:
            kw["accum_out"] = accum_out.ap
            wr.append(accum_out)
        return self.add("act", lambda e: e.activation(out=out.ap, in_=in_.ap, func=func, **kw), reads=rd, writes=wr)

    def tt(self, out, in0, in1, op, eng="dve"):
        return self.add(eng, lambda e: e.tensor_tensor(out=out.ap, in0=in0.ap, in1=in1.ap, op=op),
                        reads=(in0, in1), writes=(out,))

    def ts(self, out, in0, s1, op0, s2=None, op1=None, eng="dve"):
        rd = [in0]
        a1 = s1
        a2 = s2
        if isinstance(s1, V):
            rd.append(s1)
            a1 = s1.ap
        if isinstance(s2, V):
            rd.append(s2)
            a2 = s2.ap
        if op1 is None:
            return self.add(eng, lambda e: e.tensor_scalar(out=out.ap, in0=in0.ap, scalar1=a1, scalar2=None, op0=op0),
                            reads=rd, writes=(out,))
        return self.add(eng, lambda e: e.tensor_scalar(out=out.ap, in0=in0.ap, scalar1=a1, scalar2=a2, op0=op0, op1=op1),
                        reads=rd, writes=(out,))

    def stt(self, out, in0, scalar, in1, op0, op1):
        rd = [in0, in1]
        a = scalar
        if isinstance(scalar, V):
            rd.append(scalar)
            a = scalar.ap
        return self.add("dve", lambda e: e.scalar_tensor_tensor(out=out.ap, in0=in0.ap, scalar=a, in1=in1.ap, op0=op0, op1=op1),
                        reads=rd, writes=(out,))

    def copy(self, out, in_, eng="dve"):
        if eng == "act":
            return self.add("act", lambda e: e.copy(out=out.ap, in_=in_.ap), reads=(in_,), writes=(out,))
        return self.add(eng, lambda e: e.tensor_copy(out=out.ap, in_=in_.ap), reads=(in_,), writes=(out,))

    def recip(self, out, in_):
        return self.add("dve", lambda e: e.reciprocal(out=out.ap, in_=in_.ap), reads=(in_,), writes=(out,))

    def rsqrt_ln(self, out, in_, eps_ap, tmp=None):
        t = out if tmp is None else tmp
        self.act(t, in_, AF.Ln, bias=eps_ap)
        self.act(out, t, AF.Exp, scale=-0.5)

    def memset(self, out, val, eng="pool"):
        return self.add(eng, lambda e: e.memset(out.ap, val), reads=(), writes=(out,))

    def dma(self, out, in_, q="sp", key=None, **kw):
        if key is None:
            key = in_.buf if str(out.ap.space) == "DRAM" else out.buf
        return self.add(q, lambda e: e.dma_start(out=out.ap, in_=in_.ap, **kw), reads=(in_,), writes=(out,),
                        is_dma=True, key=key)

    def emit(self):
        nc = self.nc
        sems = {}
        for en in ("pe", "act", "dve", "pool"):
            sems[en] = nc.alloc_semaphore(self.name + "_s_" + en)
        keys = []
        cnt = {en: 0 for en in ("pe", "act", "dve", "pool")}
        for op in self.ops:
            if op.is_dma:
                if op.key.sem is None:
                    op.key.sem = nc.alloc_semaphore(self.name + "_d_" + op.key.name + str(len(keys)))
                    keys.append(op.key)
                op.tick = 16 * op.dma_idx
            elif op.needs_inc:
                cnt[op.eng] += 1
                op.tick = cnt[op.eng]
        streams = {en: [] for en in self.ENGS}
        for op in self.ops:
            streams[op.eng].append(op)

        def run_stream(en, e):
            waited = {}
            for op in streams[en]:
                need = {}
                for d in op.deps:
                    s = d.key.sem if d.is_dma else sems[d.eng]
                    k = id(s)
                    if k not in need or need[k][1] < d.tick:
                        need[k] = (s, d.tick)
                for k, (s, val) in need.items():
                    if waited.get(k, 0) >= val:
                        continue
                    waited[k] = val
                    e.wait_ge(s, val)
                ins = op.fn(e)
                if op.is_dma:
                    ins.then_inc(op.key.sem, 16)
                elif op.needs_inc:
                    ins.then_inc(sems[op.eng], 1)
            if en == "sp":
                for kb in keys:
                    e.wait_ge(kb.sem, 16 * kb.ndma)

        with nc.Block(self.name) as block:
            @block.tensor
            def _(e):
                run_stream("pe", e)

            @block.scalar
            def _(e):
                run_stream("act", e)

            @block.vector
            def _(e):
                run_stream("dve", e)

            @block.gpsimd
            def _(e):
                run_stream("pool", e)

            @block.sync
            def _(e):
                run_stream("sp", e)
        for b in self.touched.values():
            b.reset()
        for op in self.ops:
            op.fn = None
        self.ops = []


C_GQ, C_GK, C_GV, C_MQ, C_MK, C_MV, C_MO, C_MG, C_QL, C_KVL, C_KR = 0, 384, 512, 640, 896, 1152, 1408, 1664, 1680, 1936, 2192


class Builder:
    def __init__(self, depth=DEPTH, debug=False, stop_after=None):
        self.depth = depth
        self.debug = debug
        self.stop_after = stop_after
        nc = bass.Bass("TRN2", target_bir_lowering=False)
        self.nc = nc
        self.pn = 0
        L = depth

        def din(name, shape, dt=F32):
            return V(nc.dram_tensor(name, list(shape), dt, kind="ExternalInput").ap(), Buf(name))

        def dout(name, shape, dt=F32):
            return V(nc.dram_tensor(name, list(shape), dt, kind="ExternalOutput").ap(), Buf(name))

        def dscr(name, shape, dt=BF16):
            kind = "ExternalOutput" if debug else "Internal"
            return V(nc.dram_tensor(name, list(shape), dt, kind=kind).ap(), Buf(name))

        self.xp = din("xp", [GT, D])
        self.xs = din("xs", [TS, D])
        self.cond = din("cond", [16, 128])
        self.ck = din("ck", [L, PAST, 128])
        self.cv = din("cv", [L, PAST, 128])
        self.cckv = din("cckv", [L, PAST, 256])
        self.ckr = din("ckr", [L, PAST, 32])
        self.sC = din("sC", [L, 8, 64, 64])
        self.sn = din("sn", [L, 8, 64])
        self.sm = din("sm", [L, 8])
        self.w_ada = din("w_ada", [L, D, 9 * D])
        self.b_ada = din("b_ada", [L, 9 * D])
        self.norm_w = din("norm_w", [L * 3 * 8, 128])
        self.wg = din("ffn_w_gate", [L, 2, D, DFF])
        self.wu = din("ffn_w_up", [L, 2, D, DFF])
        self.wd = din("ffn_w_down", [L, 2, DFF, D])
        self.w_in = din("w_in", [L, D, DIN])
        self.gqn = din("gqa_q_norm", [L, 64])
        self.gkn = din("gqa_k_norm", [L, 64])
        self.mgb = din("mlstm_gate_b", [1, L * 16])
        self.mon = din("mlstm_out_norm", [L * 4, 64])
        self.mqn = din("mla_q_norm", [L * 2, 128])
        self.w_uq = din("mla_w_uq", [L, 256, 576])
        self.mkvn = din("mla_kv_norm", [L * 2, 128])
        self.w_ukv = din("mla_w_ukv", [L, 256, 768])
        self.w_out = din("w_out", [L, D, D])
        self.fnorm = din("final_norm", [8, 128])
        self.c_ident = din("c_ident", [128, 128])
        self.c_cos64 = din("c_cos64", [128, TS])
        self.c_sin64 = din("c_sin64", [128, TS])
        self.c_cos32 = din("c_cos32", [128, TS])
        self.c_sin32 = din("c_sin32", [128, TS])
        self.c_perm64 = din("c_perm64", [128, 128])
        self.c_perm32 = din("c_perm32", [128, 96])
        self.c_maskf = din("c_maskf", [128, 128])
        self.c_maskb = din("c_maskb", [128, 128])
        self.yp = dout("yp", [GT, D])
        self.ys = dout("ys", [TS, D])
        self.ngk = dout("ngk", [NSEQ_P, L, SEQ, 128])
        self.ngv = dout("ngv", [NSEQ_P, L, SEQ, 128])
        self.nckv = dout("nckv", [NSEQ_P, L, SEQ, 256])
        self.nkr = dout("nkr", [NSEQ_P, L, SEQ, 32])
        self.nC = dout("nC", [NSEQ_P, L, 8, 64, 64])
        self.nn = dout("nn", [NSEQ_P, L, 8, 64])
        self.nm = dout("nm", [NSEQ_P, L, 8])
        self.xT = dscr("xT", [NG, 128, 8, GT], F32)
        self.sQg = dscr("sQg", [3, 128, TT])
        self.sKg = dscr("sKg", [128, TT])
        self.sVg = dscr("sVg", [TT, 130])
        self.sQm = dscr("sQm", [6, 96, TT])
        self.sKm = dscr("sKm", [6, 96, TT])
        self.sVm = dscr("sVm", [TT, 390])
        self.sMq = dscr("sMq", [2, 128, TT])
        self.sMk = dscr("sMk", [2, 128, TT])
        self.sMkt = dscr("sMkt", [TT, 256])
        self.sMv = dscr("sMv", [TT, 260])
        self.sGate = dscr("sGate", [TT, 16], F32)
        self.sMo = dscr("sMo", [4, 64, TT])
        self.sCatM = dscr("sCatM", [4, 64, TT])

        def gsb(name, shape, dt):
            return V(nc.alloc_sbuf_tensor("g_" + name, list(shape), dt).ap(), Buf("g_" + name))

        self.ident_f = gsb("identf", [128, 128], F32)
        self.ident_b = gsb("identb", [128, 128], BF16)
        self.ones_dm = gsb("onesdm", [128, 128], BF16)
        self.ones64 = gsb("ones64", [128, 128], BF16)
        self.ones256 = gsb("ones256", [128, 128], BF16)
        self.ones_f = gsb("onesf", [128, 128], F32)
        self.ones_b = gsb("onesb", [128, 128], BF16)
        self.perm64 = gsb("perm64", [128, 128], BF16)
        self.perm32 = gsb("perm32", [128, 96], BF16)
        self.maskf4 = gsb("maskf4", [128, 512], F32)
        self.maskb4 = gsb("maskb4", [128, 512], F32)
        self.pcolA = gsb("pcolA", [128, 128], F32)
        self.pcolB = gsb("pcolB", [128, 32], F32)
        self.modc = gsb("modc", [128, L * 2 * 72], F32)
        self.gbias = gsb("gbias", [128, L * 16], F32)
        self.epsc = gsb("epsc", [128, 1], F32)

    def phase(self, name):
        self.pn += 1
        return Phase(self.nc, "p%d%s" % (self.pn, name))

    def mod(self, l, ci, slot, c=None):
        base = ((l * 2 + ci) * 9 + slot) * 8
        if c is None:
            return self.modc[:, base:base + 8]
        return self.modc[:, base + c:base + c + 1]

    def prologue(self):
        nc = self.nc
        L = self.depth
        with nc.cleanup_on_exit():
            ph = self.phase("pro")
            stA = ph.sb("stA", [128, 128], F32)
            stB = ph.sb("stB", [128, 128], F32)
            tmpf = ph.sb("tmpf", [128, 128], F32)
            tmp96 = ph.sb("tmp96", [128, 96], F32)
            ph.dma(self.ident_f, self.c_ident)
            ph.copy(self.ident_b, self.ident_f)
            ph.dma(tmpf, self.c_perm64)
            ph.copy(self.perm64, tmpf)
            ph.dma(tmp96, self.c_perm32)
            ph.copy(self.perm32, tmp96)
            for j in range(4):
                ph.dma(self.maskf4[:, j * 128:(j + 1) * 128], self.c_maskf)
                ph.dma(self.maskb4[:, j * 128:(j + 1) * 128], self.c_maskb)
            ph.memset(self.ones_dm, 1.0 / 1024.0)
            ph.memset(self.ones256, 1.0 / 256.0)
            ph.memset(self.ones_f, 1.0)
            ph.memset(self.ones_b, 1.0)
            ph.memset(self.epsc, EPS)
            ph.memset(self.ones64, 0.0)
            ph.memset(self.ones64[0:64, 0:64], 1.0 / 64.0)
            ph.memset(self.ones64[64:128, 64:128], 1.0 / 64.0)
            ph.dma(self.gbias, V(self.mgb.ap.partition_broadcast(128), self.mgb.buf))
            ph.memset(stA, 0.0)
            ph.memset(stB, 0.0)
            ph.dma(stA[0:L * 24, :], self.norm_w)
            ph.dma(stA[96:104, :], self.fnorm)
            ph.dma(stA[104:120, :], self.cond)
            ph.dma(stA[120:120 + L, 0:64], self.gqn)
            ph.dma(stA[120:120 + L, 64:128], self.gqn)
            ph.dma(stA[124:124 + L, 0:64], self.gkn)
            ph.dma(stA[124:124 + L, 64:128], self.gkn)
            ph.dma(stB[0:2 * L, :], self.mqn)
            ph.dma(stB[8:8 + 2 * L, :], self.mkvn)
            ph.dma(stB[16:16 + 4 * L, 0:64], self.mon)
            psT = ph.ps("psT")
            ph.transpose(psT[:, 0:128], stA, self.ident_f)
            ph.copy(self.pcolA, psT[:, 0:128])
            ph.transpose(psT[:, 128:256], stB, self.ident_f)
            ph.copy(self.pcolB, psT[:, 128:160])
            scT = ph.sb("scT", [128, 16], BF16)
            ph.act(scT, self.pcolA[:, 104:120], AF.Silu)
            onesr = ph.sb("onesr", [1, 2], BF16)
            ph.memset(onesr, 1.0)
            NB = 512
            wts = [ph.sb("wada%d" % i, [128, 8, NB], BF16) for i in range(3)]
            brow = ph.sb("brow", [1, 9 * D], BF16)
            psm = [ph.ps("psm%d" % i) for i in range(2)]
            mods = ph.sb("mods", [128, 72, 2], F32)
            wi = 0
            for l in range(L):
                ph.dma(brow, self.b_ada[l:l + 1, :], q="pool")
                pm = psm[l % 2]
                for nb in range(9 * D // NB):
                    wt = wts[wi % 3]
                    wi += 1
                    ph.dma(wt, self.w_ada[l, :, nb * NB:(nb + 1) * NB].re("(k p) f -> p k f", p=128), q="pool")
                    for jj in range(NB // 128):
                        j = nb * (NB // 128) + jj
                        for k in range(8):
                            ph.mm(pm[:, 2 * j:2 * j + 2], wt[:, k, jj * 128:(jj + 1) * 128],
                                  scT[:, :].re("p (c k) -> p c k", k=8)[:, :, k], start=(k == 0), stop=False)
                        ph.mm(pm[:, 2 * j:2 * j + 2], brow[0:1, j * 128:(j + 1) * 128], onesr[0:1, :], start=False, stop=True)
                ph.copy(mods, pm[:, 0:144].re("p (j c) -> p j c", c=2))
                for ci in range(2):
                    for i in range(3):
                        nw = self.pcolA[:, (l * 3 + i) * 8:(l * 3 + i) * 8 + 8]
                        sh = mods[:, (3 * i) * 8:(3 * i) * 8 + 8, ci]
                        sc = mods[:, (3 * i + 1) * 8:(3 * i + 1) * 8 + 8, ci]
                        gg = mods[:, (3 * i + 2) * 8:(3 * i + 2) * 8 + 8, ci]
                        ph.stt(self.mod(l, ci, 3 * i), sc, 1.0, nw, ALU.add, ALU.mult)
                        ph.copy(self.mod(l, ci, 3 * i + 1), sh)
                        ph.ts(self.mod(l, ci, 3 * i + 2), gg, 1.0 if i == 1 else 0.5, ALU.mult)
            ph.emit()

    def rms_stats(self, ph, xt, rstd, banks, sqb):
        for hf in range(2):
            pb = banks[hf % len(banks)]
            for c in range(8):
                sq = sqb[c % len(sqb)]
                ph.act(sq, xt[c][:, hf * 512:(hf + 1) * 512], AF.Square)
                ph.mm(pb, self.ones_dm, sq, start=(c == 0), stop=(c == 7))
            sd = sqb[0].buf
            ph.rsqrt_ln(rstd[:, hf * 512:(hf + 1) * 512], pb, self.epsc[:, 0:1])

    def ffn_phase(self, l, which, g, chain=False):
        nc = self.nc
        L = self.depth
        ci = 0 if g == 0 else 1
        first = (l == 0 and which == 0)
        last = (l == L - 1 and which == 1)
        with nc.cleanup_on_exit():
            ph = self.phase("ffn%d_%d_%d" % (l, which, g))
            xt_all = ph.sb("xt", [128, 8, GT], F32)
            xt = [xt_all.part((slice(None), c, slice(None)), "xt%d" % c) for c in range(8)]
            hT = ph.sb("hT", [128, 8, GT], BF16)
            actT = ph.sb("actT", [128, NFF, GT], BF16)
            rstd = ph.sb("rstd", [128, GT], F32)
            sqb = [ph.sb("sq%d" % i, [128, 512], BF16) for i in range(2)]
            tmpx = [ph.sb("tmpx%d" % i, [128, GT], F32) for i in range(2)]
            sgb = [ph.sb("sg%d" % i, [128, 512], F32) for i in range(2)]
            wgt = [ph.sb("wg%d" % i, [128, 8, 128], BF16) for i in range(2)]
            wut = [ph.sb("wu%d" % i, [128, 8, 128], BF16) for i in range(2)]
            wdt = [ph.sb("wd%d" % i, [128, NFF, 128], BF16) for i in range(2)]
            banks = [ph.ps("bk%d" % i) for i in range(8)]
            if first:
                src = self.xp if g == 0 else self.xs[(g - 1) * GT:g * GT, :]
                xin = [ph.sb("xin%d" % i, [128, D], F32) for i in range(2)]
                for tt in range(8):
                    xi = xin[tt % 2]
                    ph.dma(xi, src[tt * 128:(tt + 1) * 128, :])
                    for c2 in range(2):
                        pb = banks[(tt * 2 + c2) % 4]
                        for cc in range(4):
                            c = c2 * 4 + cc
                            ph.transpose(pb[:, cc * 128:(cc + 1) * 128], xi[:, c * 128:(c + 1) * 128], self.ident_f)
                        for cc in range(4):
                            c = c2 * 4 + cc
                            ph.copy(xt[c][:, tt * 128:(tt + 1) * 128], pb[:, cc * 128:(cc + 1) * 128],
                                    eng=("act" if c2 % 2 else "dve"))
            else:
                for c in range(8):
                    ph.dma(xt[c], self.xT[g, :, c, :])

            def core(ll, wh):
                sA, sB, sG = (0, 1, 2) if wh == 0 else (6, 7, 8)
                self.rms_stats(ph, xt, rstd, banks[4:6], sqb)
                for c in range(8):
                    tx = tmpx[c % 2]
                    ph.stt(tx, xt[c], self.mod(ll, ci, sA, c), rstd, ALU.mult, ALU.mult)
                    ph.act(hT[:, c, :], tx, AF.Identity, bias=self.mod(ll, ci, sB, c))
                for f in range(NFF):
                    wgf = wgt[f % 2]
                    wuf = wut[f % 2]
                    ph.dma(wgf, self.wg[ll, wh, :, f * 128:(f + 1) * 128].re("(k p) f -> p k f", p=128), q="pool")
                    ph.dma(wuf, self.wu[ll, wh, :, f * 128:(f + 1) * 128].re("(k p) f -> p k f", p=128), q="pool")
                    for hf in range(2):
                        j = (f * 2 + hf) % 2
                        pg, pu = banks[2 * j], banks[2 * j + 1]
                        for k in range(8):
                            ph.mm(pg, wgf[:, k, :], hT[:, k, hf * 512:(hf + 1) * 512], start=(k == 0), stop=(k == 7))
                        for k in range(8):
                            ph.mm(pu, wuf[:, k, :], hT[:, k, hf * 512:(hf + 1) * 512], start=(k == 0), stop=(k == 7))
                        sg = sgb[j]
                        ph.act(sg, pg, AF.Silu)
                        ph.tt(actT[:, f, hf * 512:(hf + 1) * 512], sg, pu, ALU.mult)
                for c in range(8):
                    wdc = wdt[c % 2]
                    ph.dma(wdc, self.wd[ll, wh, :, c * 128:(c + 1) * 128].re("(f p) d -> p f d", p=128), q="pool")
                    for hf in range(2):
                        pd = banks[6 + hf]
                        for f in range(NFF):
                            ph.mm(pd, wdc[:, f, :], actT[:, f, hf * 512:(hf + 1) * 512], start=(f == 0), stop=(f == NFF - 1))
                        xs_ = xt[c][:, hf * 512:(hf + 1) * 512]
                        ph.stt(xs_, pd, self.mod(ll, ci, sG, c), xs_, ALU.mult, ALU.add)

            core(l, which)
            lp = l
            do_proj = (which == 0)
            if which == 1 and chain and not last:
                core(l + 1, 0)
                lp = l + 1
                do_proj = True
            if last:
                self.final_out(ph, xt, rstd, banks, sqb, tmpx, g)
            else:
                for c in range(8):
                    ph.dma(self.xT[g, :, c, :], xt[c])
            if do_proj:
                self.rms_stats(ph, xt, rstd, banks[4:6], sqb)
                for c in range(8):
                    tx = tmpx[c % 2]
                    ph.stt(tx, xt[c], self.mod(lp, ci, 3, c), rstd, ALU.mult, ALU.mult)
                    ph.act(hT[:, c, :], tx, AF.Identity, bias=self.mod(lp, ci, 4, c))
                self.projections(ph, lp, g, hT, banks, sqb, sgb, tmpx, wgt + wut, [w.re("p f d -> p (f d)") for w in wdt])
            ph.emit()

    def final_out(self, ph, xt, rstd, banks, sqb, tmpx, g):
        self.rms_stats(ph, xt, rstd, banks[4:6], sqb)
        dst = self.yp if g == 0 else self.ys[(g - 1) * GT:g * GT, :]
        yt_all = ph.sb("ytall", [128, 8, GT], F32)
        for c in range(8):
            ph.stt(yt_all[:, c, :], xt[c], self.pcolA[:, 96 + c:97 + c], rstd, ALU.mult, ALU.mult)
        yo = [ph.sb("yo%d" % i, [128, D], F32) for i in range(2)]
        for tt in range(8):
            y = yo[tt % 2]
            for c2 in range(2):
                pb = banks[(tt * 2 + c2) % 4]
                for cc in range(4):
                    c = c2 * 4 + cc
                    ph.transpose(pb[:, cc * 128:(cc + 1) * 128], yt_all[:, c, tt * 128:(tt + 1) * 128], self.ident_f)
                ph.copy(y[:, c2 * 512:(c2 + 1) * 512], pb, eng=("act" if c2 else "dve"))
            ph.dma(dst[tt * 128:(tt + 1) * 128, :], y)

    def projections(self, ph, l, g, hT, banks, sqb, sgb, tmpx, wsm, wbig):
        ctx = (g == 0)
        t0 = g * GT
        W = self.w_in[l]
        st = {"b": 0, "o": 0, "w": 0}

        def bank():
            st["b"] += 1
            return banks[st["b"] % 8]

        obufs = [ph.sb("ob%d" % i, [128, 512], BF16) for i in range(4)]

        def obuf():
            st["o"] += 1
            return obufs[st["o"] % 4]

        def wfm():
            st["w"] += 1
            return wsm[st["w"] % len(wsm)]

        def pipeline(factories, nslot=2):
            active, free, it, more = [], list(range(nslot)), iter(factories), True
            while True:
                while free and more:
                    f = next(it, None)
                    if f is None:
                        more = False
                        break
                    sl = free.pop(0)
                    active.append((f(sl), sl))
                if not active:
                    break
                for item in list(active):
                    try:
                        next(item[0])
                    except StopIteration:
                        active.remove(item)
                        free.append(item[1])

        otf = [ph.sb("otf%d" % i, [128, 256], F32) for i in range(2)]
        qnb = [ph.sb("qnb%d" % i, [128, 512], BF16) for i in range(2)]
        qn2s = [ph.sb("qn2_%d" % i, [128, 2, 512], BF16) for i in range(2)]
        sqc = [ph.sb("sqc%d" % i, [128, 512], BF16) for i in range(2)]
        ta = [ph.sb("ta%d" % i, [128, 512], F32) for i in range(2)]
        tb = [ph.sb("tb%d" % i, [128, 512], F32) for i in range(2)]
        krbs = [ph.sb("krb%d" % i, [128, 512], BF16) for i in range(2)]
        eps = self.epsc[:, 0:1]
        if not ctx:
            p0 = (g - 1) * GT
            cos64 = ph.sb("cos64", [128, GT], F32)
            sin64 = ph.sb("sin64", [128, GT], F32)
            cos32 = ph.sb("cos32", [128, GT], F32)
            sin32 = ph.sb("sin32", [128, GT], F32)
            ph.dma(cos64, self.c_cos64[:, p0:p0 + GT])
            ph.dma(sin64, self.c_sin64[:, p0:p0 + GT])
            ph.dma(cos32, self.c_cos32[:, p0:p0 + GT])
            ph.dma(sin32, self.c_sin32[:, p0:p0 + GT])

        def loadw(wt, segs):
            off = 0
            for seg, n in segs:
                ph.dma(wt[:, :, off:off + n], seg.re("(k p) f -> p k f", p=128), q="pool")
                off += n

        def fm(wt, M, hf, pb):
            for k in range(8):
                ph.mm(pb[0:M, :], wt[:, k, 0:M], hT[:, k, hf * 512:(hf + 1) * 512], start=(k == 0), stop=(k == 7))

        def out_tok(src_f32, npart, ncol, hf, dst4, p_base=0):
            for tb_ in range(4):
                t = hf * 512 + tb_ * 128
                pb = bank()
                ph.transpose(pb[:, 0:npart], src_f32[p_base:p_base + npart, tb_ * 128:(tb_ + 1) * 128],
                             self.ident_f[p_base:p_base + npart, p_base:p_base + npart])
                yield
                ot = otf[tb_ % 2]
                ph.copy(ot[:, 0:npart], pb[:, 0:npart])
                ph.dma(dst4[t // SEQ, l, (t % SEQ):(t % SEQ) + 128, ncol:ncol + npart], ot[:, 0:npart])
                yield

        gq_w = {}

        def gqa_item(kind, i, hf):
            def gen(sl):
                if hf == 0:
                    wt = wfm()
                    gq_w[(kind, i)] = wt
                    if kind == "q":
                        loadw(wt, [(W[:, C_GQ + i * 64:C_GQ + (i + 1) * 64], 64), (W[:, C_GQ + (i + 3) * 64:C_GQ + (i + 4) * 64], 64)])
                    else:
                        loadw(wt, [(W[:, C_GK:C_GK + 128], 128)])
                wt = gq_w[(kind, i)]
                gcol = self.pcolA[:, 120 + l:121 + l] if kind == "q" else self.pcolA[:, 124 + l:125 + l]
                hsl = slice(hf * 512, (hf + 1) * 512)
                pq = bank()
                fm(wt, 128, hf, pq)
                yield
                sq = sqb[sl]
                ph.act(sq, pq, AF.Square)
                yield
                pm = bank()
                ph.mm(pm, self.ones64, sq)
                yield
                rs = sgb[sl]
                ph.act(rs, pm, AF.Ln, bias=eps)
                yield
                ph.act(rs, rs, AF.Exp, scale=-0.5)
                yield
                ob = obuf()
                if ctx:
                    if kind == "k":
                        knf = ta[sl]
                        ph.stt(knf, pq, gcol, rs, ALU.mult, ALU.mult)
                        yield
                        ph.copy(ob, knf, eng="act")
                        yield
                        yield from out_tok(knf, 128, 0, hf, self.ngk)
                    else:
                        ph.stt(ob, pq, gcol, rs, ALU.mult, ALU.mult)
                        yield
                else:
                    qn = qnb[sl]
                    ph.stt(qn, pq, gcol, rs, ALU.mult, ALU.mult)
                    yield
                    pr = bank()
                    ph.mm(pr, self.perm64, qn)
                    yield
                    ph.tt(ta[sl], qn, cos64[:, hsl], ALU.mult)
                    yield
                    ph.tt(tb[sl], pr, sin64[:, hsl], ALU.mult)
                    yield
                    ph.tt(ob, ta[sl], tb[sl], ALU.add)
                    yield
                dst = self.sQg[i] if kind == "q" else self.sKg
                ph.dma(dst[:, t0 + hf * 512:t0 + (hf + 1) * 512], ob)
            return gen

        pipeline([gqa_item(kind, i, hf) for kind, i in (("q", 0), ("q", 1), ("q", 2), ("k", 0)) for hf in range(2)])

        def tm(segs, N, evac):
            wt = wbig[st["w"] % 2].re("p (k n) -> p k n", k=8)[:, :, 0:N]
            st["w"] += 1
            loadw(wt, segs)
            for tt in range(8):
                pb = bank()
                for k in range(8):
                    ph.mm(pb[:, 0:N], hT[:, k, tt * 128:(tt + 1) * 128], wt[:, k, :], start=(k == 0), stop=(k == 7))
                evac(pb, tt)

        vts = [ph.sb("vt%d" % i, [128, 130], BF16) for i in range(2)]
        gts = [ph.sb("gt%d" % i, [128, 16], F32) for i in range(2)]
        mvts = [ph.sb("mvt%d" % i, [128, 260], BF16) for i in range(2)]
        mkts = [ph.sb("mkt%d" % i, [128, 256], BF16) for i in range(2)]
        vms = [ph.sb("vm%d" % i, [128, 390], BF16) for i in range(2)]
        for i in range(2):
            ph.memset(vts[i], 1.0)
            ph.memset(mvts[i], 1.0)
            ph.memset(vms[i], 1.0)

        def ev_a(pb, tt):
            vt = vts[tt % 2]
            ph.copy(vt.re("p (h e) -> p h e", e=65)[:, :, 0:64], pb[:, 0:128].re("p (h d) -> p h d", d=64))
            ph.dma(self.sVg[t0 + tt * 128:t0 + (tt + 1) * 128, :], vt)
            gt = gts[tt % 2]
            ph.tt(gt, pb[:, 128:144], self.gbias[:, l * 16:(l + 1) * 16], ALU.add)
            ph.dma(self.sGate[t0 + tt * 128:t0 + (tt + 1) * 128, :], gt)
            if ctx:
                ot = otf[tt % 2]
                ph.copy(ot[:, 0:128], pb[:, 0:128])
                t = tt * 128
                ph.dma(self.ngv[t // SEQ, l, (t % SEQ):(t % SEQ) + 128, :], ot[:, 0:128])

        def ev_b(pb, tt):
            mv = mvts[tt % 2]
            ph.copy(mv.re("p (h e) -> p h e", e=65)[:, :, 0:64], pb[:, 0:256].re("p (h d) -> p h d", d=64), eng="act")
            ph.dma(self.sMv[t0 + tt * 128:t0 + (tt + 1) * 128, :], mv)

        def ev_c(pb, tt):
            mk = mkts[tt % 2]
            ph.copy(mk, pb[:, 0:256], eng=("act" if tt % 2 else "dve"))
            ph.dma(self.sMkt[t0 + tt * 128:t0 + (tt + 1) * 128, :], mk)

        tm([(W[:, C_GV:C_GV + 128], 128), (W[:, C_MG:C_MG + 16], 16)], 144, ev_a)
        tm([(W[:, C_MV:C_MV + 256], 256)], 256, ev_b)
        tm([(W[:, C_MK:C_MK + 256], 256)], 256, ev_c)

        for kind, c in (("q", 0), ("q", 1), ("k", 0), ("k", 1)):
            wt = wfm()
            c0 = (C_MQ if kind == "q" else C_MK) + c * 128
            loadw(wt, [(W[:, c0:c0 + 128], 128)])
            for hf in range(2):
                pq = bank()
                fm(wt, 128, hf, pq)
                ob = obuf()
                if hf == 0:
                    ph.ts(ob, pq, 0.125 if kind == "q" else 1.0, ALU.mult)
                else:
                    ph.act(ob, pq, AF.Identity, scale=(0.125 if kind == "q" else 1.0))
                dst = self.sMq[c] if kind == "q" else self.sMk[c]
                ph.dma(dst[:, t0 + hf * 512:t0 + (hf + 1) * 512], ob)
        for h in range(4):
            wt = wfm()
            loadw(wt, [(W[:, C_MO + h * 64:C_MO + (h + 1) * 64], 64)])
            for hf in range(2):
                pq = bank()
                fm(wt, 64, hf, pq)
                ob = obuf()
                ph.act(ob[0:64, :], pq[0:64, :], AF.Sigmoid)
                ph.dma(self.sMo[h][:, t0 + hf * 512:t0 + (hf + 1) * 512], ob[0:64, :])

        wuq = ph.sb("wuq", [128, 2, 576], BF16)
        wukv = ph.sb("wukv", [128, 2, 768], BF16)
        wv = ph.sb("wv", [128, 2, 384], BF16)
        wkr = ph.sb("wkr", [128, 8, 96], BF16)
        ph.dma(wuq, self.w_uq[l].re("(c p) n -> p c n", p=128), q="pool")
        ph.dma(wukv, self.w_ukv[l].re("(c p) n -> p c n", p=128), q="pool")
        for c in range(2):
            ph.copy(wv[:, c, :].re("p (h d) -> p h d", d=64), wukv[:, c, :].re("p (h x) -> p h x", x=128)[:, :, 64:128], eng="pool")
        ph.memset(wkr, 0.0)
        ph.dma(wkr[:, :, 64:96], W[:, C_KR:C_KR + 32].re("(k p) f -> p k f", p=128), q="pool")

        def mla_item(kind, hf, wt):
            def gen(sl):
                gb = (0 if kind == "q" else 8) + l * 2
                hsl = slice(hf * 512, (hf + 1) * 512)
                ql = (ta[sl], tb[sl])
                sqs = (sqb[sl], sqc[sl])
                qn2 = qn2s[sl]
                krb = krbs[sl]
                for c in range(2):
                    pq = bank()
                    for k in range(8):
                        ph.mm(pq, wt[:, k, c * 128:(c + 1) * 128], hT[:, k, hsl], start=(k == 0), stop=(k == 7))
                    yield
                    ph.copy(ql[c], pq)
                    yield
                    ph.act(sqs[c], ql[c], AF.Square)
                    yield
                pm = bank()
                ph.mm(pm, self.ones256, sqs[0], start=True, stop=False)
                ph.mm(pm, self.ones256, sqs[1], start=False, stop=True)
                yield
                rs = sgb[sl]
                ph.act(rs, pm, AF.Ln, bias=eps)
                yield
                ph.act(rs, rs, AF.Exp, scale=-0.5)
                yield
                for c in range(2):
                    gcol = self.pcolB[:, gb + c:gb + c + 1]
                    if ctx and kind == "kv":
                        ph.stt(ql[c], ql[c], gcol, rs, ALU.mult, ALU.mult)
                        yield
                        ph.copy(qn2[:, c, :], ql[c], eng="act")
                        yield
                        yield from out_tok(ql[c], 128, c * 128, hf, self.nckv)
                    else:
                        ph.stt(qn2[:, c, :], ql[c], gcol, rs, ALU.mult, ALU.mult)
                        yield

                def rope32(dstb):
                    pr = bank()
                    ph.mm(pr[0:96, :], self.perm32[64:96, :], dstb[64:96, :])
                    yield
                    ph.tt(ta[sl][64:96, :], dstb[64:96, :], cos32[64:96, hsl], ALU.mult)
                    yield
                    ph.tt(tb[sl][64:96, :], pr[64:96, :], sin32[64:96, hsl], ALU.mult)
                    yield
                    ph.tt(dstb[64:96, :], ta[sl][64:96, :], tb[sl][64:96, :], ALU.add)
                    yield

                if kind == "q":
                    for h in range(6):
                        pa = bank()
                        ph.mm(pa[0:96, :], wuq[:, 0, h * 96:(h + 1) * 96], qn2[:, 0, :], start=True, stop=False)
                        ph.mm(pa[0:96, :], wuq[:, 1, h * 96:(h + 1) * 96], qn2[:, 1, :], start=False, stop=True)
                        yield
                        ob = obuf()
                        ph.copy(ob[0:96, :], pa[0:96, :], eng="act")
                        yield
                        if not ctx:
                            yield from rope32(ob)
                        ph.dma(self.sQm[h][:, t0 + hf * 512:t0 + (hf + 1) * 512], ob[0:96, :])
                else:
                    pk = bank()
                    for k in range(8):
                        ph.mm(pk[0:96, :], wkr[:, k, :], hT[:, k, hsl], start=(k == 0), stop=(k == 7))
                    yield
                    if ctx:
                        krf = ta[sl]
                        ph.copy(krf[64:96, :], pk[64:96, :])
                        yield
                        ph.copy(krb[64:96, :], krf[64:96, :], eng="act")
                        yield
                        yield from out_tok(krf, 32, 0, hf, self.nkr, p_base=64)
                    else:
                        ph.copy(krb[64:96, :], pk[64:96, :], eng="act")
                        yield
                        yield from rope32(krb)
                    for h in range(6):
                        pn = bank()
                        ph.mm(pn[0:64, :], wukv[:, 0, h * 128:h * 128 + 64], qn2[:, 0, :], start=True, stop=False)
                        ph.mm(pn[0:64, :], wukv[:, 1, h * 128:h * 128 + 64], qn2[:, 1, :], start=False, stop=True)
                        yield
                        ob = obuf()
                        ph.copy(ob[0:64, :], pn[0:64, :], eng="act")
                        ph.copy(ob[64:96, :], krb[64:96, :])
                        yield
                        ph.dma(self.sKm[h][:, t0 + hf * 512:t0 + (hf + 1) * 512], ob[0:96, :])
                    for tb_ in range(4):
                        pv = bank()
                        for c in range(2):
                            ph.mm(pv[:, 0:384], qn2[:, c, tb_ * 128:(tb_ + 1) * 128], wv[:, c, :], start=(c == 0), stop=(c == 1))
                        yield
                        vm = vms[(hf * 4 + tb_) % 2]
                        ph.copy(vm.re("p (h e) -> p h e", e=65)[:, :, 0:64], pv[:, 0:384].re("p (h d) -> p h d", d=64))
                        tok = t0 + hf * 512 + tb_ * 128
                        ph.dma(self.sVm[tok:tok + 128, :], vm)
                        yield
            return gen

        items = []
        for kind in ("q", "kv"):
            wt = wbig[st["w"] % 2].re("p (k n) -> p k n", k=8)[:, :, 0:256]
            st["w"] += 1
            c0 = C_QL if kind == "q" else C_KVL
            loadw(wt, [(W[:, c0:c0 + 256], 256)])
            for hf in range(2):
                items.append(mla_item(kind, hf, wt))
        pipeline(items)


def host_consts():
    def rope(rot_dim):
        half = rot_dim // 2
        freqs = (np.float32(10000.0) ** (-np.arange(0, half, 2, dtype=np.float32) / np.float32(half))).astype(np.float32)
        r = np.repeat(np.arange(TS // 64, dtype=np.float32), 64)
        c = np.tile(np.arange(64, dtype=np.float32), TS // 64)
        ang = np.concatenate([r[:, None] * freqs, c[:, None] * freqs], axis=-1).astype(np.float32)
        return np.cos(ang).astype(np.float32), np.sin(ang).astype(np.float32)

    c64, s64 = rope(64)
    c32, s32 = rope(32)
    p = np.arange(128)
    out = {
        "c_ident": np.eye(128, dtype=np.float32),
        "c_cos64": np.ascontiguousarray(c64[:, p % 32].T),
        "c_sin64": np.ascontiguousarray(s64[:, p % 32].T),
        "c_cos32": np.ascontiguousarray(c32[:, p % 16].T),
        "c_sin32": np.ascontiguousarray(s32[:, p % 16].T),
    }
    pm = np.zeros((128, 128), np.float32)
    for m in range(128):
        if (m % 64) < 32:
            pm[m + 32, m] = -1.0
        else:
            pm[m - 32, m] = 1.0
    out["c_perm64"] = pm
    p32 = np.zeros((128, 96), np.float32)
    for i in range(32):
        m = 64 + i
        if i < 16:
            p32[64 + i + 16, m] = -1.0
        else:
            p32[64 + i - 16, m] = 1.0
    out["c_perm32"] = p32
    s = np.arange(128)
    out["c_maskf"] = (s[:, None] <= s[None, :]).astype(np.float32)
    out["c_maskb"] = (s[:, None] >= s[None, :]).astype(np.float32)
    return out


def core_inputs(inp, k, depth=DEPTH, consts=None):
    L = depth
    b = k // 4
    f = lambda a: np.ascontiguousarray(np.asarray(a, dtype=np.float32))
    m = {
        "xp": f(inp["x_prompt"][4 * k:4 * k + 4]).reshape(GT, D),
        "xs": f(inp["x_sample"][b]),
        "cond": f(np.concatenate([np.asarray(inp["c_ctx"]).reshape(8, 128), np.asarray(inp["c"][b]).reshape(8, 128)], 0)),
        "ck": f(inp["cache_gqa_k"][b][:L]).reshape(L, PAST, 128),
        "cv": f(inp["cache_gqa_v"][b][:L]).reshape(L, PAST, 128),
        "cckv": f(inp["cache_mla_ckv"][b][:L]),
        "ckr": f(inp["cache_mla_krope"][b][:L]),
        "sC": f(inp["state_mlstm_C"][b][:L]).reshape(L, 8, 64, 64),
        "sn": f(inp["state_mlstm_n"][b][:L]).reshape(L, 8, 64),
        "sm": f(inp["state_mlstm_m"][b][:L]).reshape(L, 8),
        "w_ada": f(inp["w_ada"][:L]),
        "b_ada": f(inp["b_ada"][:L]),
        "norm_w": f(inp["norm_w"][:L]).reshape(L * 24, 128),
        "ffn_w_gate": f(inp["ffn_w_gate"][:L]),
        "ffn_w_up": f(inp["ffn_w_up"][:L]),
        "ffn_w_down": f(inp["ffn_w_down"][:L]),
        "w_in": f(inp["w_in"][:L]),
        "gqa_q_norm": f(inp["gqa_q_norm"][:L]),
        "gqa_k_norm": f(inp["gqa_k_norm"][:L]),
        "mlstm_gate_b": f(inp["mlstm_gate_b"][:L]).reshape(1, L * 16),
        "mlstm_out_norm": f(inp["mlstm_out_norm"][:L]).reshape(L * 4, 64),
        "mla_q_norm": f(inp["mla_q_norm"][:L]).reshape(L * 2, 128),
        "mla_w_uq": f(inp["mla_w_uq"][:L]),
        "mla_kv_norm": f(inp["mla_kv_norm"][:L]).reshape(L * 2, 128),
        "mla_w_ukv": f(inp["mla_w_ukv"][:L]),
        "w_out": f(inp["w_out"][:L]),
        "final_norm": f(inp["final_norm"]).reshape(8, 128),
    }
    m.update(consts if consts is not None else host_consts())
    return m


def _mlstm_phase(self, l, seqs=None):
    nc = self.nc
    if seqs is None:
        seqs = [(s * SEQ, SEQ // 128, True, s) for s in range(NSEQ_P)] + [(GT, TS // 128, False, 0)]
    with nc.cleanup_on_exit():
        ph = self.phase("ml%d" % l)
        banks = [ph.ps("bk%d" % i) for i in range(8)]
        st = {"b": 0}

        def bank():
            st["b"] += 1
            return banks[st["b"] % 8]

        NB = 3
        qTs = [ph.sb("qT%d" % i, [128, 2, 128], BF16) for i in range(NB)]
        kTs = [ph.sb("kT%d" % i, [128, 2, 128], BF16) for i in range(NB)]
        kts = [ph.sb("kt%d" % i, [128, 256], BF16) for i in range(NB)]
        vts = [ph.sb("vt%d" % i, [128, 260], BF16) for i in range(NB)]
        gts = [ph.sb("gt%d" % i, [128, 16], F32) for i in range(NB)]
        mos = [ph.sb("mo%d" % i, [64, 4, 128], BF16) for i in range(NB)]

        class TSet:
            pass

        TS_ = []
        for k in range(2):
            T = TSet()
            T.e1 = ph.sb("e1_%d" % k, [128, 16], F32)
            T.l1 = ph.sb("l1_%d" % k, [128, 8], F32)
            T.c8 = ph.sb("c8_%d" % k, [128, 8], F32)
            T.d1 = ph.sb("d1_%d" % k, [128, 8], F32)
            T.u8 = ph.sb("u8_%d" % k, [128, 8], F32)
            T.wold = ph.sb("wold_%d" % k, [128, 8], F32)
            T.ec8 = ph.sb("ec8_%d" % k, [128, 8], F32)
            T.dg = ph.sb("dg_%d" % k, [128, 8, 128], BF16)
            T.expc = ph.sb("expc_%d" % k, [128, 1024], F32)
            T.EM = ph.sb("EM_%d" % k, [128, 1024], F32)
            T.Ku = ph.sb("Ku_%d" % k, [128, 256], BF16)
            T.PT = [ph.sb("PT%d_%d" % (d, k), [128, 512], BF16) for d in range(2)]
            T.PvT = [ph.sb("PvT%d_%d" % (d, k), [128, 2, 128], BF16) for d in range(2)]
            T.den = [ph.sb("den%d_%d" % (d, k), [128, 512], F32) for d in range(2)]
            T.bcs = [ph.sb("bcs%d_%d" % (d, k), [64, 512], F32) for d in range(2)]
            T.hh = [ph.sb("hh%d_%d" % (d, k), [64, 512], F32) for d in range(2)]
            T.sq = ph.sb("sq_%d" % k, [64, 512], BF16)
            T.rs = ph.sb("rs_%d" % k, [64, 512], F32)
            T.hn = ph.sb("hn_%d" % k, [64, 512], F32)
            T.catm = ph.sb("catm_%d" % k, [64, 4, 128], BF16)
            TS_.append(T)
        Sf = [ph.sb("Sf%d" % d, [128, 2, 65], F32) for d in range(2)]
        Sb = [ph.sb("Sb%d" % d, [128, 4, 65], BF16) for d in range(2)]
        Sst = ph.sb("Sst", [128, TS // 128, 4, 65], BF16)
        for d in range(2):
            ph.memset(Sb[d], 0.0)
        em0 = ph.sb("em0", [128, 8], F32)
        rows = [ph.sb("rows%d" % j, [8, 2], F32) for j in range(2)]
        rt = ph.sb("rt", [8, 8], F32)
        dg8 = ph.sb("dg8", [8, 8], F32)
        emb = ph.sb("emb", [128, 8], F32)
        Sout = ph.sb("Sout", [128, 2, 2, 65], F32)
        one_col = self.ones_f[:, 0:1]
        ld = {"n": 0}

        def gate_prep(gt, bwd_only, T):
            ph.act(T.e1, gt, AF.Exp, scale=-1.0)
            ph.act(T.l1[:, 0:4], T.e1[:, 4:8], AF.Ln, bias=one_col)
            ph.act(T.l1[:, 4:8], T.e1[:, 12:16], AF.Ln, bias=one_col)
            pg = bank()
            if not bwd_only:
                ph.mm(pg[:, 0:4], self.maskf4[:, 0:128], T.l1[:, 0:4])
                ph.mm(pg[:, 8:12], self.ones_f, T.l1[:, 0:4])
            ph.mm(pg[:, 4:8], self.maskb4[:, 0:128], T.l1[:, 4:8])
            ph.mm(pg[:, 12:16], self.ones_f, T.l1[:, 4:8])
            lo = 4 if bwd_only else 0
            ph.ts(T.c8[:, lo:8], pg[:, lo:8], -1.0, ALU.mult)
            if not bwd_only:
                ph.tt(T.d1[:, 0:4], gt[:, 0:4], T.c8[:, 0:4], ALU.subtract)
            ph.tt(T.d1[:, 4:8], gt[:, 8:12], T.c8[:, 4:8], ALU.subtract)
            ph.act(T.u8[:, lo:8], T.d1[:, lo:8], AF.Exp)
            ph.act(T.wold[:, lo:8], pg[:, 8 + lo:16], AF.Exp, scale=-1.0)

        def state_update(d, kt, vt, T):
            for h in range(4):
                ph.ts(T.Ku[:, h * 64:(h + 1) * 64], kt[:, h * 64:(h + 1) * 64], T.u8[:, d * 4 + h:d * 4 + h + 1], ALU.mult)
            pS = bank()
            for h in range(4):
                p = h // 2
                ph.mm(pS[:, h * 65:(h + 1) * 65], T.Ku[:, p * 128:(p + 1) * 128], vt[:, h * 65:(h + 1) * 65])
            for h in range(4):
                p, b = h // 2, (h % 2) * 64
                sv = Sf[d][b:b + 64, p, :]
                ph.tt(sv, pS[b:b + 64, h * 65:(h + 1) * 65], sv, ALU.add)
                ph.ts(sv, sv, T.wold[b:b + 64, d * 4 + h:d * 4 + h + 1], ALU.mult)
                ph.copy(Sb[d][b:b + 64, h, :], sv, eng="pool")

        def lockstep(gens):
            gens = list(gens)
            while gens:
                for g_ in list(gens):
                    try:
                        next(g_)
                    except StopIteration:
                        gens.remove(g_)

        def prep(j, i, T, tok, ctx):
            qT, kT, kt, vt, gt, mo = qTs[i], kTs[i], kts[i], vts[i], gts[i], mos[i]
            ph.dma(qT, self.sMq[:, :, tok:tok + 128].re("c p t -> p c t"))
            ph.dma(kT, self.sMk[:, :, tok:tok + 128].re("c p t -> p c t"))
            ph.dma(kt, self.sMkt[tok:tok + 128, :])
            ph.dma(vt, self.sMv[tok:tok + 128, :])
            ph.dma(gt, self.sGate[tok:tok + 128, :])
            ph.dma(mo, self.sMo[:, :, tok:tok + 128].re("h p t -> p h t"))
            yield
            ph.act(T.e1, gt, AF.Exp, scale=-1.0)
            yield
            ph.act(T.l1[:, 0:4], T.e1[:, 4:8], AF.Ln, bias=one_col)
            ph.act(T.l1[:, 4:8], T.e1[:, 12:16], AF.Ln, bias=one_col)
            yield
            pg = bank()
            ph.mm(pg[:, 0:4], self.maskf4[:, 0:128], T.l1[:, 0:4])
            ph.mm(pg[:, 8:12], self.ones_f, T.l1[:, 0:4])
            ph.mm(pg[:, 4:8], self.maskb4[:, 0:128], T.l1[:, 4:8])
            ph.mm(pg[:, 12:16], self.ones_f, T.l1[:, 4:8])
            yield
            ph.ts(T.c8, pg[:, 0:8], -1.0, ALU.mult)
            yield
            ph.tt(T.d1[:, 0:4], gt[:, 0:4], T.c8[:, 0:4], ALU.subtract)
            ph.tt(T.d1[:, 4:8], gt[:, 8:12], T.c8[:, 4:8], ALU.subtract)
            ph.act(T.ec8, T.c8, AF.Exp)
            yield
            ph.act(T.u8, T.d1, AF.Exp)
            ph.act(T.wold, pg[:, 8:16], AF.Exp, scale=-1.0)
            if ctx:
                pr = bank()
                ph.transpose(pr[0:8, 0:128], T.d1, self.ident_f)
                ph.mm(pr[0:8, 128:129], T.l1, one_col)
                yield
                ph.add("dve", lambda e, o=rows[j][:, 0:1].ap, a=pr[0:8, 0:128].ap: e.tensor_reduce(out=o, in_=a, axis=AX.X, op=ALU.max),
                       reads=(pr,), writes=(rows[j],))
                ph.copy(rows[j][:, 1:2], pr[0:8, 128:129])
            yield
            for hd in range(8):
                ph.ts(T.dg[:, hd, :], self.ident_b, T.ec8[:, hd:hd + 1], ALU.mult)
                if hd % 2:
                    yield
            pRs = [bank(), bank()]
            for d in range(2):
                ph.mm(pRs[d], self.ones_b, T.dg[:, d * 4:(d + 1) * 4, :].re("p a b -> p (a b)"))
            yield
            for d in range(2):
                ph.copy(T.expc[:, d * 512:(d + 1) * 512], pRs[d], eng="act")
            yield
            for d in range(2):
                ph.tt(T.EM[:, d * 512:(d + 1) * 512], pRs[d], (self.maskf4 if d == 0 else self.maskb4), ALU.mult)
                yield
            pAs = [bank(), bank()]
            for h in range(4):
                p, b = h // 2, (h % 2) * 64
                ph.mm(pAs[h % 2][:, h * 128:(h + 1) * 128], kT[b:b + 64, p, :], qT[b:b + 64, p, :])
            yield
            for d in range(2):
                for h in range(4):
                    hs = slice(h * 128, (h + 1) * 128)
                    es = slice(d * 512 + h * 128, d * 512 + (h + 1) * 128)
                    ph.stt(T.PT[d][:, hs], pAs[h % 2][:, hs], T.u8[:, d * 4 + h:d * 4 + h + 1], T.EM[:, es], ALU.mult, ALU.mult)
                    p, b = h // 2, (h % 2) * 64
                    ph.tt(T.PvT[d][b:b + 64, p, :], qT[b:b + 64, p, :], T.expc[b:b + 64, es], ALU.mult, eng="pool")
                    yield

        def finish(j, i, T, tok, ctx, last):
            kt, vt, mo = kts[i], vts[i], mos[i]
            pOs = [bank(), bank()]
            for d in range(2):
                pO = pOs[d]
                for h in range(4):
                    hs = slice(h * 128, (h + 1) * 128)
                    p = h // 2
                    ph.mm(pO[0:65, hs], vt[:, h * 65:(h + 1) * 65], T.PT[d][:, hs], start=True, stop=False)
                    sst = Sb[0][:, h, :] if d == 0 else Sst[:, j, h, :]
                    ph.mm(pO[0:65, hs], sst, T.PvT[d][:, p, :], start=False, stop=True)
                yield
            for d in range(2):
                ph.copy(T.den[d][64:65, :], pOs[d][64:65, :], eng="act")
            yield
            for d in range(2):
                ph.tt(T.den[d][64:65, :], T.den[d][64:65, :], T.den[d][64:65, :], ALU.mult)
            yield
            for d in range(2):
                ph.ts(T.den[d][64:65, :], T.den[d][64:65, :], 1.0, ALU.max)
            yield
            for d in range(2):
                ph.act(T.den[d][64:65, :], T.den[d][64:65, :], AF.Ln)
            yield
            for d in range(2):
                ph.act(T.den[d][64:65, :], T.den[d][64:65, :], AF.Exp, scale=-0.5)
            yield
            pBs = [bank(), bank()]
            for d in range(2):
                ph.mm(pBs[d][0:64, :], self.ones_f[64:65, 0:64], T.den[d][64:65, :])
            yield
            for d in range(2):
                ph.copy(T.bcs[d], pBs[d][0:64, :], eng="act")
            yield
            for d in range(2):
                ph.tt(T.hh[d], pOs[d][0:64, :], T.bcs[d], ALU.mult)
                yield
            ph.tt(T.hh[0], T.hh[0], T.hh[1], ALU.add)
            yield
            ph.tt(T.sq, T.hh[0], T.hh[0], ALU.mult)
            yield
            pM = bank()
            ph.mm(pM[0:64, :], self.ones64[0:64, 0:64], T.sq)
            yield
            ph.act(T.rs, pM[0:64, :], AF.Ln, bias=self.epsc[0:64, 0:1])
            yield
            ph.act(T.rs, T.rs, AF.Exp, scale=-0.5)
            yield
            for h in range(4):
                hs = slice(h * 128, (h + 1) * 128)
                ph.stt(T.hn[:, hs], T.hh[0][:, hs], self.pcolB[0:64, 16 + l * 4 + h:17 + l * 4 + h], T.rs[:, hs], ALU.mult, ALU.mult)
                if h % 2:
                    yield
            ph.tt(T.catm, T.hn.re("p (h t) -> p h t", h=4), mo, ALU.mult)
            ph.dma(self.sCatM[:, :, tok:tok + 128].re("h p t -> p h t"), T.catm)
            yield
            if ctx or not last:
                for h in range(4):
                    ph.ts(T.Ku[:, h * 64:(h + 1) * 64], kt[:, h * 64:(h + 1) * 64], T.u8[:, h:h + 1], ALU.mult)
                yield
                pS = bank()
                for h in range(4):
                    p = h // 2
                    ph.mm(pS[:, h * 65:(h + 1) * 65], T.Ku[:, p * 128:(p + 1) * 128], vt[:, h * 65:(h + 1) * 65])
                yield
                for h in range(4):
                    p, b = h // 2, (h % 2) * 64
                    sv = Sf[0][b:b + 64, p, :]
                    ph.tt(sv, pS[b:b + 64, h * 65:(h + 1) * 65], sv, ALU.add)
                    ph.ts(sv, sv, T.wold[b:b + 64, h:h + 1], ALU.mult)
                    ph.copy(Sb[0][b:b + 64, h, :], sv, eng="pool")
                    yield

        for (tok0, nblk, ctx, sidx) in seqs:
            if ctx:
                for d in range(2):
                    ph.memset(Sf[d], 0.0)
                    for h in range(4):
                        b = (h % 2) * 64
                        ph.memset(Sb[d][b:b + 64, h, :], 0.0)
            else:
                ph.dma(em0, V(self.sm.ap[l:l + 1, :].partition_broadcast(128), self.sm.buf))
                ph.act(em0, em0, AF.Exp)
                for hd in range(8):
                    d, h = hd // 4, hd % 4
                    p, b = h // 2, (h % 2) * 64
                    ph.dma(Sf[d][b:b + 64, p, 0:64], self.sC[l, hd])
                    ph.dma(Sf[d][b:b + 64, p, 64:65], self.sn[l, hd:hd + 1, :].re("o d -> d o"), allow_slow_non_contiguous=True)
                for hd in range(8):
                    d, h = hd // 4, hd % 4
                    p, b = h // 2, (h % 2) * 64
                    sv = Sf[d][b:b + 64, p, :]
                    ph.ts(sv, sv, em0[b:b + 64, hd:hd + 1], ALU.mult)
                    ph.copy(Sb[d][b:b + 64, h, :], sv, eng="pool")
            pending = None
            for j in range(nblk - 1, -1, -1):
                i = ld["n"] % NB
                ld["n"] += 1
                T = TS_[j % 2]
                tok = tok0 + j * 128
                need = ctx or j > 0
                if need:
                    ph.dma(kts[i], self.sMkt[tok:tok + 128, :])
                    ph.dma(vts[i], self.sMv[tok:tok + 128, :])
                    ph.dma(gts[i], self.sGate[tok:tok + 128, :])
                    gate_prep(gts[i], True, T)
                if pending is not None:
                    state_update(1, *pending)
                ph.copy(Sst[:, j], Sb[1], eng="pool")
                pending = (kts[i], vts[i], T) if need else None
            if pending is not None:
                state_update(1, *pending)
            slot = {}
            for j in range(nblk):
                i = ld["n"] % NB
                ld["n"] += 1
                slot[j] = (i, TS_[j % 2], tok0 + j * 128)
                gens = [prep(j, slot[j][0], slot[j][1], slot[j][2], ctx)]
                if j >= 1:
                    gens.append(finish(j - 1, slot[j - 1][0], slot[j - 1][1], slot[j - 1][2], ctx, False))
                lockstep(gens)
            jl = nblk - 1
            lockstep([finish(jl, slot[jl][0], slot[jl][1], slot[jl][2], ctx, True)])
            if ctx:
                r0, r1 = rows[0], rows[1]
                fsel = self.maskf4[0:8, 3:4]
                bsel = self.maskb4[0:8, 4:5]
                ph.ts(rt[:, 0:1], r0[:, 1:2], -1.0, ALU.mult)
                ph.ts(rt[:, 1:2], r1[:, 1:2], -1.0, ALU.mult)
                ph.tt(rt[:, 2:3], rt[:, 0:1], rt[:, 1:2], ALU.add)
                ph.stt(rt[:, 3:4], rt[:, 1:2], fsel, rt[:, 0:1], ALU.mult, ALU.add)
                ph.tt(rt[:, 3:4], rt[:, 3:4], r0[:, 0:1], ALU.add)
                ph.stt(rt[:, 4:5], rt[:, 0:1], bsel, rt[:, 1:2], ALU.mult, ALU.add)
                ph.tt(rt[:, 4:5], rt[:, 4:5], r1[:, 0:1], ALU.add)
                ph.tt(rt[:, 5:6], rt[:, 3:4], rt[:, 4:5], ALU.max)
                ph.tt(rt[:, 5:6], rt[:, 5:6], rt[:, 2:3], ALU.max)
                ph.act(rt[:, 6:7], rt[:, 5:6], AF.Exp, scale=-1.0)
                ph.ts(dg8, self.ident_f[0:8, 0:8], rt[:, 6:7], ALU.mult)
                pE = bank()
                ph.mm(pE[:, 0:8], self.ones_f[0:8, :], dg8)
                ph.copy(emb, pE[:, 0:8])
                for hd in range(8):
                    d, h = hd // 4, hd % 4
                    p, b = h // 2, (h % 2) * 64
                    ph.ts(Sout[b:b + 64, d, p, :], Sf[d][b:b + 64, p, :], emb[b:b + 64, hd:hd + 1], ALU.mult)
                for hd in range(8):
                    d, h = hd // 4, hd % 4
                    p, b = h // 2, (h % 2) * 64
                    ph.dma(self.nC[sidx, l, hd], Sout[b:b + 64, d, p, 0:64])
                    ph.dma(self.nn[sidx, l, hd:hd + 1, :].re("o d -> d o"), Sout[b:b + 64, d, p, 64:65], allow_slow_non_contiguous=True)
                ph.dma(self.nm[sidx, l:l + 1, :].re("o d -> d o"), rt[:, 5:6], allow_slow_non_contiguous=True)
        ph.emit()


Builder.mlstm_phase = _mlstm_phase


def _attn_phase(self, l, sample, qt_limit=None):
    nc = self.nc
    with nc.cleanup_on_exit():
        ph = self.phase("at%d_%d" % (l, int(sample)))
        NKC = (PAST + TS) // 128 if sample else SEQ // 128
        NK = NKC * 128
        QN = 512 if sample else 256
        NX = 3
        xts = [ph.ps("xs%d" % i, (128, 2 * QN)) for i in range(NX)]
        obank = ph.ps("obank")
        nbank = ph.ps("nwbank")
        wbank = nbank
        banks = [obank, nbank, xts[0][:, 0:512], xts[1][:, 0:512]]
        KgT = ph.sb("KgT", [128, NK], BF16)
        Vg = ph.sb("Vg", [128, NKC, 130], BF16)
        KmT = ph.sb("KmT", [128, 6, NK], BF16)
        Vm = ph.sb("Vm", [128, NKC, 390], BF16)
        qgs = [ph.sb("qg%d" % i, [128, 6, QN], BF16) for i in range(2)]
        qms = [ph.sb("qm%d" % i, [128, 6, QN], BF16) for i in range(2)]
        for i in range(2):
            ph.memset(qgs[i], 0.0)
            ph.memset(qms[i], 0.0)
        ph.memset(KmT, 0.0)
        pts = [ph.sb("pt%d" % i, [128, 2 * QN], BF16) for i in range(NX)]
        osbs = [ph.sb("osb%d" % i, [128, 512], F32) for i in range(2)]
        catTs = [ph.sb("catT%d" % i, [128, 16, QN], BF16) for i in range(2)]
        woTs = [ph.sb("woT%d" % i, [128, 16, 128], BF16) for i in range(2)]
        for i in range(2):
            ph.memset(catTs[i][64:128], 0.0, eng="dve")
            ph.memset(woTs[i][64:128], 0.0, eng="dve")
        wo_cnt = {"n": 0}
        xcs = [ph.sb("xc%d" % i, [128, QN], F32) for i in range(2)]
        dens = [ph.sb("den%d" % i, [128, 512], F32) for i in range(2)]
        DEF = 1 if NKC // 2 == 1 else 2
        st = {"b": 0}

        def bank():
            st["b"] += 1
            return banks[st["b"] % 4]

        if sample:
            ckt = ph.sb("ckt", [128, 2, 128], F32)
            ckvt = ph.sb("ckvt", [128, 2, 256], F32)
            ckvT = ph.sb("ckvT", [128, 2, 256], BF16)
            krs = ph.sb("krs", [128, 2, 96], F32)
            krT = ph.sb("krT", [128, 256], BF16)
            wukv = ph.sb("wukv", [128, 2, 768], BF16)
            wv = ph.sb("wv", [128, 2, 384], BF16)
            ph.memset(Vg, 1.0)
            ph.memset(Vm, 1.0)
            ph.dma(wukv, self.w_ukv[l].re("(c p) n -> p c n", p=128), q="pool")
            for c in range(2):
                ph.copy(wv[:, c, :].re("p (h d) -> p h d", d=64), wukv[:, c, :].re("p (h x) -> p h x", x=128)[:, :, 64:128], eng="pool")
            ph.dma(ckt, self.ck[l].re("(c p) d -> p c d", p=128))
            ph.dma(ckvt, self.cckv[l].re("(c p) d -> p c d", p=128))
            ph.memset(krs, 0.0)
            ph.dma(krs[:, :, 64:96], self.ckr[l].re("(c p) d -> p c d", p=128))
            for kc in range(2):
                ph.dma(Vg[:, kc, :].re("p (h e) -> p h e", e=65)[:, :, 0:64],
                       self.cv[l, kc * 128:(kc + 1) * 128, :].re("p (h d) -> p h d", d=64), q="pool")
            for kc in range(2):
                pb = bank()
                ph.transpose(pb[:, 0:128], ckt[:, kc, :], self.ident_f)
                ph.copy(KgT[:, kc * 128:(kc + 1) * 128], pb[:, 0:128])
                for c in range(2):
                    pb = bank()
                    ph.transpose(pb[:, 0:128], ckvt[:, kc, c * 128:(c + 1) * 128], self.ident_f)
                    ph.copy(ckvT[:, c, kc * 128:(kc + 1) * 128], pb[:, 0:128])
                pb = bank()
                ph.transpose(pb[0:96, 0:128], krs[:, kc, :], self.ident_f)
                ph.copy(krT[64:96, kc * 128:(kc + 1) * 128], pb[64:96, 0:128])
            for h in range(6):
                pb = bank()
                for c in range(2):
                    ph.mm(pb[0:64, 0:256], wukv[:, c, h * 128:h * 128 + 64], ckvT[:, c, :], start=(c == 0), stop=(c == 1))
                ph.copy(KmT[0:64, h, 0:256], pb[0:64, 0:256], eng="act")
                ph.copy(KmT[64:96, h, 0:256], krT[64:96, :], eng="pool")
            for kc in range(2):
                pb = bank()
                for c in range(2):
                    ph.mm(pb[:, 0:384], ckvT[:, c, kc * 128:(kc + 1) * 128], wv[:, c, :], start=(c == 0), stop=(c == 1))
                ph.copy(Vm[:, kc, :].re("p (h e) -> p h e", e=65)[:, :, 0:64], pb[:, 0:384].re("p (h d) -> p h d", d=64))
            ph.dma(KgT[:, PAST:], self.sKg[:, GT:TT])
            for q4 in range(4):
                r0 = GT + q4 * GT
                ph.dma(Vg[:, 2 + q4 * 8:2 + (q4 + 1) * 8, :], self.sVg[r0:r0 + GT, :].re("(c p) e -> p c e", p=128))
                ph.dma(Vm[:, 2 + q4 * 8:2 + (q4 + 1) * 8, :], self.sVm[r0:r0 + GT, :].re("(c p) e -> p c e", p=128))
            for h in range(6):
                ph.dma(KmT[0:96, h, PAST:], self.sKm[h][:, GT:TT])
            tiles = [(GT + i * 512, 1 + i // 2, (i % 2) * 512) for i in range(TS // 512)]
        else:
            tiles = [(s * SEQ, 0, s * SEQ) for s in range(NSEQ_P)]
        if qt_limit is not None:
            tiles = tiles[:qt_limit]

        heads = [("g", h) for h in range(6)] + [("m", h) for h in range(6)]
        carry = []
        NP = NKC // 2
        for ti, (tok, g, xoff) in enumerate(tiles):
            ci = 0 if g == 0 else 1
            if not sample:
                ph.dma(KgT, self.sKg[:, tok:tok + SEQ])
                ph.dma(Vg, self.sVg[tok:tok + SEQ, :].re("(c p) e -> p c e", p=128))
                ph.dma(Vm, self.sVm[tok:tok + SEQ, :].re("(c p) e -> p c e", p=128))
                for h in range(6):
                    ph.dma(KmT[0:96, h, :], self.sKm[h][:, tok:tok + SEQ])
            qg, qm = qgs[ti % 2], qms[ti % 2]
            for j in range(2):
                ph.dma(qg[j * 64:(j + 1) * 64, 3 * j:3 * j + 3, :], self.sQg[:, j * 64:(j + 1) * 64, tok:tok + QN].re("i p t -> p i t"))
            ph.dma(qm[0:96], self.sQm[:, :, tok:tok + QN].re("h p t -> p h t"))
            catT = catTs[ti % 2]
            ph.dma(catT[0:64, 6:10, :], self.sCatM[:, :, tok:tok + QN].re("h p t -> p h t"))
            pairs = [(hi, kp) for hi in range(12) for kp in range(NP)]
            n = len(pairs)
            pend = []

            def push(pos, fn):
                k_ = len(pend)
                while k_ > 0 and pend[k_ - 1][0] > pos:
                    k_ -= 1
                pend.insert(k_, (pos, fn))

            for k_, fn_ in enumerate(carry):
                push(int((k_ + 1) * n / 9), fn_)
            carry = []

            def s_pair(i):
                hi, kp = pairs[i]
                kind, h = heads[hi]
                X = xts[i % NX]
                for u in range(2):
                    kc = 2 * kp + u
                    if kind == "g":
                        ph.mm(X[:, u * QN:(u + 1) * QN], KgT[:, kc * 128:(kc + 1) * 128], qg[:, h, :])
                    else:
                        ph.mm(X[:, u * QN:(u + 1) * QN], KmT[:, h, kc * 128:(kc + 1) * 128], qm[:, h, :])

            def pv_pair(i):
                hi, kp = pairs[i]
                kind, h = heads[hi]
                X = xts[i % NX]
                pt = pts[i % NX]
                ob = obank
                ph.act(pt[:, 0:2 * QN], X[:, 0:2 * QN], AF.Exp, scale=(0.125 if kind == "g" else 96.0 ** -0.5))
                for u in range(2):
                    kc = 2 * kp + u
                    vt = Vg[:, kc, (h // 3) * 65:(h // 3 + 1) * 65] if kind == "g" else Vm[:, kc, h * 65:(h + 1) * 65]
                    ph.mm(ob[0:65, 0:QN], vt, pt[:, u * QN:(u + 1) * QN], start=(kc == 0), stop=(kc == NKC - 1))
                if kp == NP - 1:
                    slot = h if kind == "g" else 10 + h
                    den = dens[hi % 2]
                    osb = osbs[hi % 2]
                    ph.copy(osb[0:65, 0:QN], ob[0:65, 0:QN])
                    ph.recip(den[64:65, 0:QN], osb[64:65, 0:QN])

                    def fin(osb=osb, slot=slot, den=den, catT=catT):
                        ph.mm(nbank[0:64, 0:QN], self.ones_f[64:65, 0:64], den[64:65, 0:QN])
                        ph.tt(catT[0:64, slot, :], osb[0:64, 0:QN], nbank[0:64, 0:QN], ALU.mult)
                    push(i + DEF, fin)

            LK = NX - 1
            for i in range(n + LK):
                if i < n:
                    s_pair(i)
                if i >= LK:
                    pv_pair(i - LK)
                while pend and pend[0][0] <= i - LK:
                    pend.pop(0)[1]()
            while pend:
                pend.pop(0)[1]()
            def wo_item(c, catT=catT, g=g, xoff=xoff, ci=ci):
                def run():
                    k_ = wo_cnt["n"]
                    wo_cnt["n"] += 1
                    xc = xcs[k_ % 2]
                    wt = woTs[k_ % 2]
                    ph.dma(xc, self.xT[g, :, c, xoff:xoff + QN])
                    ph.dma(wt[0:64], self.w_out[l][:, c * 128:(c + 1) * 128].re("(s p) d -> p s d", p=64), q="pool")
                    for s_ in range(16):
                        ph.mm(wbank[:, 0:QN], wt[:, s_, :], catT[:, s_, :], start=(s_ == 0), stop=(s_ == 15))
                    ph.stt(xc, wbank[:, 0:QN], self.mod(l, ci, 5, c), xc, ALU.mult, ALU.add)
                    ph.dma(self.xT[g, :, c, xoff:xoff + QN], xc)
                return run

            carry = [wo_item(c) for c in range(8)]
        for fn_ in carry:
            fn_()
        ph.emit()


Builder.attn_phase = _attn_phase


def build_program(depth=DEPTH, debug=False):
    B = Builder(depth=depth, debug=debug)
    B.prologue()
    for g in range(NG):
        B.ffn_phase(0, 0, g)
    for l in range(depth):
        B.mlstm_phase(l)
        B.attn_phase(l, False)
        B.attn_phase(l, True)
        for g in range(NG):
            B.ffn_phase(l, 1, g, chain=True)
    return B


def kernel(**inputs):
    inp = {k: np.asarray(v) for k, v in inputs.items()}
    B = build_program(DEPTH)
    consts = host_consts()
    in_maps = [core_inputs(inp, k, DEPTH, consts) for k in range(8)]
    res = run_bass_kernel_spmd(B.nc, in_maps, core_ids=list(range(8)))
    r = res.results
    L = DEPTH
    y_prompt = np.concatenate([r[k]["yp"].reshape(NSEQ_P, SEQ, D) for k in range(8)], 0)
    y_sample = np.stack([r[0]["ys"], r[4]["ys"]], 0)
    cat = lambda name, shp: np.concatenate([r[k][name].reshape((NSEQ_P,) + shp) for k in range(8)], 0)
    new_k = cat("ngk", (L, SEQ, 2, 64))
    new_v = cat("ngv", (L, SEQ, 2, 64))
    new_ckv = cat("nckv", (L, SEQ, 256))
    new_kr = cat("nkr", (L, SEQ, 32))
    new_C = cat("nC", (L, 2, 4, 64, 64))
    new_n = cat("nn", (L, 2, 4, 64))
    new_m = cat("nm", (L, 2, 4))
    outs = (y_prompt, y_sample, new_k, new_v, new_ckv, new_kr, new_C, new_n, new_m)
    return tuple(np.ascontiguousarray(o, dtype=np.float32) for o in outs)
```

```python
import os
import numpy as np
import concourse.bass as bass
import concourse.mybir as mybir
from concourse.bass_utils import run_bass_kernel_spmd

F32 = mybir.dt.float32
BF16 = mybir.dt.bfloat16
AF = mybir.ActivationFunctionType
ALU = mybir.AluOpType
AX = mybir.AxisListType

D = 1024
DFF = 2816
NFF = DFF // 128
DEPTH = 4
NSEQ_P = 4
SEQ = 256
TS = 4096
PAST = 256
GT = 1024
NG = 5
TT = NG * GT
EPS = 1e-6
DIN = 2224
SAME_ENGINE_SYNC = True


class Buf:
    __slots__ = ("name", "writers", "readers", "sem", "ndma", "last_dma", "excl")

    def __init__(self, name, excl=False):
        self.name = name
        self.excl = excl
        self.writers = []
        self.readers = []
        self.sem = None
        self.ndma = 0
        self.last_dma = None

    def reset(self):
        self.writers = []
        self.readers = []
        self.sem = None
        self.ndma = 0
        self.last_dma = None


class V:
    __slots__ = ("ap", "buf")

    def __init__(self, ap, buf):
        self.ap = ap
        self.buf = buf

    def __getitem__(self, idx):
        return V(self.ap[idx], self.buf)

    def part(self, idx, name):
        return V(self.ap[idx], Buf(name))

    def re(self, pat, **kw):
        return V(self.ap.rearrange(pat, **kw), self.buf)

    def bc(self, dt):
        return V(self.ap.bitcast(dt), self.buf)


class Op:
    __slots__ = ("eng", "fn", "deps", "tick", "needs_inc", "is_dma", "key", "dma_idx", "pos")

    def __init__(self, eng, fn, is_dma=False, key=None):
        self.eng = eng
        self.fn = fn
        self.deps = []
        self.tick = 0
        self.needs_inc = is_dma
        self.is_dma = is_dma
        self.key = key
        self.dma_idx = 0
        self.pos = 0


class Phase:
    ENGS = ("pe", "act", "dve", "pool", "sp")

    def __init__(self, nc, name):
        self.nc = nc
        self.name = name
        self.ops = []
        self.touched = {}

    def sb(self, name, shape, dt):
        t = self.nc.alloc_sbuf_tensor(self.name + "_" + name, list(shape), dt)
        return V(t.ap(), Buf(name))

    def ps(self, name, shape=(128, 512), dt=F32):
        t = self.nc.alloc_psum_tensor(self.name + "_" + name, list(shape), dt)
        return V(t.ap(), Buf(name, excl=True))

    def _touch(self, b):
        self.touched[id(b)] = b

    def add(self, eng, fn, reads=(), writes=(), is_dma=False, key=None):
        op = Op(eng, fn, is_dma, key)
        op.pos = len(self.ops)
        deps = []
        rb = [v.buf for v in reads]
        wb = [v.buf for v in writes]
        for b in rb + wb:
            self._touch(b)
        raw = set()
        for b in rb:
            deps.extend(b.writers)
            raw.update(id(w) for w in b.writers)
            if b.excl:
                deps.extend(r for r in b.readers if r.eng != eng)
        for b in wb:
            deps.extend(b.writers)
            deps.extend(b.readers)
        if is_dma:
            self._touch(key)
            if key.last_dma is not None:
                deps.append(key.last_dma)
            key.ndma += 1
            op.dma_idx = key.ndma
            key.last_dma = op
        for b in wb:
            if b.readers:
                b.writers = [op]
                b.readers = []
            else:
                b.writers = [w for w in b.writers if (w.is_dma or w.eng != eng or is_dma)] + [op]
        for b in rb:
            if b not in wb:
                b.readers = [r for r in b.readers if (r.is_dma or r.eng != eng or is_dma)] + [op]
        seen = set()
        for d in deps:
            if d is op or id(d) in seen:
                continue
            seen.add(id(d))
            if (not d.is_dma) and (not is_dma) and d.eng == eng:
                if eng == "pe" or not SAME_ENGINE_SYNC or id(d) not in raw:
                    continue
            op.deps.append(d)
            d.needs_inc = True
        self.ops.append(op)
        return op

    def mm(self, out, lhsT, rhs, start=True, stop=True, **kw):
        return self.add("pe", lambda e: e.matmul(out.ap, lhsT.ap, rhs.ap, start=start, stop=stop, **kw),
                        reads=(lhsT, rhs), writes=(out,))

    def transpose(self, out, in_, ident):
        return self.add("pe", lambda e: e.transpose(out.ap, in_.ap, ident.ap), reads=(in_, ident), writes=(out,))

    def act(self, out, in_, func, bias=None, scale=None, extra_reads=(), accum_out=None):
        kw = {}
        rd = [in_] + list(extra_reads)
        wr = [out]
        if bias is not None:
            if isinstance(bias, V):
                kw["bias"] = bias.ap
                rd.append(bias)
            else:
                kw["bias"] = bias
        if scale is not None:
            if isinstance(scale, V):
                kw["scale"] = scale.ap
                rd.append(scale)
            else:
                kw["scale"] = scale
        if accum_out is not None:
            kw["accum_out"] = accum_out.ap
            wr.append(accum_out)
        return self.add("act", lambda e: e.activation(out=out.ap, in_=in_.ap, func=func, **kw), reads=rd, writes=wr)

    def tt(self, out, in0, in1, op, eng="dve"):
        return self.add(eng, lambda e: e.tensor_tensor(out=out.ap, in0=in0.ap, in1=in1.ap, op=op),
                        reads=(in0, in1), writes=(out,))

    def ts(self, out, in0, s1, op0, s2=None, op1=None, eng="dve"):
        rd = [in0]
        a1 = s1
        a2 = s2
        if isinstance(s1, V):
            rd.append(s1)
            a1 = s1.ap
        if isinstance(s2, V):
            rd.append(s2)
            a2 = s2.ap
        if op1 is None:
            return self.add(eng, lambda e: e.tensor_scalar(out=out.ap, in0=in0.ap, scalar1=a1, scalar2=None, op0=op0),
                            reads=rd, writes=(out,))
        return self.add(eng, lambda e: e.tensor_scalar(out=out.ap, in0=in0.ap, scalar1=a1, scalar2=a2, op0=op0, op1=op1),
                        reads=rd, writes=(out,))

    def stt(self, out, in0, scalar, in1, op0, op1):
        rd = [in0, in1]
        a = scalar
        if isinstance(scalar, V):
            rd.append(scalar)
            a = scalar.ap
        return self.add("dve", lambda e: e.scalar_tensor_tensor(out=out.ap, in0=in0.ap, scalar=a, in1=in1.ap, op0=op0, op1=op1),
                        reads=rd, writes=(out,))

    def copy(self, out, in_, eng="dve"):
        if eng == "act":
            return self.add("act", lambda e: e.copy(out=out.ap, in_=in_.ap), reads=(in_,), writes=(out,))
        return self.add(eng, lambda e: e.tensor_copy(out=out.ap, in_=in_.ap), reads=(in_,), writes=(out,))

    def recip(self, out, in_):
        return self.add("dve", lambda e: e.reciprocal(out=out.ap, in_=in_.ap), reads=(in_,), writes=(out,))

    def rsqrt_ln(self, out, in_, eps_ap, tmp=None):
        t = out if tmp is None else tmp
        self.act(t, in_, AF.Ln, bias=eps_ap)
        self.act(out, t, AF.Exp, scale=-0.5)

    def memset(self, out, val, eng="pool"):
        return self.add(eng, lambda e: e.memset(out.ap, val), reads=(), writes=(out,))

    def dma(self, out, in_, q="sp", key=None, **kw):
        if key is None:
            key = in_.buf if str(out.ap.space) == "DRAM" else out.buf
        return self.add(q, lambda e: e.dma_start(out=out.ap, in_=in_.ap, **kw), reads=(in_,), writes=(out,),
                        is_dma=True, key=key)

    def emit(self):
        nc = self.nc
        sems = {}
        for en in ("pe", "act", "dve", "pool"):
            sems[en] = nc.alloc_semaphore(self.name + "_s_" + en)
        keys = []
        cnt = {en: 0 for en in ("pe", "act", "dve", "pool")}
        for op in self.ops:
            if op.is_dma:
                if op.key.sem is None:
                    op.key.sem = nc.alloc_semaphore(self.name + "_d_" + op.key.name + str(len(keys)))
                    keys.append(op.key)
                op.tick = 16 * op.dma_idx
            elif op.needs_inc:
                cnt[op.eng] += 1
                op.tick = cnt[op.eng]
        streams = {en: [] for en in self.ENGS}
        for op in self.ops:
            streams[op.eng].append(op)

        def run_stream(en, e):
            waited = {}
            for op in streams[en]:
                need = {}
                for d in op.deps:
                    s = d.key.sem if d.is_dma else sems[d.eng]
                    k = id(s)
                    if k not in need or need[k][1] < d.tick:
                        need[k] = (s, d.tick)
                for k, (s, val) in need.items():
                    if waited.get(k, 0) >= val:
                        continue
                    waited[k] = val
                    e.wait_ge(s, val)
                ins = op.fn(e)
                if op.is_dma:
                    ins.then_inc(op.key.sem, 16)
                elif op.needs_inc:
                    ins.then_inc(sems[op.eng], 1)
            if en == "sp":
                for kb in keys:
                    e.wait_ge(kb.sem, 16 * kb.ndma)

        with nc.Block(self.name) as block:
            @block.tensor
            def _(e):
                run_stream("pe", e)

            @block.scalar
            def _(e):
                run_stream("act", e)

            @block.vector
            def _(e):
                run_stream("dve", e)

            @block.gpsimd
            def _(e):
                run_stream("pool", e)

            @block.sync
            def _(e):
                run_stream("sp", e)
        for b in self.touched.values():
            b.reset()
        for op in self.ops:
            op.fn = None
        self.ops = []


C_GQ, C_GK, C_GV, C_MQ, C_MK, C_MV, C_MO, C_MG, C_QL, C_KVL, C_KR = 0, 384, 512, 640, 896, 1152, 1408, 1664, 1680, 1936, 2192


class Builder:
    def __init__(self, depth=DEPTH, debug=False, stop_after=None):
        self.depth = depth
        self.debug = debug
        self.stop_after = stop_after
        nc = bass.Bass("TRN2", target_bir_lowering=False)
        self.nc = nc
        self.pn = 0
        L = depth

        def din(name, shape, dt=F32):
            return V(nc.dram_tensor(name, list(shape), dt, kind="ExternalInput").ap(), Buf(name))

        def dout(name, shape, dt=F32):
            return V(nc.dram_tensor(name, list(shape), dt, kind="ExternalOutput").ap(), Buf(name))

        def dscr(name, shape, dt=BF16):
            kind = "ExternalOutput" if debug else "Internal"
            return V(nc.dram_tensor(name, list(shape), dt, kind=kind).ap(), Buf(name))

        self.xp = din("xp", [GT, D])
        self.xs = din("xs", [TS, D])
        self.cond = din("cond", [16, 128])
        self.ck = din("ck", [L, PAST, 128])
        self.cv = din("cv", [L, PAST, 128])
        self.cckv = din("cckv", [L, PAST, 256])
        self.ckr = din("ckr", [L, PAST, 32])
        self.sC = din("sC", [L, 8, 64, 64])
        self.sn = din("sn", [L, 8, 64])
        self.sm = din("sm", [L, 8])
        self.w_ada = din("w_ada", [L, D, 9 * D])
        self.b_ada = din("b_ada", [L, 9 * D])
        self.norm_w = din("norm_w", [L * 3 * 8, 128])
        self.wg = din("ffn_w_gate", [L, 2, D, DFF])
        self.wu = din("ffn_w_up", [L, 2, D, DFF])
        self.wd = din("ffn_w_down", [L, 2, DFF, D])
        self.w_in = din("w_in", [L, D, DIN])
        self.gqn = din("gqa_q_norm", [L, 64])
        self.gkn = din("gqa_k_norm", [L, 64])
        self.mgb = din("mlstm_gate_b", [1, L * 16])
        self.mon = din("mlstm_out_norm", [L * 4, 64])
        self.mqn = din("mla_q_norm", [L * 2, 128])
        self.w_uq = din("mla_w_uq", [L, 256, 576])
        self.mkvn = din("mla_kv_norm", [L * 2, 128])
        self.w_ukv = din("mla_w_ukv", [L, 256, 768])
        self.w_out = din("w_out", [L, D, D])
        self.fnorm = din("final_norm", [8, 128])
        self.c_ident = din("c_ident", [128, 128])
        self.c_cos64 = din("c_cos64", [128, TS])
        self.c_sin64 = din("c_sin64", [128, TS])
        self.c_cos32 = din("c_cos32", [128, TS])
        self.c_sin32 = din("c_sin32", [128, TS])
        self.c_perm64 = din("c_perm64", [128, 128])
        self.c_perm32 = din("c_perm32", [128, 96])
        self.c_maskf = din("c_maskf", [128, 128])
        self.c_maskb = din("c_maskb", [128, 128])
        self.yp = dout("yp", [GT, D])
        self.ys = dout("ys", [TS, D])
        self.ngk = dout("ngk", [NSEQ_P, L, SEQ, 128])
        self.ngv = dout("ngv", [NSEQ_P, L, SEQ, 128])
        self.nckv = dout("nckv", [NSEQ_P, L, SEQ, 256])
        self.nkr = dout("nkr", [NSEQ_P, L, SEQ, 32])
        self.nC = dout("nC", [NSEQ_P, L, 8, 64, 64])
        self.nn = dout("nn", [NSEQ_P, L, 8, 64])
        self.nm = dout("nm", [NSEQ_P, L, 8])
        self.xT = dscr("xT", [NG, 128, 8, GT], F32)
        self.sQg = dscr("sQg", [3, 128, TT])
        self.sKg = dscr("sKg", [128, TT])
        self.sVg = dscr("sVg", [TT, 130])
        self.sQm = dscr("sQm", [6, 96, TT])
        self.sKm = dscr("sKm", [6, 96, TT])
        self.sVm = dscr("sVm", [TT, 390])
        self.sMq = dscr("sMq", [2, 128, TT])
        self.sMk = dscr("sMk", [2, 128, TT])
        self.sMkt = dscr("sMkt", [TT, 256])
        self.sMv = dscr("sMv", [TT, 260])
        self.sGate = dscr("sGate", [TT, 16], F32)
        self.sMo = dscr("sMo", [4, 64, TT])
        self.sCatM = dscr("sCatM", [4, 64, TT])

        def gsb(name, shape, dt):
            return V(nc.alloc_sbuf_tensor("g_" + name, list(shape), dt).ap(), Buf("g_" + name))

        self.ident_f = gsb("identf", [128, 128], F32)
        self.ident_b = gsb("identb", [128, 128], BF16)
        self.ones_dm = gsb("onesdm", [128, 128], BF16)
        self.ones64 = gsb("ones64", [128, 128], BF16)
        self.ones256 = gsb("ones256", [128, 128], BF16)
        self.ones_f = gsb("onesf", [128, 128], F32)
        self.ones_b = gsb("onesb", [128, 128], BF16)
        self.perm64 = gsb("perm64", [128, 128], BF16)
        self.perm32 = gsb("perm32", [128, 96], BF16)
        self.maskf4 = gsb("maskf4", [128, 512], F32)
        self.maskb4 = gsb("maskb4", [128, 512], F32)
        self.pcolA = gsb("pcolA", [128, 128], F32)
        self.pcolB = gsb("pcolB", [128, 32], F32)
        self.modc = gsb("modc", [128, L * 2 * 72], F32)
        self.gbias = gsb("gbias", [128, L * 16], F32)
        self.epsc = gsb("epsc", [128, 1], F32)

    def phase(self, name):
        self.pn += 1
        return Phase(self.nc, "p%d%s" % (self.pn, name))

    def mod(self, l, ci, slot, c=None):
        base = ((l * 2 + ci) * 9 + slot) * 8
        if c is None:
            return self.modc[:, base:base + 8]
        return self.modc[:, base + c:base + c + 1]

    def prologue(self):
        nc = self.nc
        L = self.depth
        with nc.cleanup_on_exit():
            ph = self.phase("pro")
            stA = ph.sb("stA", [128, 128], F32)
            stB = ph.sb("stB", [128, 128], F32)
            tmpf = ph.sb("tmpf", [128, 128], F32)
            tmp96 = ph.sb("tmp96", [128, 96], F32)
            ph.dma(self.ident_f, self.c_ident)
            ph.copy(self.ident_b, self.ident_f)
            ph.dma(tmpf, self.c_perm64)
            ph.copy(self.perm64, tmpf)
            ph.dma(tmp96, self.c_perm32)
            ph.copy(self.perm32, tmp96)
            for j in range(4):
                ph.dma(self.maskf4[:, j * 128:(j + 1) * 128], self.c_maskf)
                ph.dma(self.maskb4[:, j * 128:(j + 1) * 128], self.c_maskb)
            ph.memset(self.ones_dm, 1.0 / 1024.0)
            ph.memset(self.ones256, 1.0 / 256.0)
            ph.memset(self.ones_f, 1.0)
            ph.memset(self.ones_b, 1.0)
            ph.memset(self.epsc, EPS)
            ph.memset(self.ones64, 0.0)
            ph.memset(self.ones64[0:64, 0:64], 1.0 / 64.0)
            ph.memset(self.ones64[64:128, 64:128], 1.0 / 64.0)
            ph.dma(self.gbias, V(self.mgb.ap.partition_broadcast(128), self.mgb.buf))
            ph.memset(stA, 0.0)
            ph.memset(stB, 0.0)
            ph.dma(stA[0:L * 24, :], self.norm_w)
            ph.dma(stA[96:104, :], self.fnorm)
            ph.dma(stA[104:120, :], self.cond)
            ph.dma(stA[120:120 + L, 0:64], self.gqn)
            ph.dma(stA[120:120 + L, 64:128], self.gqn)
            ph.dma(stA[124:124 + L, 0:64], self.gkn)
            ph.dma(stA[124:124 + L, 64:128], self.gkn)
            ph.dma(stB[0:2 * L, :], self.mqn)
            ph.dma(stB[8:8 + 2 * L, :], self.mkvn)
            ph.dma(stB[16:16 + 4 * L, 0:64], self.mon)
            psT = ph.ps("psT")
            ph.transpose(psT[:, 0:128], stA, self.ident_f)
            ph.copy(self.pcolA, psT[:, 0:128])
            ph.transpose(psT[:, 128:256], stB, self.ident_f)
            ph.copy(self.pcolB, psT[:, 128:160])
            scT = ph.sb("scT", [128, 16], BF16)
            ph.act(scT, self.pcolA[:, 104:120], AF.Silu)
            onesr = ph.sb("onesr", [1, 2], BF16)
            ph.memset(onesr, 1.0)
            NB = 512
            wts = [ph.sb("wada%d" % i, [128, 8, NB], BF16) for i in range(3)]
            brow = ph.sb("brow", [1, 9 * D], BF16)
            psm = [ph.ps("psm%d" % i) for i in range(2)]
            mods = ph.sb("mods", [128, 72, 2], F32)
            wi = 0
            for l in range(L):
                ph.dma(brow, self.b_ada[l:l + 1, :], q="pool")
                pm = psm[l % 2]
                for nb in range(9 * D // NB):
                    wt = wts[wi % 3]
                    wi += 1
                    ph.dma(wt, self.w_ada[l, :, nb * NB:(nb + 1) * NB].re("(k p) f -> p k f", p=128), q="pool")
                    for jj in range(NB // 128):
                        j = nb * (NB // 128) + jj
                        for k in range(8):
                            ph.mm(pm[:, 2 * j:2 * j + 2], wt[:, k, jj * 128:(jj + 1) * 128],
                                  scT[:, :].re("p (c k) -> p c k", k=8)[:, :, k], start=(k == 0), stop=False)
                        ph.mm(pm[:, 2 * j:2 * j + 2], brow[0:1, j * 128:(j + 1) * 128], onesr[0:1, :], start=False, stop=True)
                ph.copy(mods, pm[:, 0:144].re("p (j c) -> p j c", c=2))
                for ci in range(2):
                    for i in range(3):
                        nw = self.pcolA[:, (l * 3 + i) * 8:(l * 3 + i) * 8 + 8]
                        sh = mods[:, (3 * i) * 8:(3 * i) * 8 + 8, ci]
                        sc = mods[:, (3 * i + 1) * 8:(3 * i + 1) * 8 + 8, ci]
                        gg = mods[:, (3 * i + 2) * 8:(3 * i + 2) * 8 + 8, ci]
                        ph.stt(self.mod(l, ci, 3 * i), sc, 1.0, nw, ALU.add, ALU.mult)
                        ph.copy(self.mod(l, ci, 3 * i + 1), sh)
                        ph.ts(self.mod(l, ci, 3 * i + 2), gg, 1.0 if i == 1 else 0.5, ALU.mult)
            ph.emit()

    def rms_stats(self, ph, xt, rstd, banks, sqb):
        for hf in range(2):
            pb = banks[hf % len(banks)]
            for c in range(8):
                sq = sqb[c % len(sqb)]
                ph.act(sq, xt[c][:, hf * 512:(hf + 1) * 512], AF.Square)
                ph.mm(pb, self.ones_dm, sq, start=(c == 0), stop=(c == 7))
            sd = sqb[0].buf
            ph.rsqrt_ln(rstd[:, hf * 512:(hf + 1) * 512], pb, self.epsc[:, 0:1])

    def ffn_phase(self, l, which, g, chain=False):
        nc = self.nc
        L = self.depth
        ci = 0 if g == 0 else 1
        first = (l == 0 and which == 0)
        last = (l == L - 1 and which == 1)
        with nc.cleanup_on_exit():
            ph = self.phase("ffn%d_%d_%d" % (l, which, g))
            xt_all = ph.sb("xt", [128, 8, GT], F32)
            xt = [xt_all.part((slice(None), c, slice(None)), "xt%d" % c) for c in range(8)]
            hT = ph.sb("hT", [128, 8, GT], BF16)
            actT = ph.sb("actT", [128, NFF, GT], BF16)
            rstd = ph.sb("rstd", [128, GT], F32)
            sqb = [ph.sb("sq%d" % i, [128, 512], BF16) for i in range(2)]
            tmpx = [ph.sb("tmpx%d" % i, [128, GT], F32) for i in range(2)]
            sgb = [ph.sb("sg%d" % i, [128, 512], F32) for i in range(2)]
            wgt = [ph.sb("wg%d" % i, [128, 8, 128], BF16) for i in range(2)]
            wut = [ph.sb("wu%d" % i, [128, 8, 128], BF16) for i in range(2)]
            wdt = [ph.sb("wd%d" % i, [128, NFF, 128], BF16) for i in range(2)]
            banks = [ph.ps("bk%d" % i) for i in range(8)]
            if first:
                src = self.xp if g == 0 else self.xs[(g - 1) * GT:g * GT, :]
                xin = [ph.sb("xin%d" % i, [128, D], F32) for i in range(2)]
                for tt in range(8):
                    xi = xin[tt % 2]
                    ph.dma(xi, src[tt * 128:(tt + 1) * 128, :])
                    for c2 in range(2):
                        pb = banks[(tt * 2 + c2) % 4]
                        for cc in range(4):
                            c = c2 * 4 + cc
                            ph.transpose(pb[:, cc * 128:(cc + 1) * 128], xi[:, c * 128:(c + 1) * 128], self.ident_f)
                        for cc in range(4):
                            c = c2 * 4 + cc
                            ph.copy(xt[c][:, tt * 128:(tt + 1) * 128], pb[:, cc * 128:(cc + 1) * 128],
                                    eng=("act" if c2 % 2 else "dve"))
            else:
                for c in range(8):
                    ph.dma(xt[c], self.xT[g, :, c, :])

            def core(ll, wh):
                sA, sB, sG = (0, 1, 2) if wh == 0 else (6, 7, 8)
                self.rms_stats(ph, xt, rstd, banks[4:6], sqb)
                for c in range(8):
                    tx = tmpx[c % 2]
                    ph.stt(tx, xt[c], self.mod(ll, ci, sA, c), rstd, ALU.mult, ALU.mult)
                    ph.act(hT[:, c, :], tx, AF.Identity, bias=self.mod(ll, ci, sB, c))
                for f in range(NFF):
                    wgf = wgt[f % 2]
                    wuf = wut[f % 2]
                    ph.dma(wgf, self.wg[ll, wh, :, f * 128:(f + 1) * 128].re("(k p) f -> p k f", p=128), q="pool")
                    ph.dma(wuf, self.wu[ll, wh, :, f * 128:(f + 1) * 128].re("(k p) f -> p k f", p=128), q="pool")
                    for hf in range(2):
                        j = (f * 2 + hf) % 2
                        pg, pu = banks[2 * j], banks[2 * j + 1]
                        for k in range(8):
                            ph.mm(pg, wgf[:, k, :], hT[:, k, hf * 512:(hf + 1) * 512], start=(k == 0), stop=(k == 7))
                        for k in range(8):
                            ph.mm(pu, wuf[:, k, :], hT[:, k, hf * 512:(hf + 1) * 512], start=(k == 0), stop=(k == 7))
                        sg = sgb[j]
                        ph.act(sg, pg, AF.Silu)
                        ph.tt(actT[:, f, hf * 512:(hf + 1) * 512], sg, pu, ALU.mult)
                for c in range(8):
                    wdc = wdt[c % 2]
                    ph.dma(wdc, self.wd[ll, wh, :, c * 128:(c + 1) * 128].re("(f p) d -> p f d", p=128), q="pool")
                    for hf in range(2):
                        pd = banks[6 + hf]
                        for f in range(NFF):
                            ph.mm(pd, wdc[:, f, :], actT[:, f, hf * 512:(hf + 1) * 512], start=(f == 0), stop=(f == NFF - 1))
                        xs_ = xt[c][:, hf * 512:(hf + 1) * 512]
                        ph.stt(xs_, pd, self.mod(ll, ci, sG, c), xs_, ALU.mult, ALU.add)

            core(l, which)
            lp = l
            do_proj = (which == 0)
            if which == 1 and chain and not last:
                core(l + 1, 0)
                lp = l + 1
                do_proj = True
            if last:
                self.final_out(ph, xt, rstd, banks, sqb, tmpx, g)
            else:
                for c in range(8):
                    ph.dma(self.xT[g, :, c, :], xt[c])
            if do_proj:
                self.rms_stats(ph, xt, rstd, banks[4:6], sqb)
                for c in range(8):
                    tx = tmpx[c % 2]
                    ph.stt(tx, xt[c], self.mod(lp, ci, 3, c), rstd, ALU.mult, ALU.mult)
                    ph.act(hT[:, c, :], tx, AF.Identity, bias=self.mod(lp, ci, 4, c))
                self.projections(ph, lp, g, hT, banks, sqb, sgb, tmpx, wgt + wut, [w.re("p f d -> p (f d)") for w in wdt])
            ph.emit()

    def final_out(self, ph, xt, rstd, banks, sqb, tmpx, g):
        self.rms_stats(ph, xt, rstd, banks[4:6], sqb)
        dst = self.yp if g == 0 else self.ys[(g - 1) * GT:g * GT, :]
        yt_all = ph.sb("ytall", [128, 8, GT], F32)
        for c in range(8):
            ph.stt(yt_all[:, c, :], xt[c], self.pcolA[:, 96 + c:97 + c], rstd, ALU.mult, ALU.mult)
        yo = [ph.sb("yo%d" % i, [128, D], F32) for i in range(2)]
        for tt in range(8):
            y = yo[tt % 2]
            for c2 in range(2):
                pb = banks[(tt * 2 + c2) % 4]
                for cc in range(4):
                    c = c2 * 4 + cc
                    ph.transpose(pb[:, cc * 128:(cc + 1) * 128], yt_all[:, c, tt * 128:(tt + 1) * 128], self.ident_f)
                ph.copy(y[:, c2 * 512:(c2 + 1) * 512], pb, eng=("act" if c2 else "dve"))
            ph.dma(dst[tt * 128:(tt + 1) * 128, :], y)

    def projections(self, ph, l, g, hT, banks, sqb, sgb, tmpx, wsm, wbig):
        ctx = (g == 0)
        t0 = g * GT
        W = self.w_in[l]
        st = {"b": 0, "o": 0, "w": 0}

        def bank():
            st["b"] += 1
            return banks[st["b"] % 8]

        obufs = [ph.sb("ob%d" % i, [128, 512], BF16) for i in range(4)]

        def obuf():
            st["o"] += 1
            return obufs[st["o"] % 4]

        def wfm():
            st["w"] += 1
            return wsm[st["w"] % len(wsm)]

        def pipeline(factories, nslot=2):
            active, free, it, more = [], list(range(nslot)), iter(factories), True
            while True:
                while free and more:
                    f = next(it, None)
                    if f is None:
                        more = False
                        break
                    sl = free.pop(0)
                    active.append((f(sl), sl))
                if not active:
                    break
                for item in list(active):
                    try:
                        next(item[0])
                    except StopIteration:
                        active.remove(item)
                        free.append(item[1])

        otf = [ph.sb("otf%d" % i, [128, 256], F32) for i in range(2)]
        qnb = [ph.sb("qnb%d" % i, [128, 512], BF16) for i in range(2)]
        qn2s = [ph.sb("qn2_%d" % i, [128, 2, 512], BF16) for i in range(2)]
        sqc = [ph.sb("sqc%d" % i, [128, 512], BF16) for i in range(2)]
        ta = [ph.sb("ta%d" % i, [128, 512], F32) for i in range(2)]
        tb = [ph.sb("tb%d" % i, [128, 512], F32) for i in range(2)]
        krbs = [ph.sb("krb%d" % i, [128, 512], BF16) for i in range(2)]
        eps = self.epsc[:, 0:1]
        if not ctx:
            p0 = (g - 1) * GT
            cos64 = ph.sb("cos64", [128, GT], F32)
            sin64 = ph.sb("sin64", [128, GT], F32)
            cos32 = ph.sb("cos32", [128, GT], F32)
            sin32 = ph.sb("sin32", [128, GT], F32)
            ph.dma(cos64, self.c_cos64[:, p0:p0 + GT])
            ph.dma(sin64, self.c_sin64[:, p0:p0 + GT])
            ph.dma(cos32, self.c_cos32[:, p0:p0 + GT])
            ph.dma(sin32, self.c_sin32[:, p0:p0 + GT])

        def loadw(wt, segs):
            off = 0
            for seg, n in segs:
                ph.dma(wt[:, :, off:off + n], seg.re("(k p) f -> p k f", p=128), q="pool")
                off += n

        def fm(wt, M, hf, pb):
            for k in range(8):
                ph.mm(pb[0:M, :], wt[:, k, 0:M], hT[:, k, hf * 512:(hf + 1) * 512], start=(k == 0), stop=(k == 7))

        def out_tok(src_f32, npart, ncol, hf, dst4, p_base=0):
            for tb_ in range(4):
                t = hf * 512 + tb_ * 128
                pb = bank()
                ph.transpose(pb[:, 0:npart], src_f32[p_base:p_base + npart, tb_ * 128:(tb_ + 1) * 128],
                             self.ident_f[p_base:p_base + npart, p_base:p_base + npart])
                yield
                ot = otf[tb_ % 2]
                ph.copy(ot[:, 0:npart], pb[:, 0:npart])
                ph.dma(dst4[t // SEQ, l, (t % SEQ):(t % SEQ) + 128, ncol:ncol + npart], ot[:, 0:npart])
                yield

        gq_w = {}

        def gqa_item(kind, i, hf):
            def gen(sl):
                if hf == 0:
                    wt = wfm()
                    gq_w[(kind, i)] = wt
                    if kind == "q":
                        loadw(wt, [(W[:, C_GQ + i * 64:C_GQ + (i + 1) * 64], 64), (W[:, C_GQ + (i + 3) * 64:C_GQ + (i + 4) * 64], 64)])
                    else:
                        loadw(wt, [(W[:, C_GK:C_GK + 128], 128)])
                wt = gq_w[(kind, i)]
                gcol = self.pcolA[:, 120 + l:121 + l] if kind == "q" else self.pcolA[:, 124 + l:125 + l]
                hsl = slice(hf * 512, (hf + 1) * 512)
                pq = bank()
                fm(wt, 128, hf, pq)
                yield
                sq = sqb[sl]
                ph.act(sq, pq, AF.Square)
                yield
                pm = bank()
                ph.mm(pm, self.ones64, sq)
                yield
                rs = sgb[sl]
                ph.act(rs, pm, AF.Ln, bias=eps)
                yield
                ph.act(rs, rs, AF.Exp, scale=-0.5)
                yield
                ob = obuf()
                if ctx:
                    if kind == "k":
                        knf = ta[sl]
                        ph.stt(knf, pq, gcol, rs, ALU.mult, ALU.mult)
                        yield
                        ph.copy(ob, knf, eng="act")
                        yield
                        yield from out_tok(knf, 128, 0, hf, self.ngk)
                    else:
                        ph.stt(ob, pq, gcol, rs, ALU.mult, ALU.mult)
                        yield
                else:
                    qn = qnb[sl]
                    ph.stt(qn, pq, gcol, rs, ALU.mult, ALU.mult)
                    yield
                    pr = bank()
                    ph.mm(pr, self.perm64, qn)
                    yield
                    ph.tt(ta[sl], qn, cos64[:, hsl], ALU.mult)
                    yield
                    ph.tt(tb[sl], pr, sin64[:, hsl], ALU.mult)
                    yield
                    ph.tt(ob, ta[sl], tb[sl], ALU.add)
                    yield
                dst = self.sQg[i] if kind == "q" else self.sKg
                ph.dma(dst[:, t0 + hf * 512:t0 + (hf + 1) * 512], ob)
            return gen

        pipeline([gqa_item(kind, i, hf) for kind, i in (("q", 0), ("q", 1), ("q", 2), ("k", 0)) for hf in range(2)])

        def tm(segs, N, evac):
            wt = wbig[st["w"] % 2].re("p (k n) -> p k n", k=8)[:, :, 0:N]
            st["w"] += 1
            loadw(wt, segs)
            for tt in range(8):
                pb = bank()
                for k in range(8):
                    ph.mm(pb[:, 0:N], hT[:, k, tt * 128:(tt + 1) * 128], wt[:, k, :], start=(k == 0), stop=(k == 7))
                evac(pb, tt)

        vts = [ph.sb("vt%d" % i, [128, 130], BF16) for i in range(2)]
        gts = [ph.sb("gt%d" % i, [128, 16], F32) for i in range(2)]
        mvts = [ph.sb("mvt%d" % i, [128, 260], BF16) for i in range(2)]
        mkts = [ph.sb("mkt%d" % i, [128, 256], BF16) for i in range(2)]
        vms = [ph.sb("vm%d" % i, [128, 390], BF16) for i in range(2)]
        for i in range(2):
            ph.memset(vts[i], 1.0)
            ph.memset(mvts[i], 1.0)
            ph.memset(vms[i], 1.0)

        def ev_a(pb, tt):
            vt = vts[tt % 2]
            ph.copy(vt.re("p (h e) -> p h e", e=65)[:, :, 0:64], pb[:, 0:128].re("p (h d) -> p h d", d=64))
            ph.dma(self.sVg[t0 + tt * 128:t0 + (tt + 1) * 128, :], vt)
            gt = gts[tt % 2]
            ph.tt(gt, pb[:, 128:144], self.gbias[:, l * 16:(l + 1) * 16], ALU.add)
            ph.dma(self.sGate[t0 + tt * 128:t0 + (tt + 1) * 128, :], gt)
            if ctx:
                ot = otf[tt % 2]
                ph.copy(ot[:, 0:128], pb[:, 0:128])
                t = tt * 128
                ph.dma(self.ngv[t // SEQ, l, (t % SEQ):(t % SEQ) + 128, :], ot[:, 0:128])

        def ev_b(pb, tt):
            mv = mvts[tt % 2]
            ph.copy(mv.re("p (h e) -> p h e", e=65)[:, :, 0:64], pb[:, 0:256].re("p (h d) -> p h d", d=64), eng="act")
            ph.dma(self.sMv[t0 + tt * 128:t0 + (tt + 1) * 128, :], mv)

        def ev_c(pb, tt):
            mk = mkts[tt % 2]
            ph.copy(mk, pb[:, 0:256], eng=("act" if tt % 2 else "dve"))
            ph.dma(self.sMkt[t0 + tt * 128:t0 + (tt + 1) * 128, :], mk)

        tm([(W[:, C_GV:C_GV + 128], 128), (W[:, C_MG:C_MG + 16], 16)], 144, ev_a)
        tm([(W[:, C_MV:C_MV + 256], 256)], 256, ev_b)
        tm([(W[:, C_MK:C_MK + 256], 256)], 256, ev_c)

        for kind, c in (("q", 0), ("q", 1), ("k", 0), ("k", 1)):
            wt = wfm()
            c0 = (C_MQ if kind == "q" else C_MK) + c * 128
            loadw(wt, [(W[:, c0:c0 + 128], 128)])
            for hf in range(2):
                pq = bank()
                fm(wt, 128, hf, pq)
                ob = obuf()
                if hf == 0:
                    ph.ts(ob, pq, 0.125 if kind == "q" else 1.0, ALU.mult)
                else:
                    ph.act(ob, pq, AF.Identity, scale=(0.125 if kind == "q" else 1.0))
                dst = self.sMq[c] if kind == "q" else self.sMk[c]
                ph.dma(dst[:, t0 + hf * 512:t0 + (hf + 1) * 512], ob)
        for h in range(4):
            wt = wfm()
            loadw(wt, [(W[:, C_MO + h * 64:C_MO + (h + 1) * 64], 64)])
            for hf in range(2):
                pq = bank()
                fm(wt, 64, hf, pq)
                ob = obuf()
                ph.act(ob[0:64, :], pq[0:64, :], AF.Sigmoid)
                ph.dma(self.sMo[h][:, t0 + hf * 512:t0 + (hf + 1) * 512], ob[0:64, :])

        wuq = ph.sb("wuq", [128, 2, 576], BF16)
        wukv = ph.sb("wukv", [128, 2, 768], BF16)
        wv = ph.sb("wv", [128, 2, 384], BF16)
        wkr = ph.sb("wkr", [128, 8, 96], BF16)
        ph.dma(wuq, self.w_uq[l].re("(c p) n -> p c n", p=128), q="pool")
        ph.dma(wukv, self.w_ukv[l].re("(c p) n -> p c n", p=128), q="pool")
        for c in range(2):
            ph.copy(wv[:, c, :].re("p (h d) -> p h d", d=64), wukv[:, c, :].re("p (h x) -> p h x", x=128)[:, :, 64:128], eng="pool")
        ph.memset(wkr, 0.0)
        ph.dma(wkr[:, :, 64:96], W[:, C_KR:C_KR + 32].re("(k p) f -> p k f", p=128), q="pool")

        def mla_item(kind, hf, wt):
            def gen(sl):
                gb = (0 if kind == "q" else 8) + l * 2
                hsl = slice(hf * 512, (hf + 1) * 512)
                ql = (ta[sl], tb[sl])
                sqs = (sqb[sl], sqc[sl])
                qn2 = qn2s[sl]
                krb = krbs[sl]
                for c in range(2):
                    pq = bank()
                    for k in range(8):
                        ph.mm(pq, wt[:, k, c * 128:(c + 1) * 128], hT[:, k, hsl], start=(k == 0), stop=(k == 7))
                    yield
                    ph.copy(ql[c], pq)
                    yield
                    ph.act(sqs[c], ql[c], AF.Square)
                    yield
                pm = bank()
                ph.mm(pm, self.ones256, sqs[0], start=True, stop=False)
                ph.mm(pm, self.ones256, sqs[1], start=False, stop=True)
                yield
                rs = sgb[sl]
                ph.act(rs, pm, AF.Ln, bias=eps)
                yield
                ph.act(rs, rs, AF.Exp, scale=-0.5)
                yield
                for c in range(2):
                    gcol = self.pcolB[:, gb + c:gb + c + 1]
                    if ctx and kind == "kv":
                        ph.stt(ql[c], ql[c], gcol, rs, ALU.mult, ALU.mult)
                        yield
                        ph.copy(qn2[:, c, :], ql[c], eng="act")
                        yield
                        yield from out_tok(ql[c], 128, c * 128, hf, self.nckv)
                    else:
                        ph.stt(qn2[:, c, :], ql[c], gcol, rs, ALU.mult, ALU.mult)
                        yield

                def rope32(dstb):
                    pr = bank()
                    ph.mm(pr[0:96, :], self.perm32[64:96, :], dstb[64:96, :])
                    yield
                    ph.tt(ta[sl][64:96, :], dstb[64:96, :], cos32[64:96, hsl], ALU.mult)
                    yield
                    ph.tt(tb[sl][64:96, :], pr[64:96, :], sin32[64:96, hsl], ALU.mult)
                    yield
                    ph.tt(dstb[64:96, :], ta[sl][64:96, :], tb[sl][64:96, :], ALU.add)
                    yield

                if kind == "q":
                    for h in range(6):
                        pa = bank()
                        ph.mm(pa[0:96, :], wuq[:, 0, h * 96:(h + 1) * 96], qn2[:, 0, :], start=True, stop=False)
                        ph.mm(pa[0:96, :], wuq[:, 1, h * 96:(h + 1) * 96], qn2[:, 1, :], start=False, stop=True)
                        yield
                        ob = obuf()
                        ph.copy(ob[0:96, :], pa[0:96, :], eng="act")
                        yield
                        if not ctx:
                            yield from rope32(ob)
                        ph.dma(self.sQm[h][:, t0 + hf * 512:t0 + (hf + 1) * 512], ob[0:96, :])
                else:
                    pk = bank()
                    for k in range(8):
                        ph.mm(pk[0:96, :], wkr[:, k, :], hT[:, k, hsl], start=(k == 0), stop=(k == 7))
                    yield
                    if ctx:
                        krf = ta[sl]
                        ph.copy(krf[64:96, :], pk[64:96, :])
                        yield
                        ph.copy(krb[64:96, :], krf[64:96, :], eng="act")
                        yield
                        yield from out_tok(krf, 32, 0, hf, self.nkr, p_base=64)
                    else:
                        ph.copy(krb[64:96, :], pk[64:96, :], eng="act")
                        yield
                        yield from rope32(krb)
                    for h in range(6):
                        pn = bank()
                        ph.mm(pn[0:64, :], wukv[:, 0, h * 128:h * 128 + 64], qn2[:, 0, :], start=True, stop=False)
                        ph.mm(pn[0:64, :], wukv[:, 1, h * 128:h * 128 + 64], qn2[:, 1, :], start=False, stop=True)
                        yield
                        ob = obuf()
                        ph.copy(ob[0:64, :], pn[0:64, :], eng="act")
                        ph.copy(ob[64:96, :], krb[64:96, :])
                        yield
                        ph.dma(self.sKm[h][:, t0 + hf * 512:t0 + (hf + 1) * 512], ob[0:96, :])
                    for tb_ in range(4):
                        pv = bank()
                        for c in range(2):
                            ph.mm(pv[:, 0:384], qn2[:, c, tb_ * 128:(tb_ + 1) * 128], wv[:, c, :], start=(c == 0), stop=(c == 1))
                        yield
                        vm = vms[(hf * 4 + tb_) % 2]
                        ph.copy(vm.re("p (h e) -> p h e", e=65)[:, :, 0:64], pv[:, 0:384].re("p (h d) -> p h d", d=64))
                        tok = t0 + hf * 512 + tb_ * 128
                        ph.dma(self.sVm[tok:tok + 128, :], vm)
                        yield
            return gen

        items = []
        for kind in ("q", "kv"):
            wt = wbig[st["w"] % 2].re("p (k n) -> p k n", k=8)[:, :, 0:256]
            st["w"] += 1
            c0 = C_QL if kind == "q" else C_KVL
            loadw(wt, [(W[:, c0:c0 + 256], 256)])
            for hf in range(2):
                items.append(mla_item(kind, hf, wt))
        pipeline(items)


def host_consts():
    def rope(rot_dim):
        half = rot_dim // 2
        freqs = (np.float32(10000.0) ** (-np.arange(0, half, 2, dtype=np.float32) / np.float32(half))).astype(np.float32)
        r = np.repeat(np.arange(TS // 64, dtype=np.float32), 64)
        c = np.tile(np.arange(64, dtype=np.float32), TS // 64)
        ang = np.concatenate([r[:, None] * freqs, c[:, None] * freqs], axis=-1).astype(np.float32)
        return np.cos(ang).astype(np.float32), np.sin(ang).astype(np.float32)

    c64, s64 = rope(64)
    c32, s32 = rope(32)
    p = np.arange(128)
    out = {
        "c_ident": np.eye(128, dtype=np.float32),
        "c_cos64": np.ascontiguousarray(c64[:, p % 32].T),
        "c_sin64": np.ascontiguousarray(s64[:, p % 32].T),
        "c_cos32": np.ascontiguousarray(c32[:, p % 16].T),
        "c_sin32": np.ascontiguousarray(s32[:, p % 16].T),
    }
    pm = np.zeros((128, 128), np.float32)
    for m in range(128):
        if (m % 64) < 32:
            pm[m + 32, m] = -1.0
        else:
            pm[m - 32, m] = 1.0
    out["c_perm64"] = pm
    p32 = np.zeros((128, 96), np.float32)
    for i in range(32):
        m = 64 + i
        if i < 16:
            p32[64 + i + 16, m] = -1.0
        else:
            p32[64 + i - 16, m] = 1.0
    out["c_perm32"] = p32
    s = np.arange(128)
    out["c_maskf"] = (s[:, None] <= s[None, :]).astype(np.float32)
    out["c_maskb"] = (s[:, None] >= s[None, :]).astype(np.float32)
    return out


def core_inputs(inp, k, depth=DEPTH, consts=None):
    L = depth
    b = k // 4
    f = lambda a: np.ascontiguousarray(np.asarray(a, dtype=np.float32))
    m = {
        "xp": f(inp["x_prompt"][4 * k:4 * k + 4]).reshape(GT, D),
        "xs": f(inp["x_sample"][b]),
        "cond": f(np.concatenate([np.asarray(inp["c_ctx"]).reshape(8, 128), np.asarray(inp["c"][b]).reshape(8, 128)], 0)),
        "ck": f(inp["cache_gqa_k"][b][:L]).reshape(L, PAST, 128),
        "cv": f(inp["cache_gqa_v"][b][:L]).reshape(L, PAST, 128),
        "cckv": f(inp["cache_mla_ckv"][b][:L]),
        "ckr": f(inp["cache_mla_krope"][b][:L]),
        "sC": f(inp["state_mlstm_C"][b][:L]).reshape(L, 8, 64, 64),
        "sn": f(inp["state_mlstm_n"][b][:L]).reshape(L, 8, 64),
        "sm": f(inp["state_mlstm_m"][b][:L]).reshape(L, 8),
        "w_ada": f(inp["w_ada"][:L]),
        "b_ada": f(inp["b_ada"][:L]),
        "norm_w": f(inp["norm_w"][:L]).reshape(L * 24, 128),
        "ffn_w_gate": f(inp["ffn_w_gate"][:L]),
        "ffn_w_up": f(inp["ffn_w_up"][:L]),
        "ffn_w_down": f(inp["ffn_w_down"][:L]),
        "w_in": f(inp["w_in"][:L]),
        "gqa_q_norm": f(inp["gqa_q_norm"][:L]),
        "gqa_k_norm": f(inp["gqa_k_norm"][:L]),
        "mlstm_gate_b": f(inp["mlstm_gate_b"][:L]).reshape(1, L * 16),
        "mlstm_out_norm": f(inp["mlstm_out_norm"][:L]).reshape(L * 4, 64),
        "mla_q_norm": f(inp["mla_q_norm"][:L]).reshape(L * 2, 128),
        "mla_w_uq": f(inp["mla_w_uq"][:L]),
        "mla_kv_norm": f(inp["mla_kv_norm"][:L]).reshape(L * 2, 128),
        "mla_w_ukv": f(inp["mla_w_ukv"][:L]),
        "w_out": f(inp["w_out"][:L]),
        "final_norm": f(inp["final_norm"]).reshape(8, 128),
    }
    m.update(consts if consts is not None else host_consts())
    return m


def _mlstm_phase(self, l, seqs=None):
    nc = self.nc
    if seqs is None:
        seqs = [(s * SEQ, SEQ // 128, True, s) for s in range(NSEQ_P)] + [(GT, TS // 128, False, 0)]
    with nc.cleanup_on_exit():
        ph = self.phase("ml%d" % l)
        banks = [ph.ps("bk%d" % i) for i in range(8)]
        st = {"b": 0}

        def bank():
            st["b"] += 1
            return banks[st["b"] % 8]

        NB = 3
        qTs = [ph.sb("qT%d" % i, [128, 2, 128], BF16) for i in range(NB)]
        kTs = [ph.sb("kT%d" % i, [128, 2, 128], BF16) for i in range(NB)]
        kts = [ph.sb("kt%d" % i, [128, 256], BF16) for i in range(NB)]
        vts = [ph.sb("vt%d" % i, [128, 260], BF16) for i in range(NB)]
        gts = [ph.sb("gt%d" % i, [128, 16], F32) for i in range(NB)]
        mos = [ph.sb("mo%d" % i, [64, 4, 128], BF16) for i in range(NB)]

        class TSet:
            pass

        TS_ = []
        for k in range(2):
            T = TSet()
            T.e1 = ph.sb("e1_%d" % k, [128, 16], F32)
            T.l1 = ph.sb("l1_%d" % k, [128, 8], F32)
            T.c8 = ph.sb("c8_%d" % k, [128, 8], F32)
            T.d1 = ph.sb("d1_%d" % k, [128, 8], F32)
            T.u8 = ph.sb("u8_%d" % k, [128, 8], F32)
            T.wold = ph.sb("wold_%d" % k, [128, 8], F32)
            T.ec8 = ph.sb("ec8_%d" % k, [128, 8], F32)
            T.dg = ph.sb("dg_%d" % k, [128, 8, 128], BF16)
            T.expc = ph.sb("expc_%d" % k, [128, 1024], F32)
            T.EM = ph.sb("EM_%d" % k, [128, 1024], F32)
            T.Ku = ph.sb("Ku_%d" % k, [128, 256], BF16)
            T.PT = [ph.sb("PT%d_%d" % (d, k), [128, 512], BF16) for d in range(2)]
            T.PvT = [ph.sb("PvT%d_%d" % (d, k), [128, 2, 128], BF16) for d in range(2)]
            T.den = [ph.sb("den%d_%d" % (d, k), [128, 512], F32) for d in range(2)]
            T.bcs = [ph.sb("bcs%d_%d" % (d, k), [64, 512], F32) for d in range(2)]
            T.hh = [ph.sb("hh%d_%d" % (d, k), [64, 512], F32) for d in range(2)]
            T.sq = ph.sb("sq_%d" % k, [64, 512], BF16)
            T.rs = ph.sb("rs_%d" % k, [64, 512], F32)
            T.hn = ph.sb("hn_%d" % k, [64, 512], F32)
            T.catm = ph.sb("catm_%d" % k, [64, 4, 128], BF16)
            TS_.append(T)
        Sf = [ph.sb("Sf%d" % d, [128, 2, 65], F32) for d in range(2)]
        Sb = [ph.sb("Sb%d" % d, [128, 4, 65], BF16) for d in range(2)]
        Sst = ph.sb("Sst", [128, TS // 128, 4, 65], BF16)
        for d in range(2):
            ph.memset(Sb[d], 0.0)
        em0 = ph.sb("em0", [128, 8], F32)
        rows = [ph.sb("rows%d" % j, [8, 2], F32) for j in range(2)]
        rt = ph.sb("rt", [8, 8], F32)
        dg8 = ph.sb("dg8", [8, 8], F32)
        emb = ph.sb("emb", [128, 8], F32)
        Sout = ph.sb("Sout", [128, 2, 2, 65], F32)
        one_col = self.ones_f[:, 0:1]
        ld = {"n": 0}

        def gate_prep(gt, bwd_only, T):
            ph.act(T.e1, gt, AF.Exp, scale=-1.0)
            ph.act(T.l1[:, 0:4], T.e1[:, 4:8], AF.Ln, bias=one_col)
            ph.act(T.l1[:, 4:8], T.e1[:, 12:16], AF.Ln, bias=one_col)
            pg = bank()
            if not bwd_only:
                ph.mm(pg[:, 0:4], self.maskf4[:, 0:128], T.l1[:, 0:4])
                ph.mm(pg[:, 8:12], self.ones_f, T.l1[:, 0:4])
            ph.mm(pg[:, 4:8], self.maskb4[:, 0:128], T.l1[:, 4:8])
            ph.mm(pg[:, 12:16], self.ones_f, T.l1[:, 4:8])
            lo = 4 if bwd_only else 0
            ph.ts(T.c8[:, lo:8], pg[:, lo:8], -1.0, ALU.mult)
            if not bwd_only:
                ph.tt(T.d1[:, 0:4], gt[:, 0:4], T.c8[:, 0:4], ALU.subtract)
            ph.tt(T.d1[:, 4:8], gt[:, 8:12], T.c8[:, 4:8], ALU.subtract)
            ph.act(T.u8[:, lo:8], T.d1[:, lo:8], AF.Exp)
            ph.act(T.wold[:, lo:8], pg[:, 8 + lo:16], AF.Exp, scale=-1.0)

        def state_update(d, kt, vt, T):
            for h in range(4):
                ph.ts(T.Ku[:, h * 64:(h + 1) * 64], kt[:, h * 64:(h + 1) * 64], T.u8[:, d * 4 + h:d * 4 + h + 1], ALU.mult)
            pS = bank()
            for h in range(4):
                p = h // 2
                ph.mm(pS[:, h * 65:(h + 1) * 65], T.Ku[:, p * 128:(p + 1) * 128], vt[:, h * 65:(h + 1) * 65])
            for h in range(4):
                p, b = h // 2, (h % 2) * 64
                sv = Sf[d][b:b + 64, p, :]
                ph.tt(sv, pS[b:b + 64, h * 65:(h + 1) * 65], sv, ALU.add)
                ph.ts(sv, sv, T.wold[b:b + 64, d * 4 + h:d * 4 + h + 1], ALU.mult)
                ph.copy(Sb[d][b:b + 64, h, :], sv, eng="pool")

        def lockstep(gens):
            gens = list(gens)
            while gens:
                for g_ in list(gens):
                    try:
                        next(g_)
                    except StopIteration:
                        gens.remove(g_)

        def prep(j, i, T, tok, ctx):
            qT, kT, kt, vt, gt, mo = qTs[i], kTs[i], kts[i], vts[i], gts[i], mos[i]
            ph.dma(qT, self.sMq[:, :, tok:tok + 128].re("c p t -> p c t"))
            ph.dma(kT, self.sMk[:, :, tok:tok + 128].re("c p t -> p c t"))
            ph.dma(kt, self.sMkt[tok:tok + 128, :])
            ph.dma(vt, self.sMv[tok:tok + 128, :])
            ph.dma(gt, self.sGate[tok:tok + 128, :])
            ph.dma(mo, self.sMo[:, :, tok:tok + 128].re("h p t -> p h t"))
            yield
            ph.act(T.e1, gt, AF.Exp, scale=-1.0)
            yield
            ph.act(T.l1[:, 0:4], T.e1[:, 4:8], AF.Ln, bias=one_col)
            ph.act(T.l1[:, 4:8], T.e1[:, 12:16], AF.Ln, bias=one_col)
            yield
            pg = bank()
            ph.mm(pg[:, 0:4], self.maskf4[:, 0:128], T.l1[:, 0:4])
            ph.mm(pg[:, 8:12], self.ones_f, T.l1[:, 0:4])
            ph.mm(pg[:, 4:8], self.maskb4[:, 0:128], T.l1[:, 4:8])
            ph.mm(pg[:, 12:16], self.ones_f, T.l1[:, 4:8])
            yield
            ph.ts(T.c8, pg[:, 0:8], -1.0, ALU.mult)
            yield
            ph.tt(T.d1[:, 0:4], gt[:, 0:4], T.c8[:, 0:4], ALU.subtract)
            ph.tt(T.d1[:, 4:8], gt[:, 8:12], T.c8[:, 4:8], ALU.subtract)
            ph.act(T.ec8, T.c8, AF.Exp)
            yield
            ph.act(T.u8, T.d1, AF.Exp)
            ph.act(T.wold, pg[:, 8:16], AF.Exp, scale=-1.0)
            if ctx:
                pr = bank()
                ph.transpose(pr[0:8, 0:128], T.d1, self.ident_f)
                ph.mm(pr[0:8, 128:129], T.l1, one_col)
                yield
                ph.add("dve", lambda e, o=rows[j][:, 0:1].ap, a=pr[0:8, 0:128].ap: e.tensor_reduce(out=o, in_=a, axis=AX.X, op=ALU.max),
                       reads=(pr,), writes=(rows[j],))
                ph.copy(rows[j][:, 1:2], pr[0:8, 128:129])
            yield
            for hd in range(8):
                ph.ts(T.dg[:, hd, :], self.ident_b, T.ec8[:, hd:hd + 1], ALU.mult)
                if hd % 2:
                    yield
            pRs = [bank(), bank()]
            for d in range(2):
                ph.mm(pRs[d], self.ones_b, T.dg[:, d * 4:(d + 1) * 4, :].re("p a b -> p (a b)"))
            yield
            for d in range(2):
                ph.copy(T.expc[:, d * 512:(d + 1) * 512], pRs[d], eng="act")
            yield
            for d in range(2):
                ph.tt(T.EM[:, d * 512:(d + 1) * 512], pRs[d], (self.maskf4 if d == 0 else self.maskb4), ALU.mult)
                yield
            pAs = [bank(), bank()]
            for h in range(4):
                p, b = h // 2, (h % 2) * 64
                ph.mm(pAs[h % 2][:, h * 128:(h + 1) * 128], kT[b:b + 64, p, :], qT[b:b + 64, p, :])
            yield
            for d in range(2):
                for h in range(4):
                    hs = slice(h * 128, (h + 1) * 128)
                    es = slice(d * 512 + h * 128, d * 512 + (h + 1) * 128)
                    ph.stt(T.PT[d][:, hs], pAs[h % 2][:, hs], T.u8[:, d * 4 + h:d * 4 + h + 1], T.EM[:, es], ALU.mult, ALU.mult)
                    p, b = h // 2, (h % 2) * 64
                    ph.tt(T.PvT[d][b:b + 64, p, :], qT[b:b + 64, p, :], T.expc[b:b + 64, es], ALU.mult, eng="pool")
                    yield

        def finish(j, i, T, tok, ctx, last):
            kt, vt, mo = kts[i], vts[i], mos[i]
            pOs = [bank(), bank()]
            for d in range(2):
                pO = pOs[d]
                for h in range(4):
                    hs = slice(h * 128, (h + 1) * 128)
                    p = h // 2
                    ph.mm(pO[0:65, hs], vt[:, h * 65:(h + 1) * 65], T.PT[d][:, hs], start=True, stop=False)
                    sst = Sb[0][:, h, :] if d == 0 else Sst[:, j, h, :]
                    ph.mm(pO[0:65, hs], sst, T.PvT[d][:, p, :], start=False, stop=True)
                yield
            for d in range(2):
                ph.copy(T.den[d][64:65, :], pOs[d][64:65, :], eng="act")
            yield
            for d in range(2):
                ph.tt(T.den[d][64:65, :], T.den[d][64:65, :], T.den[d][64:65, :], ALU.mult)
            yield
            for d in range(2):
                ph.ts(T.den[d][64:65, :], T.den[d][64:65, :], 1.0, ALU.max)
            yield
            for d in range(2):
                ph.act(T.den[d][64:65, :], T.den[d][64:65, :], AF.Ln)
            yield
            for d in range(2):
                ph.act(T.den[d][64:65, :], T.den[d][64:65, :], AF.Exp, scale=-0.5)
            yield
            pBs = [bank(), bank()]
            for d in range(2):
                ph.mm(pBs[d][0:64, :], self.ones_f[64:65, 0:64], T.den[d][64:65, :])
            yield
            for d in range(2):
                ph.copy(T.bcs[d], pBs[d][0:64, :], eng="act")
            yield
            for d in range(2):
                ph.tt(T.hh[d], pOs[d][0:64, :], T.bcs[d], ALU.mult)
                yield
            ph.tt(T.hh[0], T.hh[0], T.hh[1], ALU.add)
            yield
            ph.tt(T.sq, T.hh[0], T.hh[0], ALU.mult)
            yield
            pM = bank()
            ph.mm(pM[0:64, :], self.ones64[0:64, 0:64], T.sq)
            yield
            ph.act(T.rs, pM[0:64, :], AF.Ln, bias=self.epsc[0:64, 0:1])
            yield
            ph.act(T.rs, T.rs, AF.Exp, scale=-0.5)
            yield
            for h in range(4):
                hs = slice(h * 128, (h + 1) * 128)
                ph.stt(T.hn[:, hs], T.hh[0][:, hs], self.pcolB[0:64, 16 + l * 4 + h:17 + l * 4 + h], T.rs[:, hs], ALU.mult, ALU.mult)
                if h % 2:
                    yield
            ph.tt(T.catm, T.hn.re("p (h t) -> p h t", h=4), mo, ALU.mult)
            ph.dma(self.sCatM[:, :, tok:tok + 128].re("h p t -> p h t"), T.catm)
            yield
            if ctx or not last:
                for h in range(4):
                    ph.ts(T.Ku[:, h * 64:(h + 1) * 64], kt[:, h * 64:(h + 1) * 64], T.u8[:, h:h + 1], ALU.mult)
                yield
                pS = bank()
                for h in range(4):
                    p = h // 2
                    ph.mm(pS[:, h * 65:(h + 1) * 65], T.Ku[:, p * 128:(p + 1) * 128], vt[:, h * 65:(h + 1) * 65])
                yield
                for h in range(4):
                    p, b = h // 2, (h % 2) * 64
                    sv = Sf[0][b:b + 64, p, :]
                    ph.tt(sv, pS[b:b + 64, h * 65:(h + 1) * 65], sv, ALU.add)
                    ph.ts(sv, sv, T.wold[b:b + 64, h:h + 1], ALU.mult)
                    ph.copy(Sb[0][b:b + 64, h, :], sv, eng="pool")
                    yield

        for (tok0, nblk, ctx, sidx) in seqs:
            if ctx:
                for d in range(2):
                    ph.memset(Sf[d], 0.0)
                    for h in range(4):
                        b = (h % 2) * 64
                        ph.memset(Sb[d][b:b + 64, h, :], 0.0)
            else:
                ph.dma(em0, V(self.sm.ap[l:l + 1, :].partition_broadcast(128), self.sm.buf))
                ph.act(em0, em0, AF.Exp)
                for hd in range(8):
                    d, h = hd // 4, hd % 4
                    p, b = h // 2, (h % 2) * 64
                    ph.dma(Sf[d][b:b + 64, p, 0:64], self.sC[l, hd])
                    ph.dma(Sf[d][b:b + 64, p, 64:65], self.sn[l, hd:hd + 1, :].re("o d -> d o"), allow_slow_non_contiguous=True)
                for hd in range(8):
                    d, h = hd // 4, hd % 4
                    p, b = h // 2, (h % 2) * 64
                    sv = Sf[d][b:b + 64, p, :]
                    ph.ts(sv, sv, em0[b:b + 64, hd:hd + 1], ALU.mult)
                    ph.copy(Sb[d][b:b + 64, h, :], sv, eng="pool")
            pending = None
            for j in range(nblk - 1, -1, -1):
                i = ld["n"] % NB
                ld["n"] += 1
                T = TS_[j % 2]
                tok = tok0 + j * 128
                need = ctx or j > 0
                if need:
                    ph.dma(kts[i], self.sMkt[tok:tok + 128, :])
                    ph.dma(vts[i], self.sMv[tok:tok + 128, :])
                    ph.dma(gts[i], self.sGate[tok:tok + 128, :])
                    gate_prep(gts[i], True, T)
                if pending is not None:
                    state_update(1, *pending)
                ph.copy(Sst[:, j], Sb[1], eng="pool")
                pending = (kts[i], vts[i], T) if need else None
            if pending is not None:
                state_update(1, *pending)
            slot = {}
            for j in range(nblk):
                i = ld["n"] % NB
                ld["n"] += 1
                slot[j] = (i, TS_[j % 2], tok0 + j * 128)
                gens = [prep(j, slot[j][0], slot[j][1], slot[j][2], ctx)]
                if j >= 1:
                    gens.append(finish(j - 1, slot[j - 1][0], slot[j - 1][1], slot[j - 1][2], ctx, False))
                lockstep(gens)
            jl = nblk - 1
            lockstep([finish(jl, slot[jl][0], slot[jl][1], slot[jl][2], ctx, True)])
            if ctx:
                r0, r1 = rows[0], rows[1]
                fsel = self.maskf4[0:8, 3:4]
                bsel = self.maskb4[0:8, 4:5]
                ph.ts(rt[:, 0:1], r0[:, 1:2], -1.0, ALU.mult)
                ph.ts(rt[:, 1:2], r1[:, 1:2], -1.0, ALU.mult)
                ph.tt(rt[:, 2:3], rt[:, 0:1], rt[:, 1:2], ALU.add)
                ph.stt(rt[:, 3:4], rt[:, 1:2], fsel, rt[:, 0:1], ALU.mult, ALU.add)
                ph.tt(rt[:, 3:4], rt[:, 3:4], r0[:, 0:1], ALU.add)
                ph.stt(rt[:, 4:5], rt[:, 0:1], bsel, rt[:, 1:2], ALU.mult, ALU.add)
                ph.tt(rt[:, 4:5], rt[:, 4:5], r1[:, 0:1], ALU.add)
                ph.tt(rt[:, 5:6], rt[:, 3:4], rt[:, 4:5], ALU.max)
                ph.tt(rt[:, 5:6], rt[:, 5:6], rt[:, 2:3], ALU.max)
                ph.act(rt[:, 6:7], rt[:, 5:6], AF.Exp, scale=-1.0)
                ph.ts(dg8, self.ident_f[0:8, 0:8], rt[:, 6:7], ALU.mult)
                pE = bank()
                ph.mm(pE[:, 0:8], self.ones_f[0:8, :], dg8)
                ph.copy(emb, pE[:, 0:8])
                for hd in range(8):
                    d, h = hd // 4, hd % 4
                    p, b = h // 2, (h % 2) * 64
                    ph.ts(Sout[b:b + 64, d, p, :], Sf[d][b:b + 64, p, :], emb[b:b + 64, hd:hd + 1], ALU.mult)
                for hd in range(8):
                    d, h = hd // 4, hd % 4
                    p, b = h // 2, (h % 2) * 64
                    ph.dma(self.nC[sidx, l, hd], Sout[b:b + 64, d, p, 0:64])
                    ph.dma(self.nn[sidx, l, hd:hd + 1, :].re("o d -> d o"), Sout[b:b + 64, d, p, 64:65], allow_slow_non_contiguous=True)
                ph.dma(self.nm[sidx, l:l + 1, :].re("o d -> d o"), rt[:, 5:6], allow_slow_non_contiguous=True)
        ph.emit()


Builder.mlstm_phase = _mlstm_phase


def _attn_phase(self, l, sample, qt_limit=None):
    nc = self.nc
    with nc.cleanup_on_exit():
        ph = self.phase("at%d_%d" % (l, int(sample)))
        NKC = (PAST + TS) // 128 if sample else SEQ // 128
        NK = NKC * 128
        QN = 512 if sample else 256
        NX = 3
        xts = [ph.ps("xs%d" % i, (128, 2 * QN)) for i in range(NX)]
        obank = ph.ps("obank")
        nbank = ph.ps("nwbank")
        wbank = nbank
        banks = [obank, nbank, xts[0][:, 0:512], xts[1][:, 0:512]]
        KgT = ph.sb("KgT", [128, NK], BF16)
        Vg = ph.sb("Vg", [128, NKC, 130], BF16)
        KmT = ph.sb("KmT", [128, 6, NK], BF16)
        Vm = ph.sb("Vm", [128, NKC, 390], BF16)
        qgs = [ph.sb("qg%d" % i, [128, 6, QN], BF16) for i in range(2)]
        qms = [ph.sb("qm%d" % i, [128, 6, QN], BF16) for i in range(2)]
        for i in range(2):
            ph.memset(qgs[i], 0.0)
            ph.memset(qms[i], 0.0)
        ph.memset(KmT, 0.0)
        pts = [ph.sb("pt%d" % i, [128, 2 * QN], BF16) for i in range(NX)]
        osbs = [ph.sb("osb%d" % i, [128, 512], F32) for i in range(2)]
        catTs = [ph.sb("catT%d" % i, [128, 16, QN], BF16) for i in range(2)]
        woTs = [ph.sb("woT%d" % i, [128, 16, 128], BF16) for i in range(2)]
        for i in range(2):
            ph.memset(catTs[i][64:128], 0.0, eng="dve")
            ph.memset(woTs[i][64:128], 0.0, eng="dve")
        wo_cnt = {"n": 0}
        xcs = [ph.sb("xc%d" % i, [128, QN], F32) for i in range(8)]
        dens = [ph.sb("den%d" % i, [128, 512], F32) for i in range(2)]
        DEF = 1 if NKC // 2 == 1 else 2
        st = {"b": 0}

        def bank():
            st["b"] += 1
            return banks[st["b"] % 4]

        if sample:
            ckt = ph.sb("ckt", [128, 2, 128], F32)
            ckvt = ph.sb("ckvt", [128, 2, 256], F32)
            ckvT = ph.sb("ckvT", [128, 2, 256], BF16)
            krs = ph.sb("krs", [128, 2, 96], F32)
            krT = ph.sb("krT", [128, 256], BF16)
            wukv = ph.sb("wukv", [128, 2, 768], BF16)
            wv = ph.sb("wv", [128, 2, 384], BF16)
            ph.memset(Vg, 1.0)
            ph.memset(Vm, 1.0)
            ph.dma(wukv, self.w_ukv[l].re("(c p) n -> p c n", p=128), q="pool")
            for c in range(2):
                ph.copy(wv[:, c, :].re("p (h d) -> p h d", d=64), wukv[:, c, :].re("p (h x) -> p h x", x=128)[:, :, 64:128], eng="pool")
            ph.dma(ckt, self.ck[l].re("(c p) d -> p c d", p=128))
            ph.dma(ckvt, self.cckv[l].re("(c p) d -> p c d", p=128))
            ph.memset(krs, 0.0)
            ph.dma(krs[:, :, 64:96], self.ckr[l].re("(c p) d -> p c d", p=128))
            for kc in range(2):
                ph.dma(Vg[:, kc, :].re("p (h e) -> p h e", e=65)[:, :, 0:64],
                       self.cv[l, kc * 128:(kc + 1) * 128, :].re("p (h d) -> p h d", d=64), q="pool")
            for kc in range(2):
                pb = bank()
                ph.transpose(pb[:, 0:128], ckt[:, kc, :], self.ident_f)
                ph.copy(KgT[:, kc * 128:(kc + 1) * 128], pb[:, 0:128])
                for c in range(2):
                    pb = bank()
                    ph.transpose(pb[:, 0:128], ckvt[:, kc, c * 128:(c + 1) * 128], self.ident_f)
                    ph.copy(ckvT[:, c, kc * 128:(kc + 1) * 128], pb[:, 0:128])
                pb = bank()
                ph.transpose(pb[0:96, 0:128], krs[:, kc, :], self.ident_f)
                ph.copy(krT[64:96, kc * 128:(kc + 1) * 128], pb[64:96, 0:128])
            for h in range(6):
                pb = bank()
                for c in range(2):
                    ph.mm(pb[0:64, 0:256], wukv[:, c, h * 128:h * 128 + 64], ckvT[:, c, :], start=(c == 0), stop=(c == 1))
                ph.copy(KmT[0:64, h, 0:256], pb[0:64, 0:256], eng="act")
                ph.copy(KmT[64:96, h, 0:256], krT[64:96, :], eng="pool")
            for kc in range(2):
                pb = bank()
                for c in range(2):
                    ph.mm(pb[:, 0:384], ckvT[:, c, kc * 128:(kc + 1) * 128], wv[:, c, :], start=(c == 0), stop=(c == 1))
                ph.copy(Vm[:, kc, :].re("p (h e) -> p h e", e=65)[:, :, 0:64], pb[:, 0:384].re("p (h d) -> p h d", d=64))
            ph.dma(KgT[:, PAST:], self.sKg[:, GT:TT])
            for q4 in range(4):
                r0 = GT + q4 * GT
                ph.dma(Vg[:, 2 + q4 * 8:2 + (q4 + 1) * 8, :], self.sVg[r0:r0 + GT, :].re("(c p) e -> p c e", p=128))
                ph.dma(Vm[:, 2 + q4 * 8:2 + (q4 + 1) * 8, :], self.sVm[r0:r0 + GT, :].re("(c p) e -> p c e", p=128))
            for h in range(6):
                ph.dma(KmT[0:96, h, PAST:], self.sKm[h][:, GT:TT])
            tiles = [(GT + i * 512, 1 + i // 2, (i % 2) * 512) for i in range(TS // 512)]
        else:
            tiles = [(s * SEQ, 0, s * SEQ) for s in range(NSEQ_P)]
        if qt_limit is not None:
            tiles = tiles[:qt_limit]

        heads = [("g", h) for h in range(6)] + [("m", h) for h in range(6)]
        carry = []
        NP = NKC // 2
        for ti, (tok, g, xoff) in enumerate(tiles):
            ci = 0 if g == 0 else 1
            if not sample:
                ph.dma(KgT, self.sKg[:, tok:tok + SEQ])
                ph.dma(Vg, self.sVg[tok:tok + SEQ, :].re("(c p) e -> p c e", p=128))
                ph.dma(Vm, self.sVm[tok:tok + SEQ, :].re("(c p) e -> p c e", p=128))
                for h in range(6):
                    ph.dma(KmT[0:96, h, :], self.sKm[h][:, tok:tok + SEQ])
            qg, qm = qgs[ti % 2], qms[ti % 2]
            for j in range(2):
                ph.dma(qg[j * 64:(j + 1) * 64, 3 * j:3 * j + 3, :], self.sQg[:, j * 64:(j + 1) * 64, tok:tok + QN].re("i p t -> p i t"))
            ph.dma(qm[0:96], self.sQm[:, :, tok:tok + QN].re("h p t -> p h t"))
            catT = catTs[ti % 2]
            ph.dma(catT[0:64, 6:10, :], self.sCatM[:, :, tok:tok + QN].re("h p t -> p h t"))
            pairs = [(hi, kp) for hi in range(12) for kp in range(NP)]
            n = len(pairs)
            pend = []

            def push(pos, fn):
                k_ = len(pend)
                while k_ > 0 and pend[k_ - 1][0] > pos:
                    k_ -= 1
                pend.insert(k_, (pos, fn))

            for k_, fn_ in enumerate(carry):
                push(int((k_ + 1) * n / 9), fn_)
            carry = []

            def s_pair(i):
                hi, kp = pairs[i]
                kind, h = heads[hi]
                X = xts[i % NX]
                for u in range(2):
                    kc = 2 * kp + u
                    if kind == "g":
                        ph.mm(X[:, u * QN:(u + 1) * QN], KgT[:, kc * 128:(kc + 1) * 128], qg[:, h, :])
                    else:
                        ph.mm(X[:, u * QN:(u + 1) * QN], KmT[:, h, kc * 128:(kc + 1) * 128], qm[:, h, :])

            def pv_pair(i):
                hi, kp = pairs[i]
                kind, h = heads[hi]
                X = xts[i % NX]
                pt = pts[i % NX]
                ob = obank
                ph.act(pt[:, 0:2 * QN], X[:, 0:2 * QN], AF.Exp, scale=(0.125 if kind == "g" else 96.0 ** -0.5))
                for u in range(2):
                    kc = 2 * kp + u
                    vt = Vg[:, kc, (h // 3) * 65:(h // 3 + 1) * 65] if kind == "g" else Vm[:, kc, h * 65:(h + 1) * 65]
                    ph.mm(ob[0:65, 0:QN], vt, pt[:, u * QN:(u + 1) * QN], start=(kc == 0), stop=(kc == NKC - 1))
                if kp == NP - 1:
                    slot = h if kind == "g" else 10 + h
                    den = dens[hi % 2]
                    osb = osbs[hi % 2]
                    ph.copy(osb[0:65, 0:QN], ob[0:65, 0:QN])
                    ph.recip(den[64:65, 0:QN], osb[64:65, 0:QN])

                    def fin(osb=osb, slot=slot, den=den, catT=catT):
                        ph.mm(nbank[0:64, 0:QN], self.ones_f[64:65, 0:64], den[64:65, 0:QN])
                        ph.tt(catT[0:64, slot, :], osb[0:64, 0:QN], nbank[0:64, 0:QN], ALU.mult)
                    push(i + DEF, fin)

            LK = NX - 1
            for i in range(n + LK):
                if i < n:
                    s_pair(i)
                if i >= LK:
                    pv_pair(i - LK)
                while pend and pend[0][0] <= i - LK:
                    pend.pop(0)[1]()
            while pend:
                pend.pop(0)[1]()
            for c in range(8):
                ph.dma(xcs[c], self.xT[g, :, c, xoff:xoff + QN])

            def wo_item(c, catT=catT, g=g, xoff=xoff, ci=ci):
                def run():
                    k_ = wo_cnt["n"]
                    wo_cnt["n"] += 1
                    xc = xcs[c]
                    wt = woTs[k_ % 2]
                    ph.dma(wt[0:64], self.w_out[l][:, c * 128:(c + 1) * 128].re("(s p) d -> p s d", p=64), q="pool")
                    for s_ in range(16):
                        ph.mm(wbank[:, 0:QN], wt[:, s_, :], catT[:, s_, :], start=(s_ == 0), stop=(s_ == 15))
                    ph.stt(xc, wbank[:, 0:QN], self.mod(l, ci, 5, c), xc, ALU.mult, ALU.add)
                    ph.dma(self.xT[g, :, c, xoff:xoff + QN], xc)
                return run

            carry = [wo_item(c) for c in range(8)]
        for fn_ in carry:
            fn_()
        ph.emit()


Builder.attn_phase = _attn_phase


def build_program(depth=DEPTH, debug=False):
    B = Builder(depth=depth, debug=debug)
    B.prologue()
    for g in range(NG):
        B.ffn_phase(0, 0, g)
    for l in range(depth):
        B.mlstm_phase(l)
        B.attn_phase(l, False)
        B.attn_phase(l, True)
        for g in range(NG):
            B.ffn_phase(l, 1, g, chain=True)
    return B


def kernel(**inputs):
    inp = {k: np.asarray(v) for k, v in inputs.items()}
    B = build_program(DEPTH)
    consts = host_consts()
    in_maps = [core_inputs(inp, k, DEPTH, consts) for k in range(8)]
    res = run_bass_kernel_spmd(B.nc, in_maps, core_ids=list(range(8)))
    r = res.results
    L = DEPTH
    y_prompt = np.concatenate([r[k]["yp"].reshape(NSEQ_P, SEQ, D) for k in range(8)], 0)
    y_sample = np.stack([r[0]["ys"], r[4]["ys"]], 0)
    cat = lambda name, shp: np.concatenate([r[k][name].reshape((NSEQ_P,) + shp) for k in range(8)], 0)
    new_k = cat("ngk", (L, SEQ, 2, 64))
    new_v = cat("ngv", (L, SEQ, 2, 64))
    new_ckv = cat("nckv", (L, SEQ, 256))
    new_kr = cat("nkr", (L, SEQ, 32))
    new_C = cat("nC", (L, 2, 4, 64, 64))
    new_n = cat("nn", (L, 2, 4, 64))
    new_m = cat("nm", (L, 2, 4))
    outs = (y_prompt, y_sample, new_k, new_v, new_ckv, new_kr, new_C, new_n, new_m)
    return tuple(np.ascontiguousarray(o, dtype=np.float32) for o in outs)
```
